# Optimizing a Trainium2 kernel written in Bass

```python
import math
import jax, jax.numpy as jnp
from jax import lax
import numpy as np

D_MODEL = 1024
BATCH = 32
SEQ = 256
DEPTH = 1
DEC_BATCH = 8
DEC_SEQ = 4096
PAST_LEN = 512

GRID_W = 64
N_DIR = 2
D_LRU = 1024
LRU_HEADS = 16
LRU_BLOCK = D_LRU // LRU_HEADS
CONV_W = 4
LRU_C = 8.0
D_S5 = 512
S5_GROUP = 16
S5_GROUPS = D_S5 // S5_GROUP
S5_STATE = 64
D_FF = 2816
EPS = 1e-6

kernel_name = "hybrid_rglru_s5_diffusion_step"


def _rmsnorm(x, g):
    x32 = x.astype(jnp.float32)
    y = x32 * lax.rsqrt(jnp.mean(x32 * x32, axis=-1, keepdims=True) + EPS) * g.astype(jnp.float32)
    return y.astype(x.dtype)


def _dwconv(x, w, b):
    y = lax.conv_general_dilated(x, w[:, None, :].astype(x.dtype), window_strides=(1,),
                                 padding=[(2, 1)], dimension_numbers=("NWC", "WIO", "NWC"),
                                 feature_group_count=x.shape[-1])
    return y + b.astype(x.dtype)


def _to_col_major(x):
    bsz, t, ch = x.shape
    rows = t // GRID_W
    return x.reshape(bsz, rows, GRID_W, ch).transpose(0, 2, 1, 3).reshape(bsz, t, ch)


def _from_col_major(x):
    bsz, t, ch = x.shape
    rows = t // GRID_W
    return x.reshape(bsz, GRID_W, rows, ch).transpose(0, 2, 1, 3).reshape(bsz, t, ch)


def _linear_scan(a, b, h0):
    def comb(e1, e2):
        return (e1[0] * e2[0], e2[0] * e1[1] + e2[1])
    a_cum, b_cum = lax.associative_scan(comb, (a, b), axis=1)
    return a_cum * h0[:, None] + b_cum


def _complex_linear_scan(a_re, a_im, b_re, b_im, h0_re, h0_im):
    def comb(e1, e2):
        a1r, a1i, b1r, b1i = e1
        a2r, a2i, b2r, b2i = e2
        return (a2r * a1r - a2i * a1i, a2r * a1i + a2i * a1r,
                a2r * b1r - a2i * b1i + b2r, a2r * b1i + a2i * b1r + b2i)
    ar, ai, br, bi = lax.associative_scan(comb, (a_re, a_im, b_re, b_im), axis=1)
    h0r, h0i = h0_re[:, None], h0_im[:, None]
    return ar * h0r - ai * h0i + br, ar * h0i + ai * h0r + bi


def _rglru_bidir(u, lam, w_r, b_r, w_i, b_i, h0, return_state):
    bsz, t, ch = u.shape
    out = jnp.zeros_like(u)
    finals = []
    for d in range(N_DIR):
        ud = u if d == 0 else jnp.flip(u, axis=1)
        ub = ud.reshape(bsz, t, LRU_HEADS, LRU_BLOCK)
        r = jax.nn.sigmoid(jnp.einsum("bthi,hij->bthj", ub, w_r[d].astype(jnp.float32)).reshape(bsz, t, ch)
                           + b_r[d].astype(jnp.float32))
        i = jax.nn.sigmoid(jnp.einsum("bthi,hij->bthj", ub, w_i[d].astype(jnp.float32)).reshape(bsz, t, ch)
                           + b_i[d].astype(jnp.float32))
        log_a = -LRU_C * r * jax.nn.softplus(-lam[d].astype(jnp.float32))
        a = jnp.exp(log_a)
        b = jnp.sqrt(-jnp.expm1(2.0 * log_a)) * (i * ud)
        h = _linear_scan(a, b, h0[:, d])
        if return_state:
            finals.append(h[:, -1])
        out = out + (h if d == 0 else jnp.flip(h, axis=1))
    fin = jnp.stack(finals, axis=1) if return_state else None
    return out, fin


def _s5_bidir(u, a_re, a_im, log_dt, b_re, b_im, c_re, c_im, h0_re, h0_im, return_state):
    bsz, t, ch = u.shape
    y = jnp.zeros((bsz, t, S5_GROUPS, S5_GROUP), jnp.float32)
    fin_re, fin_im = [], []
    for d in range(N_DIR):
        lr = a_re[d].astype(jnp.float32)
        li = a_im[d].astype(jnp.float32)
        dt = jnp.exp(log_dt[d].astype(jnp.float32))[:, None]
        mag = jnp.exp(lr * dt)
        abar_re, abar_im = mag * jnp.cos(li * dt), mag * jnp.sin(li * dt)
        den = lr * lr + li * li
        nr, ni = abar_re - 1.0, abar_im
        f_re = (nr * lr + ni * li) / den
        f_im = (ni * lr - nr * li) / den
        br_d, bi_d = b_re[d].astype(jnp.float32), b_im[d].astype(jnp.float32)
        bb_re = f_re[..., None] * br_d - f_im[..., None] * bi_d
        bb_im = f_re[..., None] * bi_d + f_im[..., None] * br_d
        ud = (u if d == 0 else jnp.flip(u, axis=1)).reshape(bsz, t, S5_GROUPS, S5_GROUP)
        bu_re = jnp.einsum("btgh,gph->btgp", ud, bb_re)
        bu_im = jnp.einsum("btgh,gph->btgp", ud, bb_im)
        ar = jnp.broadcast_to(abar_re, bu_re.shape)
        ai = jnp.broadcast_to(abar_im, bu_re.shape)
        h_re, h_im = _complex_linear_scan(ar, ai, bu_re, bu_im,
                                          h0_re[:, d].astype(jnp.float32), h0_im[:, d].astype(jnp.float32))
        if return_state:
            fin_re.append(h_re[:, -1])
            fin_im.append(h_im[:, -1])
        yd = (jnp.einsum("btgp,ghp->btgh", h_re, c_re[d].astype(jnp.float32))
              - jnp.einsum("btgp,ghp->btgh", h_im, c_im[d].astype(jnp.float32)))
        y = y + (yd if d == 0 else jnp.flip(yd, axis=1))
    if return_state:
        return y.reshape(bsz, t, ch), jnp.stack(fin_re, axis=1), jnp.stack(fin_im, axis=1)
    return y.reshape(bsz, t, ch), None, None


def _layer(x, cvec, p, h0_lru, h0_re, h0_im, grid_order, return_state):
    dtype = x.dtype
    mod = (jax.nn.silu(cvec) @ p["w_mod"] + p["b_mod"])[:, None, :]
    sh1, sc1, g1, sh2, sc2, g2 = jnp.split(mod, 6, axis=-1)
    hn = _rmsnorm(x, p["g_pre_mix"]) * (1.0 + sc1) + sh1
    z = hn @ p["w_in"]
    xa, ga, xs = jnp.split(z, [D_LRU, 2 * D_LRU], axis=-1)
    ua = _dwconv(xa, p["conv_w"], p["conv_b"]).astype(jnp.float32)
    ha, fin_lru = _rglru_bidir(ua, p["lru_lambda"], p["lru_w_r"], p["lru_b_r"], p["lru_w_i"], p["lru_b_i"],
                               h0_lru, return_state)
    ya = jax.nn.gelu(ga) * ha.astype(dtype)
    if grid_order:
        xs = _to_col_major(xs)
    us = xs.astype(jnp.float32)
    hs, fin_re, fin_im = _s5_bidir(us, p["s5_a_re"], p["s5_a_im"], p["s5_log_dt"], p["s5_b_re"], p["s5_b_im"],
                                   p["s5_c_re"], p["s5_c_im"], h0_re, h0_im, return_state)
    vs = jax.nn.gelu(hs + p["s5_d"].astype(jnp.float32) * us).astype(dtype)
    ys = vs * jax.nn.sigmoid(vs @ p["s5_w_glu"] + p["s5_b_glu"])
    if grid_order:
        ys = _from_col_major(ys)
    gate_a, gate_b = jnp.split(jax.nn.sigmoid(hn @ p["w_gate"] + p["b_gate"]), 2, axis=-1)
    m = (gate_a * (ya @ p["w_proj_lru"]) + gate_b * (ys @ p["w_proj_s5"])) @ p["w_out"]
    x = x + g1 * _rmsnorm(m, p["g_post_mix"])
    hn2 = _rmsnorm(x, p["g_pre_ffn"]) * (1.0 + sc2) + sh2
    u1, u3 = jnp.split(hn2 @ p["w_ff_in"], 2, axis=-1)
    f = (jax.nn.silu(u1) * u3) @ p["w_ff_out"]
    x = x + g2 * _rmsnorm(f, p["g_post_ffn"])
    return x, fin_lru, fin_re, fin_im


def setup_inputs(seed: int = 0) -> dict:
    key = jax.random.key(seed)
    ks = iter(jax.random.split(key, 48))

    def nrm(shape, scale):
        return jax.random.normal(next(ks), shape, jnp.float32) * scale

    D = D_MODEL
    a_init = jax.random.uniform(next(ks), (DEPTH, N_DIR, D_LRU), jnp.float32, 0.9, 0.999)
    n_idx = jnp.arange(S5_STATE, dtype=jnp.float32)
    return {
        "x_prompt": nrm((BATCH, SEQ, D), 1.0),
        "x_sample": nrm((DEC_BATCH, DEC_SEQ, D), 1.0),
        "c": nrm((DEC_BATCH, D), 1.0),
        "state_lru": nrm((DEC_BATCH, DEPTH, N_DIR, D_LRU), 0.5),
        "state_s5_re": nrm((DEC_BATCH, DEPTH, N_DIR, S5_GROUPS, S5_STATE), 0.5),
        "state_s5_im": nrm((DEC_BATCH, DEPTH, N_DIR, S5_GROUPS, S5_STATE), 0.5),
        "c_ctx": nrm((D,), 1.0),
        "w_mod": nrm((DEPTH, D, 6 * D), D ** -0.5),
        "b_mod": nrm((DEPTH, 6 * D), 0.02),
        "g_pre_mix": 1.0 + nrm((DEPTH, D), 0.05),
        "g_post_mix": 1.0 + nrm((DEPTH, D), 0.05),
        "g_pre_ffn": 1.0 + nrm((DEPTH, D), 0.05),
        "g_post_ffn": 1.0 + nrm((DEPTH, D), 0.05),
        "w_in": nrm((DEPTH, D, 2 * D_LRU + D_S5), D ** -0.5),
        "conv_w": nrm((DEPTH, CONV_W, D_LRU), CONV_W ** -0.5),
        "conv_b": nrm((DEPTH, D_LRU), 0.02),
        "lru_w_r": nrm((DEPTH, N_DIR, LRU_HEADS, LRU_BLOCK, LRU_BLOCK), LRU_BLOCK ** -0.5),
        "lru_b_r": nrm((DEPTH, N_DIR, D_LRU), 0.02),
        "lru_w_i": nrm((DEPTH, N_DIR, LRU_HEADS, LRU_BLOCK, LRU_BLOCK), LRU_BLOCK ** -0.5),
        "lru_b_i": nrm((DEPTH, N_DIR, D_LRU), 0.02),
        "lru_lambda": jnp.log(a_init / (1.0 - a_init)),
        "s5_a_re": -0.5 + nrm((DEPTH, N_DIR, S5_GROUPS, S5_STATE), 0.01),
        "s5_a_im": math.pi * n_idx + nrm((DEPTH, N_DIR, S5_GROUPS, S5_STATE), 0.01),
        "s5_log_dt": jax.random.uniform(next(ks), (DEPTH, N_DIR, S5_GROUPS), jnp.float32,
                                        math.log(1e-3), math.log(1e-1)),
        "s5_b_re": nrm((DEPTH, N_DIR, S5_GROUPS, S5_STATE, S5_GROUP), (2 * S5_GROUP) ** -0.5),
        "s5_b_im": nrm((DEPTH, N_DIR, S5_GROUPS, S5_STATE, S5_GROUP), (2 * S5_GROUP) ** -0.5),
        "s5_c_re": nrm((DEPTH, N_DIR, S5_GROUPS, S5_GROUP, S5_STATE), S5_STATE ** -0.5),
        "s5_c_im": nrm((DEPTH, N_DIR, S5_GROUPS, S5_GROUP, S5_STATE), S5_STATE ** -0.5),
        "s5_d": nrm((DEPTH, D_S5), 0.5),
        "s5_w_glu": nrm((DEPTH, D_S5, D_S5), D_S5 ** -0.5),
        "s5_b_glu": nrm((DEPTH, D_S5), 0.02),
        "w_proj_lru": nrm((DEPTH, D_LRU, D), D_LRU ** -0.5),
        "w_proj_s5": nrm((DEPTH, D_S5, D), D_S5 ** -0.5),
        "w_gate": nrm((DEPTH, D, 2 * D), D ** -0.5),
        "b_gate": nrm((DEPTH, 2 * D), 0.02),
        "w_out": nrm((DEPTH, D, D), D ** -0.5),
        "w_ff_in": nrm((DEPTH, D, 2 * D_FF), D ** -0.5),
        "w_ff_out": nrm((DEPTH, D_FF, D), D_FF ** -0.5),
    }


def reference(x_prompt, x_sample, c, state_lru, state_s5_re, state_s5_im, c_ctx,
              w_mod, b_mod, g_pre_mix, g_post_mix, g_pre_ffn, g_post_ffn, w_in, conv_w, conv_b,
              lru_w_r, lru_b_r, lru_w_i, lru_b_i, lru_lambda,
              s5_a_re, s5_a_im, s5_log_dt, s5_b_re, s5_b_im, s5_c_re, s5_c_im, s5_d, s5_w_glu, s5_b_glu,
              w_proj_lru, w_proj_s5, w_gate, b_gate, w_out, w_ff_in, w_ff_out):
    stacked = (("w_mod", w_mod), ("b_mod", b_mod), ("g_pre_mix", g_pre_mix), ("g_post_mix", g_post_mix),
               ("g_pre_ffn", g_pre_ffn), ("g_post_ffn", g_post_ffn), ("w_in", w_in), ("conv_w", conv_w),
               ("conv_b", conv_b), ("lru_w_r", lru_w_r), ("lru_b_r", lru_b_r), ("lru_w_i", lru_w_i),
               ("lru_b_i", lru_b_i), ("lru_lambda", lru_lambda), ("s5_a_re", s5_a_re), ("s5_a_im", s5_a_im),
               ("s5_log_dt", s5_log_dt), ("s5_b_re", s5_b_re), ("s5_b_im", s5_b_im), ("s5_c_re", s5_c_re),
               ("s5_c_im", s5_c_im), ("s5_d", s5_d), ("s5_w_glu", s5_w_glu), ("s5_b_glu", s5_b_glu),
               ("w_proj_lru", w_proj_lru), ("w_proj_s5", w_proj_s5), ("w_gate", w_gate), ("b_gate", b_gate),
               ("w_out", w_out), ("w_ff_in", w_ff_in), ("w_ff_out", w_ff_out))
    n_ctx = x_prompt.shape[0]
    ctx_c = jnp.broadcast_to(c_ctx, (n_ctx, D_MODEL))
    zero_lru = jnp.zeros((n_ctx, N_DIR, D_LRU), jnp.float32)
    zero_s5 = jnp.zeros((n_ctx, N_DIR, S5_GROUPS, S5_STATE), jnp.float32)
    y_prompt, y_sample = x_prompt, x_sample
    lru_list, re_list, im_list = [], [], []
    for l in range(DEPTH):
        p = {name: arr[l] for name, arr in stacked}
        y_prompt, f_lru, f_re, f_im = _layer(y_prompt, ctx_c, p, zero_lru, zero_s5, zero_s5,
                                             grid_order=False, return_state=True)
        lru_list.append(f_lru)
        re_list.append(f_re)
        im_list.append(f_im)
        y_sample, _, _, _ = _layer(y_sample, c, p, state_lru[:, l].astype(jnp.float32),
                                   state_s5_re[:, l], state_s5_im[:, l],
                                   grid_order=True, return_state=False)
    new_state_lru = jnp.stack(lru_list, axis=1)
    new_state_s5_re = jnp.stack(re_list, axis=1)
    new_state_s5_im = jnp.stack(im_list, axis=1)
    return (y_prompt, y_sample, new_state_lru, new_state_s5_re, new_state_s5_im)
```

```python
import numpy as np
import concourse.bass as bass
import concourse.mybir as mybir
from concourse.bass_utils import run_bass_kernel_spmd

F32, BF16 = mybir.dt.float32, mybir.dt.bfloat16
AF = mybir.ActivationFunctionType
ALU = mybir.AluOpType
PI = float(np.pi)
NV = 192
SP = 8
LP = 3
ECN = 512 // SP
ARENA_WORDS = 47000


class TK:
    def __init__(s, nc):
        s.nc = nc
        s.eng = {'pe': nc.tensor, 'act': nc.scalar, 'dve': nc.vector, 'pool': nc.gpsimd, 'sp': nc.sync}
        s.sem = {e: nc.alloc_semaphore('sem_' + e) for e in s.eng}
        s.cnt = {e: 0 for e in s.eng}
        s.seen = {e: {} for e in s.eng}
        s.lastw = {}
        s.readers = {}
        s.dsem = {}

    def _wait(s, e, ev):
        name, obj, val = ev
        if name == 'sem_' + e and e == 'pe':
            return
        if s.seen[e].get(name, 0) >= val:
            return
        s.eng[e].wait_ge(obj, val)
        s.seen[e][name] = val

    def deps(s, e, reads, writes):
        for k in reads:
            if k in s.lastw:
                s._wait(e, s.lastw[k])
        for k in writes:
            if k in s.lastw:
                s._wait(e, s.lastw[k])
            for ev in s.readers.get(k, {}).values():
                s._wait(e, ev)

    def commit(s, ev, reads, writes):
        for k in reads:
            s.readers.setdefault(k, {})[ev[0]] = ev
        for k in writes:
            s.lastw[k] = ev
            s.readers[k] = {}

    def op(s, e, fn, reads=(), writes=(), strict=False):
        s.deps(e, reads, writes)
        own = 'sem_' + e
        if not strict and e != 'pe':
            for k in list(reads) + list(writes):
                ev = s.lastw.get(k)
                if ev is not None and ev[0] == own and ev[2] >= s.cnt[e] - 1:
                    strict = True
            for k in writes:
                ev = s.readers.get(k, {}).get(own)
                if ev is not None and ev[2] >= s.cnt[e] - 1:
                    strict = True
        if strict and s.cnt[e] > 0:
            s.eng[e].wait_ge(s.sem[e], s.cnt[e])
        ins = fn(s.eng[e])
        s.cnt[e] += 1
        ins.then_inc(s.sem[e], 1)
        s.commit(('sem_' + e, s.sem[e], s.cnt[e]), reads, writes)

    def mm(s, mms, reads=(), writes=()):
        s.deps('pe', reads, writes)
        ins = None
        for (o, l, r, kw) in mms:
            ins = s.nc.tensor.matmul(o, lhsT=l, rhs=r, **kw)
        s.cnt['pe'] += 1
        ins.then_inc(s.sem['pe'], 1)
        s.commit(('sem_pe', s.sem['pe'], s.cnt['pe']), reads, writes)

    def dma(s, q, out, in_, reads=(), writes=(), skey=None):
        s.deps(q, reads, writes)
        if skey not in s.dsem:
            s.dsem[skey] = [s.nc.alloc_semaphore('d_' + skey), 0]
        d = s.dsem[skey]
        d[1] += 16
        s.eng[q].dma_start(out=out, in_=in_).then_inc(d[0], 16)
        s.commit(('d_' + skey, d[0], d[1]), reads, writes)

    def barrier(s):
        evs = [('sem_' + e, s.sem[e], s.cnt[e]) for e in s.eng if s.cnt[e] > 0]
        evs += [('d_' + k, d[0], d[1]) for k, d in s.dsem.items()]
        for e in s.eng:
            for ev in evs:
                s._wait(e, ev)
        s.lastw.clear()
        s.readers.clear()


class Arena:
    def __init__(s, t, words):
        s.t, s.words, s.off = t, words, 0

    def f32(s, n):
        n = (n + 7) // 8 * 8
        ap = s.t[:, s.off:s.off + n]
        s.off += n
        assert s.off <= s.words, ("arena overflow", s.off, s.words)
        return ap

    def bf16(s, n):
        w = ((n + 1) // 2 + 7) // 8 * 8
        ap = s.t[:, s.off:s.off + w].bitcast(BF16)
        s.off += w
        assert s.off <= s.words, ("arena overflow", s.off, s.words)
        return ap[:, 0:n]


def build_nc(debug=False, jobs_sel=(0, 1), phases=(1, 2, 3, 4)):
    nc = bass.Bass("TRN2", target_bir_lowering=False)
    D = 1024
    dbg_n = [0]

    def dbg(name, ap, keys):
        if not debug:
            return
        shp = [int(x) for x in ap.shape]
        o = nc.dram_tensor("dbg_" + name, shp, ap.dtype, kind="ExternalOutput").ap()
        tk.dma('pool', o, ap, reads=keys, writes=['dbgout'], skey='dbg%d' % (dbg_n[0] % 4))
        dbg_n[0] += 1

    def din(name, shape):
        return nc.dram_tensor(name, list(shape), F32, kind="ExternalInput").ap()

    def dout(name, shape):
        return nc.dram_tensor(name, list(shape), F32, kind="ExternalOutput").ap()

    def dscr(name, shape, dt=BF16):
        return nc.dram_tensor(name, list(shape), dt, kind="Internal").ap()

    xT = [din("xT_p", [D, 1024]), din("xT_s", [D, 4096])]
    yT = [dout("yT_p", [D, 1024]), dout("yT_s", [D, 4096])]
    cv_d = din("cv", [128, 16])
    vecs_d = din("vecs", [128, NV])
    lruw_d = din("lruw", [128, 32 * 128])
    h0l_d = din("h0l", [128, 16])
    s5p_d = din("s5p", [128, 5 * 32])
    s5bc_d = din("s5bc", [128, 4 * 32 * 16])
    ident_d = din("ident", [128, 128])
    w_mod = din("w_mod", [D, 6144])
    w_in = din("w_in", [D, 2560])
    w_gate = din("w_gate", [D, 2048])
    w_out = din("w_out", [D, D])
    w_pl = din("w_pl", [D, D])
    w_ps = din("w_ps", [512, D])
    w_fi = din("w_fi", [D, 5632])
    w_fo = din("w_fo", [2816, D])
    w_glu = din("w_glu", [512, 512])
    finl_d = dout("fin_lru", [128, 64])
    fins_d = dout("fin_s5", [128, 256])

    WG = dscr("WG", [D, 2048]); WO = dscr("WO", [D, D]); WPL = dscr("WPL", [D, D]); WPS = dscr("WPS", [512, D])
    WFI = dscr("WFI", [D, 5632]); WFO = dscr("WFO", [2816, D])
    hnD = [dscr("hnD0", [D, 1024]), dscr("hnD1", [D, 4096])]
    yaD = [dscr("yaD0", [D, 1024]), dscr("yaD1", [D, 4096])]
    vsD = [dscr("vsD0", [512, 1024]), dscr("vsD1", [512, 4096])]
    S5B = dscr("S5B", [8, 128, SP * 256])
    S5C = dscr("S5C", [2, 128, SP * 2048])
    ECD = dscr("ECD", [128, 2, 32 * ECN], F32)
    ysD = [dscr("ysD0", [512, 1024]), dscr("ysD1", [512, 4096])]

    tk = TK(nc)
    arena_t = nc.alloc_sbuf_tensor("arena", [128, ARENA_WORDS], F32)
    A = Arena(arena_t, ARENA_WORDS)
    PS2 = [nc.alloc_psum_tensor(f"pp{i}", [128, 1024], F32) for i in range(4)]
    PS = [PS2[i // 2][:, (i % 2) * 512:(i % 2) * 512 + 512] for i in range(8)]
    psc = [0]

    def nps(lo=0, hi=8):
        i = lo + psc[0] % (hi - lo)
        psc[0] += 1
        return i

    def kp(w, c0, c1):
        return w[:, c0:c1].rearrange("(k p) n -> p k n", p=128)

    vecs = A.f32(NV)
    cv = A.f32(16)
    h0l = A.f32(16)
    ident = A.f32(128)
    s5p = A.f32(160)
    lruw = A.bf16(32 * 128)
    ones_bf = A.bf16(128)
    epsc = A.f32(8)
    modc = A.f32(96)
    cA1 = A.f32(16); cB1 = A.f32(16); cG1 = A.f32(16); cA2 = A.f32(16); cB2 = A.f32(16); cG2 = A.f32(16)
    kco = A.f32(16)
    WKr = A.f32(9 * 32); WKi = A.f32(9 * 32)
    PWr = A.f32(16 * 32); PWi = A.f32(16 * 32)
    RP = A.f32(10 * 32)
    hm1r = A.f32(32); hm1i = A.f32(32)
    cth = A.f32(32); sth = A.f32(32); rho = A.f32(32)
    ini0r = A.f32(32); ini0i = A.f32(32)
    finl = A.f32(64)
    fins = A.f32(256)
    PERSIST = A.off

    def col(ap, i):
        return ap[:, i:i + 1]

    tk.dma('sp', vecs, vecs_d, writes=['vecs'], skey='ld0')
    tk.dma('sp', cv, cv_d, writes=['cv'], skey='ld1')
    tk.dma('sp', h0l, h0l_d, writes=['h0l'], skey='ld2')
    tk.dma('sp', ident, ident_d, writes=['ident'], skey='ld3')
    tk.dma('sp', s5p, s5p_d, writes=['s5p'], skey='ld4')
    tk.dma('pool', lruw, lruw_d, writes=['lruw'], skey='ld5')
    tk.op('dve', lambda e: e.memset(ones_bf, 1.0), writes=['ones'])
    tk.op('dve', lambda e: e.memset(epsc[:, 0:1], 1e-6), writes=['epsc'])
    tk.op('dve', lambda e: e.memset(epsc[:, 1:2], 1.0), writes=['epsc'])
    tk.op('dve', lambda e: e.memset(epsc[:, 2:3], 0.0), writes=['epsc'])
    tk.op('dve', lambda e: e.memset(finl, 0.0), writes=['finl'])
    tk.op('dve', lambda e: e.memset(fins, 0.0), writes=['fins'])
    EPS, ONE, ZERO = epsc[:, 0:1], epsc[:, 1:2], epsc[:, 2:3]

    m0 = A.off
    scb = A.bf16(16)
    tk.op('act', lambda e: e.activation(out=scb, in_=cv, func=AF.Silu), reads=['cv'], writes=['scb'])
    scb3 = scb.rearrange("p (k j) -> p k j", j=2)
    wms = [A.bf16(8 * 512) for _ in range(2)]
    psm = PS[7]
    for blk in range(12):
        wslot = wms[blk % 2].rearrange("p (k n) -> p k n", k=8)
        tk.dma('pool', wslot, kp(w_mod, blk * 512, blk * 512 + 512), writes=[f'wm{blk % 2}'], skey=f'wm{blk % 2}')
        for oc in range(4):
            c = blk * 4 + oc
            tk.mm([(psm[:, 2 * c:2 * c + 2], wslot[:, k, oc * 128:(oc + 1) * 128], scb3[:, k, :],
                    dict(start=(k == 0), stop=(k == 7))) for k in range(8)],
                  reads=[f'wm{blk % 2}', 'scb'], writes=['psm'])
    modc3 = modc.rearrange("p (c j) -> p c j", j=2)
    tk.op('dve', lambda e: e.tensor_tensor(out=modc3, in0=psm[:, 0:96].rearrange("p (c j) -> p c j", j=2),
                                           in1=vecs[:, 32:80].unsqueeze(2).broadcast_to([128, 48, 2]), op=ALU.add),
          reads=['psm', 'vecs'], writes=['modc'])

    def msec(s):
        return modc[:, 16 * s:16 * s + 16].rearrange("p (k j) -> p k j", j=2)

    def vb(c0):
        return vecs[:, c0:c0 + 8].unsqueeze(2).broadcast_to([128, 8, 2])

    def c3(ap):
        return ap.rearrange("p (k j) -> p k j", j=2)
    tk.op('dve', lambda e: e.scalar_tensor_tensor(out=c3(cA1), in0=msec(1), scalar=1.0, in1=vb(0), op0=ALU.add, op1=ALU.mult),
          reads=['modc'], writes=['cA1'])
    tk.op('dve', lambda e: e.tensor_copy(out=c3(cB1), in_=msec(0)), reads=['modc'], writes=['cB1'])
    tk.op('dve', lambda e: e.tensor_tensor(out=c3(cG1), in0=msec(2), in1=vb(8), op=ALU.mult), reads=['modc'], writes=['cG1'])
    tk.op('dve', lambda e: e.scalar_tensor_tensor(out=c3(cA2), in0=msec(4), scalar=1.0, in1=vb(16), op0=ALU.add, op1=ALU.mult),
          reads=['modc'], writes=['cA2'])
    tk.op('dve', lambda e: e.tensor_copy(out=c3(cB2), in_=msec(3)), reads=['modc'], writes=['cB2'])
    tk.op('dve', lambda e: e.tensor_tensor(out=c3(cG2), in0=msec(5), in1=vb(24), op=ALU.mult), reads=['modc'], writes=['cG2'])

    def cj(cst, k, j):
        return cst[:, 2 * k + j:2 * k + j + 1]

    tk.barrier()
    A.off = m0
    for (dst, src, rows, key) in ((WG, w_gate, D, 'cg'), (WPL, w_pl, D, 'cpl'), (WPS, w_ps, 512, 'cps'),
                                  (WO, w_out, D, 'co'), (WFI, w_fi, D, 'cfi'), (WFO, w_fo, 2816, 'cfo')):
        tk.dma('pool', dst.rearrange("(p a) n -> p (a n)", p=128), src.rearrange("(p a) n -> p (a n)", p=128),
               writes=['W' + key], skey=key)


    tl0 = A.f32(16)
    tk.op('act', lambda e: e.activation(out=tl0, in_=vecs[:, 152:168], func=AF.Exp, scale=-1.0), reads=['vecs'], writes=['tl0'])
    tk.op('act', lambda e: e.activation(out=tl0, in_=tl0, func=AF.Ln, bias=ONE, scale=1.0), reads=['epsc'], writes=['tl0'])
    tk.op('dve', lambda e: e.tensor_scalar(out=kco, in0=tl0, scalar1=-8.0, scalar2=None, op0=ALU.mult), reads=['tl0'], writes=['kco'])

    s5bc = A.f32(4 * 512)
    tk.dma('sp', s5bc, s5bc_d, writes=['s5bc'], skey='ld6')
    a_re, a_im, ldt = s5p[:, 0:32], s5p[:, 32:64], s5p[:, 64:96]
    h0r, h0i = s5p[:, 96:128], s5p[:, 128:160]
    T_ = [A.f32(32) for _ in range(12)]
    dt_, th_, r1, r2, nr, den, fre, fim, t8, t9, t10, t11 = T_

    def dv(fn, r, w):
        tk.op('dve', fn, reads=r, writes=w, strict=True)
    S = ['s5c']
    tk.op('act', lambda e: e.activation(out=dt_, in_=ldt, func=AF.Exp), reads=['s5p'], writes=S)
    dv(lambda e: e.tensor_tensor(out=t8, in0=a_re, in1=dt_, op=ALU.mult), S, S)
    tk.op('act', lambda e: e.activation(out=rho, in_=t8, func=AF.Exp), reads=S, writes=S)
    dv(lambda e: e.tensor_tensor(out=th_, in0=a_im, in1=dt_, op=ALU.mult), S, S)
    zi_t = nc.alloc_sbuf_tensor("zi_t", [128, 32], mybir.dt.int32)
    zi_ = zi_t[:, :]
    for (rr, sh) in ((r1, 8.0), (r2, 8.25)):
        dv(lambda e, rr=rr, sh=sh: e.tensor_scalar(out=rr, in0=th_, scalar1=1.0 / (2.0 * PI), scalar2=sh, op0=ALU.mult, op1=ALU.add), S, S)
        dv(lambda e, rr=rr: e.tensor_copy(out=zi_, in_=rr), S, S)
        dv(lambda e: e.tensor_copy(out=t8, in_=zi_), S, S)
        dv(lambda e, rr=rr: e.tensor_tensor(out=rr, in0=rr, in1=t8, op=ALU.subtract), S, S)
        dv(lambda e, rr=rr: e.tensor_scalar(out=t8, in0=rr, scalar1=0.5, scalar2=None, op0=ALU.is_gt), S, S)
        dv(lambda e, rr=rr: e.tensor_tensor(out=rr, in0=rr, in1=t8, op=ALU.subtract), S, S)
        dv(lambda e, rr=rr: e.tensor_scalar(out=rr, in0=rr, scalar1=2.0 * PI, scalar2=None, op0=ALU.mult), S, S)
    tk.op('act', lambda e: e.activation(out=sth, in_=r1, func=AF.Sin), reads=S, writes=S)
    tk.op('act', lambda e: e.activation(out=cth, in_=r2, func=AF.Sin), reads=S, writes=S)
    dv(lambda e: e.tensor_tensor(out=t9, in0=rho, in1=cth, op=ALU.mult), S, S)
    dv(lambda e: e.tensor_tensor(out=t10, in0=rho, in1=sth, op=ALU.mult), S, S)
    dv(lambda e: e.tensor_scalar(out=nr, in0=t9, scalar1=-1.0, scalar2=None, op0=ALU.add), S, S)
    dv(lambda e: e.tensor_tensor(out=den, in0=a_re, in1=a_re, op=ALU.mult), S, S)
    dv(lambda e: e.tensor_tensor(out=t8, in0=a_im, in1=a_im, op=ALU.mult), S, S)
    dv(lambda e: e.tensor_tensor(out=den, in0=den, in1=t8, op=ALU.add), S, S)
    dv(lambda e: e.reciprocal(out=den, in_=den), S, S)
    dv(lambda e: e.tensor_tensor(out=fre, in0=nr, in1=a_re, op=ALU.mult), S, S)
    dv(lambda e: e.tensor_tensor(out=t8, in0=t10, in1=a_im, op=ALU.mult), S, S)
    dv(lambda e: e.tensor_tensor(out=fre, in0=fre, in1=t8, op=ALU.add), S, S)
    dv(lambda e: e.tensor_tensor(out=fre, in0=fre, in1=den, op=ALU.mult), S, S)
    dv(lambda e: e.tensor_tensor(out=fim, in0=t10, in1=a_re, op=ALU.mult), S, S)
    dv(lambda e: e.tensor_tensor(out=t8, in0=nr, in1=a_im, op=ALU.mult), S, S)
    dv(lambda e: e.tensor_tensor(out=fim, in0=fim, in1=t8, op=ALU.subtract), S, S)
    dv(lambda e: e.tensor_tensor(out=fim, in0=fim, in1=den, op=ALU.mult), S, S)
    Bre = s5bc[:, 0:512].rearrange("p (u h) -> p u h", h=16)
    Bim = s5bc[:, 512:1024].rearrange("p (u h) -> p u h", h=16)
    Cre = s5bc[:, 1024:1536].rearrange("p (u h) -> p u h", h=16)
    Cim = s5bc[:, 1536:2048].rearrange("p (u h) -> p u h", h=16)
    bbr = A.f32(512); bbi = A.f32(512); tb = A.f32(512)
    bbr3 = bbr.rearrange("p (u h) -> p u h", h=16); bbi3 = bbi.rearrange("p (u h) -> p u h", h=16)
    tb3 = tb.rearrange("p (u h) -> p u h", h=16)

    def bc16(ap):
        return ap.unsqueeze(2).broadcast_to([128, 32, 16])
    S2 = ['s5c', 's5bc']
    dv(lambda e: e.tensor_tensor(out=bbr3, in0=Bre, in1=bc16(fre), op=ALU.mult), S2, S)
    dv(lambda e: e.tensor_tensor(out=tb3, in0=Bim, in1=bc16(fim), op=ALU.mult), S2, S)
    dv(lambda e: e.tensor_tensor(out=bbr3, in0=bbr3, in1=tb3, op=ALU.subtract), S, S)
    dv(lambda e: e.tensor_tensor(out=bbi3, in0=Bim, in1=bc16(fre), op=ALU.mult), S2, S)
    dv(lambda e: e.tensor_tensor(out=tb3, in0=Bre, in1=bc16(fim), op=ALU.mult), S2, S)
    dv(lambda e: e.tensor_tensor(out=bbi3, in0=bbi3, in1=tb3, op=ALU.add), S, S)
    BZr = A.f32(1024); BZi = A.f32(1024)
    BZr3 = BZr.rearrange("p (u m) -> p u m", m=32); BZi3 = BZi.rearrange("p (u m) -> p u m", m=32)
    C0r = A.f32(1024); C0i = A.f32(1024)
    C0r3 = C0r.rearrange("p (u m) -> p u m", m=32); C0i3 = C0i.rearrange("p (u m) -> p u m", m=32)
    for z_ in (BZr, BZi, C0r, C0i):
        dv(lambda e, z_=z_: e.memset(z_, 0.0), [], S)
    for (lo, hi, c0) in ((0, 64, 0), (64, 128, 16)):
        dv(lambda e, lo=lo, hi=hi, c0=c0: e.tensor_copy(out=BZr3[lo:hi, :, c0:c0 + 16], in_=bbr3[lo:hi]), S, S)
        dv(lambda e, lo=lo, hi=hi, c0=c0: e.tensor_copy(out=BZi3[lo:hi, :, c0:c0 + 16], in_=bbi3[lo:hi]), S, S)
        dv(lambda e, lo=lo, hi=hi, c0=c0: e.tensor_copy(out=C0r3[lo:hi, :, c0:c0 + 16], in_=Cre[lo:hi]), S2, S)
        dv(lambda e, lo=lo, hi=hi, c0=c0: e.tensor_copy(out=C0i3[lo:hi, :, c0:c0 + 16], in_=Cim[lo:hi]), S2, S)
    WKr3 = WKr.rearrange("p (k u) -> p k u", u=32); WKi3 = WKi.rearrange("p (k u) -> p k u", u=32)
    dv(lambda e: e.tensor_copy(out=WKr3[:, 0, :], in_=cth), S, S)
    dv(lambda e: e.tensor_scalar(out=WKi3[:, 0, :], in0=sth, scalar1=-1.0, scalar2=None, op0=ALU.mult), S, S)
    for k in range(8):
        dv(lambda e, k=k: e.tensor_tensor(out=t8, in0=WKr3[:, k, :], in1=WKr3[:, k, :], op=ALU.mult), S, S)
        dv(lambda e, k=k: e.tensor_tensor(out=t9, in0=WKi3[:, k, :], in1=WKi3[:, k, :], op=ALU.mult), S, S)
        dv(lambda e, k=k: e.tensor_tensor(out=WKr3[:, k + 1, :], in0=t8, in1=t9, op=ALU.subtract), S, S)
        dv(lambda e, k=k: e.tensor_tensor(out=t8, in0=WKr3[:, k, :], in1=WKi3[:, k, :], op=ALU.mult), S, S)
        dv(lambda e, k=k: e.tensor_scalar(out=WKi3[:, k + 1, :], in0=t8, scalar1=2.0, scalar2=None, op0=ALU.mult), S, S)
    PWr3 = PWr.rearrange("p (k u) -> p k u", u=32); PWi3 = PWi.rearrange("p (k u) -> p k u", u=32)
    RP3 = RP.rearrange("p (k u) -> p k u", u=32)
    dv(lambda e: e.memset(PWr3[:, 0, :], 1.0), [], S)
    dv(lambda e: e.memset(PWi3[:, 0, :], 0.0), [], S)
    dv(lambda e: e.memset(RP3[:, 0, :], 1.0), [], S)
    for k in range(15):
        dv(lambda e, k=k: e.tensor_tensor(out=t8, in0=PWr3[:, k, :], in1=cth, op=ALU.mult), S, S)
        dv(lambda e, k=k: e.tensor_tensor(out=t9, in0=PWi3[:, k, :], in1=sth, op=ALU.mult), S, S)
        dv(lambda e, k=k: e.tensor_tensor(out=PWr3[:, k + 1, :], in0=t8, in1=t9, op=ALU.subtract), S, S)
        dv(lambda e, k=k: e.tensor_tensor(out=t8, in0=PWr3[:, k, :], in1=sth, op=ALU.mult), S, S)
        dv(lambda e, k=k: e.tensor_tensor(out=t9, in0=PWi3[:, k, :], in1=cth, op=ALU.mult), S, S)
        dv(lambda e, k=k: e.tensor_tensor(out=PWi3[:, k + 1, :], in0=t8, in1=t9, op=ALU.add), S, S)
    for k in range(9):
        dv(lambda e, k=k: e.tensor_tensor(out=RP3[:, k + 1, :], in0=RP3[:, k, :], in1=rho, op=ALU.mult), S, S)
    dv(lambda e: e.tensor_tensor(out=t8, in0=cth, in1=h0r, op=ALU.mult), ['s5c', 's5p'], S)
    dv(lambda e: e.tensor_tensor(out=t9, in0=sth, in1=h0i, op=ALU.mult), ['s5c', 's5p'], S)
    dv(lambda e: e.tensor_tensor(out=ini0r, in0=t8, in1=t9, op=ALU.subtract), S, S)
    dv(lambda e: e.tensor_tensor(out=t8, in0=sth, in1=h0r, op=ALU.mult), ['s5c', 's5p'], S)
    dv(lambda e: e.tensor_tensor(out=t9, in0=cth, in1=h0i, op=ALU.mult), ['s5c', 's5p'], S)
    dv(lambda e: e.tensor_tensor(out=ini0i, in0=t8, in1=t9, op=ALU.add), S, S)
    dv(lambda e: e.tensor_tensor(out=t8, in0=PWr3[:, SP - 1, :], in1=h0r, op=ALU.mult), ['s5c', 's5p'], S)
    dv(lambda e: e.tensor_tensor(out=t9, in0=PWi3[:, SP - 1, :], in1=h0i, op=ALU.mult), ['s5c', 's5p'], S)
    dv(lambda e: e.tensor_tensor(out=hm1r, in0=t8, in1=t9, op=ALU.add), S, S)
    dv(lambda e: e.tensor_tensor(out=t8, in0=PWr3[:, SP - 1, :], in1=h0i, op=ALU.mult), ['s5c', 's5p'], S)
    dv(lambda e: e.tensor_tensor(out=t9, in0=PWi3[:, SP - 1, :], in1=h0r, op=ALU.mult), ['s5c', 's5p'], S)
    dv(lambda e: e.tensor_tensor(out=hm1i, in0=t8, in1=t9, op=ALU.subtract), S, S)
    ECr_t = A.f32(32 * ECN); ECi_t = A.f32(32 * ECN)
    ECr3 = ECr_t.rearrange("p (u c) -> p u c", c=ECN); ECi3 = ECi_t.rearrange("p (u c) -> p u c", c=ECN)
    WKr3 = WKr.rearrange("p (k u) -> p k u", u=32); WKi3 = WKi.rearrange("p (k u) -> p k u", u=32)
    eq1 = A.f32(16 * ECN).rearrange("p (u c) -> p u c", c=ECN // 2)
    eq2 = A.f32(16 * ECN).rearrange("p (u c) -> p u c", c=ECN // 2)
    EK_ = ['s5c']
    dv(lambda e: e.memset(ECr3[:, :, 0:1], 1.0), [], EK_)
    dv(lambda e: e.memset(ECi3[:, :, 0:1], 0.0), [], EK_)
    for k in range(ECN.bit_length() - 1):
        n = 1 << k
        wr = WKr3[:, LP + k, :].unsqueeze(2).broadcast_to([128, 32, n])
        wi = WKi3[:, LP + k, :].unsqueeze(2).broadcast_to([128, 32, n])
        e0r, e0i = ECr3[:, :, 0:n], ECi3[:, :, 0:n]
        q1, q2 = eq1[:, :, 0:n], eq2[:, :, 0:n]
        dv(lambda e: e.tensor_tensor(out=q1, in0=e0r, in1=wr, op=ALU.mult), EK_, EK_)
        dv(lambda e: e.tensor_tensor(out=q2, in0=e0i, in1=wi, op=ALU.mult), EK_, EK_)
        dv(lambda e: e.tensor_tensor(out=ECr3[:, :, n:2 * n], in0=q1, in1=q2, op=ALU.subtract), EK_, EK_)
        dv(lambda e: e.tensor_tensor(out=q1, in0=e0r, in1=wi, op=ALU.mult), EK_, EK_)
        dv(lambda e: e.tensor_tensor(out=q2, in0=e0i, in1=wr, op=ALU.mult), EK_, EK_)
        dv(lambda e: e.tensor_tensor(out=ECi3[:, :, n:2 * n], in0=q1, in1=q2, op=ALU.add), EK_, EK_)
    tk.dma('pool', ECD[:, 0, :], ECr_t, reads=['s5c'], writes=['ECD'], skey='ecd0')
    tk.dma('pool', ECD[:, 1, :], ECi_t, reads=['s5c'], writes=['ECD'], skey='ecd1')
    stB = A.bf16(8 * SP * 256).rearrange("p (x j c m) -> p x j c m", x=8, j=SP, c=2)
    stC = A.bf16(SP * 2048).rearrange("p (t j c u m) -> p t j c u m", t=2, j=SP, c=2, u=16)
    ZA = [[A.f32(1024), A.f32(1024)] for _ in range(1)]
    zt = [eq1.rearrange("p u c -> p (u c)")[:, 0:1024], eq2.rearrange("p u c -> p (u c)")[:, 0:1024]]
    pt_ = [A.f32(1024) for _ in range(2)]
    ff = A.f32(64)
    ffr, ffi = ff[:, 0:32], ff[:, 32:64]

    def v33(ap):
        return ap.rearrange("p (u m) -> p u m", m=32)

    def pl_(fn, r, w):
        tk.op('pool', fn, reads=r, writes=w)

    def dv2(fn, r, w):
        tk.op('dve', fn, reads=r, writes=w)

    def b32(ap):
        return ap.unsqueeze(2).broadcast_to([128, 32, 32])
    for jj in range(SP):
        pr, pi_ = b32(PWr3[:, jj, :]), b32(PWi3[:, jj, :])
        zb = 0
        Zr_, Zi_ = ZA[zb]
        ZK = [f'Z{zb}']
        z0, z1 = v33(zt[0]), v33(zt[1])
        dv2(lambda e: e.tensor_tensor(out=z0, in0=BZr3, in1=pr, op=ALU.mult), S, ['zt0'])
        dv2(lambda e: e.tensor_tensor(out=z1, in0=BZi3, in1=pi_, op=ALU.mult), S, ['zt1'])
        dv2(lambda e: e.tensor_tensor(out=v33(Zr_), in0=z0, in1=z1, op=ALU.add), ['zt0', 'zt1'], ZK)
        dv2(lambda e: e.tensor_tensor(out=z0, in0=BZi3, in1=pr, op=ALU.mult), S, ['zt0'])
        dv2(lambda e: e.tensor_tensor(out=z1, in0=BZr3, in1=pi_, op=ALU.mult), S, ['zt1'])
        dv2(lambda e: e.tensor_tensor(out=v33(Zi_), in0=z0, in1=z1, op=ALU.subtract), ['zt0', 'zt1'], ZK)
        for c, Z_ in enumerate((Zr_, Zi_)):
            for q in range(4):
                for d in range(2):
                    u0 = d * 16 + 4 * q
                    pi = nps(0, 8)
                    tk.deps('pe', ZK + ['ident'], [f'ps{pi}'])
                    ins = nc.tensor.transpose(PS[pi][:, 0:128], Z_[:, u0 * 32:u0 * 32 + 128], ident)
                    tk.cnt['pe'] += 1
                    ins.then_inc(tk.sem['pe'], 1)
                    tk.commit(('sem_pe', tk.sem['pe'], tk.cnt['pe']), ZK + ['ident'], [f'ps{pi}'])
                    tk.op('act', lambda e, pi=pi, c=c, q=q, d=d: e.copy(out=stB[:, q * 2 + d, jj, c, :], in_=PS[pi][:, 0:128]),
                          reads=[f'ps{pi}'], writes=['stB'])
    for d in range(2):
        U16 = slice(d * 16, d * 16 + 16)
        for jj in range(SP):
            pr, pi_ = b32(PWr3[:, jj, :]), b32(PWi3[:, jj, :])
            z0, z1 = v33(zt[0]), v33(zt[1])
            pl_(lambda e: e.tensor_tensor(out=ffr, in0=RP3[:, jj + 1, :], in1=PWr3[:, jj + SP, :], op=ALU.mult), S, ['ff'])
            pl_(lambda e: e.tensor_tensor(out=ffi, in0=RP3[:, jj + 1, :], in1=PWi3[:, jj + SP, :], op=ALU.mult), S, ['ff'])
            fr_, fi_ = b32(ffr), b32(ffi)
            for (tsel, xr, xi, rk, fn_, t0_, t1_, tk0, tk1) in ((0, pr, pi_, S, dv2, z0, z1, 'zt0', 'zt1'),
                                                               (1, fr_, fi_, ['ff'] + S, pl_, v33(pt_[0]), v33(pt_[1]), 'pt0', 'pt1')):
                dre = stC[:, tsel, jj, 0, :, :]
                dim = stC[:, tsel, jj, 1, :, :]
                fn_(lambda e: e.tensor_tensor(out=t0_[:, U16, :], in0=C0r3[:, U16, :], in1=xr[:, U16, :], op=ALU.mult), rk, [tk0])
                fn_(lambda e: e.tensor_tensor(out=t1_[:, U16, :], in0=C0i3[:, U16, :], in1=xi[:, U16, :], op=ALU.mult), rk, [tk1])
                fn_(lambda e: e.tensor_tensor(out=dre, in0=t0_[:, U16, :], in1=t1_[:, U16, :], op=ALU.subtract), [tk0, tk1], ['stC'])
                fn_(lambda e: e.tensor_tensor(out=t0_[:, U16, :], in0=C0r3[:, U16, :], in1=xi[:, U16, :], op=ALU.mult), rk, [tk0])
                fn_(lambda e: e.tensor_tensor(out=t1_[:, U16, :], in0=C0i3[:, U16, :], in1=xr[:, U16, :], op=ALU.mult), rk, [tk1])
                fn_(lambda e: e.tensor_tensor(out=t0_[:, U16, :], in0=t0_[:, U16, :], in1=t1_[:, U16, :], op=ALU.add), [tk0, tk1], [tk0])
                fn_(lambda e: e.tensor_scalar(out=dim, in0=t0_[:, U16, :], scalar1=-1.0, scalar2=None, op0=ALU.mult), [tk0], ['stC'])
        tk.dma('pool', S5C[d], stC.rearrange("p t j c u m -> p (t j c u m)"), reads=['stC'], writes=['S5C'], skey='stC')
    for x in range(8):
        tk.dma('pool', S5B[x], stB[:, x].rearrange("p j c m -> p (j c m)"), reads=['stB'], writes=['S5B'], skey='stB')
    for nm, ap_ in (('rho', rho), ('cth', cth), ('sth', sth), ('PWr', PWr), ('PWi', PWi), ('RP', RP),
                    ('hm1r', hm1r), ('ini0r', ini0r)):
        dbg(nm, ap_, ['s5c'])
    tk.barrier()
    A.off = PERSIST

    jobs = [dict(j=0, nseq=4, T=256, TL=256), dict(j=1, nseq=1, T=4096, TL=512)]

    def rms_rstd(sq3, rstd, keyin):
        pi = nps()
        TL = rstd.shape[1]
        tk.mm([(PS[pi][:, 0:TL], ones_bf, sq3[:, k, :], dict(start=(k == 0), stop=(k == 7))) for k in range(8)],
              reads=[keyin, 'ones'], writes=[f'ps{pi}'])
        tk.op('act', lambda e: e.activation(out=rstd, in_=PS[pi][:, 0:TL], func=AF.Sqrt, bias=EPS, scale=1.0 / D),
              reads=[f'ps{pi}', 'epsc'], writes=['rstd'])
        tk.op('dve', lambda e: e.reciprocal(out=rstd, in_=rstd), reads=['rstd'], writes=['rstd'])

    for job in jobs:
        if job['j'] not in jobs_sel:
            continue
        j, nseq, T, TL = job['j'], job['nseq'], job['T'], job['TL']
        NT = nseq * T
        ntile = NT // TL
        tps = T // TL
        xTj, yTj = xT[j], yT[j]
        hnDk = hnD[j].rearrange("(k p) t -> p k t", p=128)
        yaDk = yaD[j].rearrange("(k p) t -> p k t", p=128)
        ysDk = ysD[j].rearrange("(k p) t -> p k t", p=128)
        xk = xTj.rearrange("(k p) t -> p k t", p=128)
        yk = yTj.rearrange("(k p) t -> p k t", p=128)

        A.off = PERSIST
        xs = [A.f32(8 * TL).rearrange("p (k t) -> p k t", k=8) for _ in range(2)]
        sqs = A.bf16(8 * TL).rearrange("p (k t) -> p k t", k=8)
        hno = [A.bf16(8 * TL).rearrange("p (k t) -> p k t", k=8) for _ in range(2)]
        rstd = A.f32(TL)
        tmp = [A.f32(TL) for _ in range(2)]
        for ti in range(ntile):
            t0 = ti * TL
            sl = ti % 2
            tk.dma('sp', xs[sl], xk[:, :, t0:t0 + TL], writes=[f'x{sl}'], skey=f'x{sl}')
            tk.op('act', lambda e: e.activation(out=sqs, in_=xs[sl], func=AF.Square), reads=[f'x{sl}'], writes=['sqs'])
            rms_rstd(sqs, rstd, 'sqs')
            for k in range(8):
                tm = tmp[k % 2]
                tk.op('dve', lambda e: e.scalar_tensor_tensor(out=tm, in0=xs[sl][:, k, :], scalar=cj(cA1, k, j), in1=rstd,
                                                              op0=ALU.mult, op1=ALU.mult),
                      reads=[f'x{sl}', 'rstd'], writes=[f'tmp{k % 2}'])
                tk.op('act', lambda e: e.activation(out=hno[sl][:, k, :], in_=tm, func=AF.Identity, bias=cj(cB1, k, j), scale=1.0),
                      reads=[f'tmp{k % 2}'], writes=[f'hno{sl}'])
            tk.dma('pool', hnDk[:, :, t0:t0 + TL], hno[sl], reads=[f'hno{sl}'], writes=['hnD'], skey=f'hno{sl}')
        tk.barrier()

        A.off = PERSIST
        G = 1024
        ngrp = NT // G
        SEG = min(T, G)
        spg = G // SEG
        xa_pad = A.f32(nseq * (T + 3)).rearrange("p (s t) -> p s t", s=nseq)
        u_f = A.f32(NT)
        u_b = A.bf16(NT)
        hf = A.f32(NT)
        u3 = u_f.rearrange("p (s t) -> p s t", s=nseq)
        wxa = [A.bf16(8 * 128).rearrange("p (k n) -> p k n", k=8) for _ in range(2)]
        wga = [A.bf16(8 * 128).rearrange("p (k n) -> p k n", k=8) for _ in range(2)]
        hnl = [A.bf16(8 * 512).rearrange("p (k t) -> p k t", k=8) for _ in range(3)]
        NB = 2
        tr = [[A.f32(G) for _ in range(NB)] for _ in range(5)]
        cb = A.f32(8)
        cbc = [0]
        yao = [A.bf16(G) for _ in range(2)]
        lruw3 = lruw.rearrange("p (i m) -> p i m", m=128)
        hnc = [0]

        def load_hn(t0):
            sl = hnc[0] % 3
            hnc[0] += 1
            tk.dma('sp', hnl[sl], hnDk[:, :, t0:t0 + 512], writes=[f'hnl{sl}'], skey=f'hnl{sl}')
            return sl
        tk.op('dve', lambda e: e.memset(xa_pad, 0.0), writes=['xa_pad'])
        it = [0]
        psR, psI = PS2[0][:, :], PS2[1][:, :]
        def conv_grp(g, q):
            c0 = g * 1024
            uo = u_f[:, c0:c0 + 1024]
            tk.op('dve', lambda e: e.tensor_scalar(out=uo, in0=xa_pad[:, 0, c0:c0 + 1024], scalar1=col(vecs, 80 + q), scalar2=col(vecs, 112 + q),
                                                   op0=ALU.mult, op1=ALU.add), reads=['xa_pad', 'vecs'], writes=['u_f'])
            for tap in range(1, 4):
                tk.op('dve', lambda e: e.scalar_tensor_tensor(out=uo, in0=xa_pad[:, 0, c0 + tap:c0 + tap + 1024], scalar=col(vecs, 80 + 8 * tap + q),
                                                              in1=uo, op0=ALU.mult, op1=ALU.add), reads=['xa_pad', 'u_f'], writes=['u_f'])
            tk.op('act', lambda e: e.copy(out=u_b[:, c0:c0 + 1024], in_=uo), reads=['u_f'], writes=['u_b'])

        for q in range(8 if 2 in phases else 0):
            ws = q % 2
            tk.dma('pool', wxa[ws], kp(w_in, q * 128, q * 128 + 128), writes=[f'wxa{ws}'], skey=f'wxa{ws}')
            tk.dma('pool', wga[ws], kp(w_in, 1024 + q * 128, 1024 + q * 128 + 128), writes=[f'wga{ws}'], skey=f'wga{ws}')
            for ti in range(NT // 512):
                t0 = ti * 512
                sl = load_hn(t0)
                pi = nps(4, 8)
                tk.mm([(PS[pi], wxa[ws][:, k, :], hnl[sl][:, k, :], dict(start=(k == 0), stop=(k == 7))) for k in range(8)],
                      reads=[f'wxa{ws}', f'hnl{sl}'], writes=[f'ps{pi}'])
                if T >= 512:
                    s_, tt = t0 // T, t0 % T
                    o_, i_ = xa_pad[:, s_, 2 + tt:2 + tt + 512], PS[pi]
                else:
                    ns_ = 512 // T
                    s_ = t0 // T
                    o_, i_ = xa_pad[:, s_:s_ + ns_, 2:2 + T], PS[pi].rearrange("p (s t) -> p s t", s=ns_)
                tk.op('dve', lambda e: e.tensor_copy(out=o_, in_=i_), reads=[f'ps{pi}'], writes=['xa_pad'])
                if T >= 2048 and ti >= 2 and ti % 2 == 0:
                    conv_grp((ti - 2) // 2, q)
            if T >= 2048:
                conv_grp(NT // 1024 - 1, q)
            else:
                tk.op('dve', lambda e: e.tensor_scalar(out=u3, in0=xa_pad[:, :, 0:T], scalar1=col(vecs, 80 + q), scalar2=col(vecs, 112 + q),
                                                       op0=ALU.mult, op1=ALU.add), reads=['xa_pad', 'vecs'], writes=['u_f'])
                for tap in range(1, 4):
                    tk.op('dve', lambda e: e.scalar_tensor_tensor(out=u3, in0=xa_pad[:, :, tap:tap + T], scalar=col(vecs, 80 + 8 * tap + q),
                                                                  in1=u3, op0=ALU.mult, op1=ALU.add), reads=['xa_pad', 'u_f'], writes=['u_f'])
                tk.op('act', lambda e: e.copy(out=u_b, in_=u_f), reads=['u_f'], writes=['u_b'])
            for d in range(2):
                carry = (col(h0l, d * 8 + q) if j == 1 else ZERO)
                ckey = 'h0l' if j == 1 else 'epsc'
                order = list(range(ngrp)) if d == 0 else list(range(ngrp - 1, -1, -1))
                for r0_ in range(0, len(order), 2):
                    rnd = order[r0_:r0_ + 2]
                    ctxs = []
                    for g in rnd:
                        g0 = g * G
                        b_ = it[0] % NB
                        it[0] += 1
                        r_, i_, a_, a2_, iu_ = [tr[x][b_] for x in range(5)]
                        ctxs.append((g, g0, b_, r_, i_, a_, a2_, iu_))
                        K = lambda n, b_=b_: f'{n}{b_}'
                        for h in range(2):
                            ub = u_b[:, g0 + h * 512:g0 + (h + 1) * 512]
                            tk.mm([(psR[:, h * 512:(h + 1) * 512], lruw3[:, (d * 2 + 0) * 8 + q, :], ub, dict(start=True, stop=True))],
                                  reads=['lruw', 'u_b'], writes=['psR'])
                            tk.mm([(psI[:, h * 512:(h + 1) * 512], lruw3[:, (d * 2 + 1) * 8 + q, :], ub, dict(start=True, stop=True))],
                                  reads=['lruw', 'u_b'], writes=['psI'])
                        tk.op('act', lambda e: e.activation(out=r_, in_=psR, func=AF.Sigmoid, bias=col(vecs, 120 + d * 8 + q), scale=1.0),
                              reads=['psR'], writes=[K('r')])
                        tk.op('act', lambda e: e.activation(out=i_, in_=psI, func=AF.Sigmoid, bias=col(vecs, 136 + d * 8 + q), scale=1.0),
                              reads=['psI'], writes=[K('i')])
                    for (g, g0, b_, r_, i_, a_, a2_, iu_) in ctxs:
                        K = lambda n, b_=b_: f'{n}{b_}'
                        tk.op('act', lambda e: e.activation(out=a_, in_=r_, func=AF.Exp, scale=col(kco, d * 8 + q)),
                              reads=[K('r'), 'kco'], writes=[K('a')])
                        tk.op('dve', lambda e: e.tensor_tensor(out=a2_, in0=a_, in1=a_, op=ALU.mult), reads=[K('a')], writes=[K('a2')])
                        tk.op('pool', lambda e: e.tensor_tensor(out=iu_, in0=i_, in1=u_f[:, g0:g0 + G], op=ALU.mult),
                              reads=[K('i'), 'u_f'], writes=[K('iu')])
                    for (g, g0, b_, r_, i_, a_, a2_, iu_) in ctxs:
                        K = lambda n, b_=b_: f'{n}{b_}'
                        tk.op('act', lambda e: e.activation(out=a2_, in_=a2_, func=AF.Sqrt, bias=ONE, scale=-1.0),
                              reads=[K('a2'), 'epsc'], writes=[K('a2')])
                        tk.op('dve', lambda e: e.tensor_tensor(out=iu_, in0=a2_, in1=iu_, op=ALU.mult), reads=[K('a2'), K('iu')], writes=[K('iu')])
                    gel = []
                    for (g, g0, b_, r_, i_, a_, a2_, iu_) in ctxs:
                        K = lambda n, b_=b_: f'{n}{b_}'
                        bb_, hb_ = iu_, r_
                        for ss in range(spg):
                            lo = ss * SEG
                            gs = g0 + lo
                            sq_ = gs // T
                            if j == 0:
                                carry, ckey = ZERO, 'epsc'
                            if d == 0:
                                tk.op('dve', lambda e: e.tensor_tensor_scan(out=hf[:, gs:gs + SEG], data0=a_[:, lo:lo + SEG], data1=bb_[:, lo:lo + SEG],
                                                                             initial=carry, op0=ALU.mult, op1=ALU.add),
                                      reads=[K('a'), K('iu'), ckey], writes=['hf'])
                                carry, ckey = hf[:, gs + SEG - 1:gs + SEG], 'hf'
                                if j == 0:
                                    tk.op('pool', lambda e: e.tensor_copy(out=col(finl, (sq_ * 2 + 0) * 8 + q), in_=carry), reads=['hf'], writes=['finl'])
                            else:
                                tk.op('dve', lambda e: e.tensor_tensor_scan(out=hb_[:, lo:lo + SEG][:, ::-1], data0=a_[:, lo:lo + SEG][:, ::-1],
                                                                             data1=bb_[:, lo:lo + SEG][:, ::-1], initial=carry, op0=ALU.mult, op1=ALU.add),
                                      reads=[K('a'), K('iu'), ckey], writes=[K('r')])
                                cbi = cbc[0] % 8
                                cbc[0] += 1
                                tk.op('dve', lambda e: e.tensor_copy(out=cb[:, cbi:cbi + 1], in_=hb_[:, lo:lo + 1]), reads=[K('r')], writes=['cb'])
                                carry, ckey = cb[:, cbi:cbi + 1], 'cb'
                                if j == 0:
                                    tk.op('pool', lambda e: e.tensor_copy(out=col(finl, (sq_ * 2 + 1) * 8 + q), in_=carry), reads=[ckey], writes=['finl'])
                        if d == 1:
                            tk.op('dve', lambda e: e.tensor_tensor(out=i_, in0=hb_, in1=hf[:, g0:g0 + G], op=ALU.add),
                                  reads=[K('r'), 'hf'], writes=[K('i')])
                            pgi = b_
                            pg2 = PS2[2 + pgi][:, :]
                            pgb = [f'ps{4 + 2 * pgi}', f'ps{5 + 2 * pgi}']
                            for h in range(2):
                                sl = load_hn(g0 + h * 512)
                                tk.mm([(pg2[:, h * 512:(h + 1) * 512], wga[ws][:, k, :], hnl[sl][:, k, :], dict(start=(k == 0), stop=(k == 7))) for k in range(8)],
                                      reads=[f'wga{ws}', f'hnl{sl}'], writes=pgb)
                            gel.append((g0, b_, i_, a_, pg2, pgb))
                    for (g0, b_, i_, a_, pg2, pgb) in gel:
                        K = lambda n, b_=b_: f'{n}{b_}'
                        tk.op('act', lambda e: e.activation(out=a_, in_=pg2, func=AF.Gelu_apprx_tanh), reads=pgb + [K('a')], writes=[K('a')])
                        tk.op('dve', lambda e: e.tensor_tensor(out=yao[b_], in0=a_, in1=i_, op=ALU.mult),
                              reads=[K('a'), K('i')], writes=[f'yao{b_}'])
                        tk.dma('pool', yaD[j][q * 128:(q + 1) * 128, g0:g0 + G], yao[b_], reads=[f'yao{b_}'], writes=['yaD'], skey=f'yao{b_}')
        tk.barrier()

        A.off = PERSIST
        nch = TL // SP
        sg = [A.f32(TL) for _ in range(2)]
        vso = [A.bf16(TL) for _ in range(2)]
        P3MARK = A.off
        us_f = A.f32(NT)
        hs_f = A.f32(NT)
        usb = [A.bf16(NT) for _ in range(2)]
        wus = A.bf16(8 * 128).rearrange("p (k n) -> p k n", k=8)
        WB = SP * 256
        ECs2 = [A.f32(2 * 4 * ECN).rearrange("p (c u n) -> p c u n", c=2, u=4) for _ in range(2)]
        wS2 = [A.bf16(3 * WB) for _ in range(2)]
        Dt2 = [A.f32(4 * TL).rearrange("p (u t) -> p u t", u=4) for _ in range(2)]
        gR = A.f32(4 * TL).rearrange("p (u t) -> p u t", u=4)
        gI = A.f32(4 * TL).rearrange("p (u t) -> p u t", u=4)
        hnl = [gR.rearrange("p u t -> p (u t)").bitcast(BF16)[:, 0:8 * TL].rearrange("p (k t) -> p k t", k=8),
               gI.rearrange("p u t -> p (u t)").bitcast(BF16)[:, 0:8 * TL].rearrange("p (k t) -> p k t", k=8)]
        GA = [[f'gR{pl}' for pl in range(4)], [f'gI{pl}' for pl in range(4)]]
        gRb2 = [A.bf16(4 * TL).rearrange("p (u t) -> p u t", u=4) for _ in range(3)]
        gIb2 = [A.bf16(4 * TL).rearrange("p (u t) -> p u t", u=4) for _ in range(3)]

        def c4(n=nch):
            return A.f32(4 * n).rearrange("p (u c) -> p u c", u=4)
        X1, X2, Xr, Xi, Kr, Ki = [c4() for _ in range(6)]
        Lr2 = [c4() for _ in range(3)]
        Li2 = [c4() for _ in range(3)]
        HBr = [c4(nch + 1) for _ in range(2)]
        HBi = [c4(nch + 1) for _ in range(2)]
        HBrb2 = [A.bf16(4 * nch).rearrange("p (u c) -> p u c", u=4) for _ in range(2)]
        HBib2 = [A.bf16(4 * nch).rearrange("p (u c) -> p u c", u=4) for _ in range(2)]
        Km1r = [A.f32(8)[:, 0:4] for _ in range(2)]
        Km1i = [A.f32(8)[:, 0:4] for _ in range(2)]
        tq4 = [A.f32(8)[:, 0:4] for _ in range(2)]
        PWr3 = PWr.rearrange("p (k u) -> p k u", u=32); PWi3 = PWi.rearrange("p (k u) -> p k u", u=32)
        RP3 = RP.rearrange("p (k u) -> p k u", u=32)
        CK = ['chunk']

        def pv(fn, r, w):
            tk.op('pool', fn, reads=r, writes=w)
        tcnt = [0]
        for q in range(4):
            tk.dma('pool', wus, kp(w_in, 2048 + q * 128, 2048 + q * 128 + 128), writes=['wus'], skey='wus')
            for ti in range(ntile):
                hs_ = ti % 2
                tk.dma('sp', hnl[hs_], hnDk[:, :, ti * TL:(ti + 1) * TL], writes=[f'hnl{hs_}'] + GA[hs_], skey=f'hnl{hs_}')
                pi = nps(0, 2)
                tk.mm([(PS[pi][:, 0:TL], wus[:, k, :], hnl[hs_][:, k, :], dict(start=(k == 0), stop=(k == 7))) for k in range(8)],
                      reads=['wus', f'hnl{hs_}'], writes=[f'ps{pi}'])
                if j == 1:
                    r0 = ti * 8
                    o_ = us_f.rearrange("p (c r) -> p r c", r=64)[:, r0:r0 + 8, :]
                    i_ = PS[pi][:, 0:512].rearrange("p (r c) -> p r c", c=64)
                else:
                    o_ = us_f[:, ti * TL:(ti + 1) * TL]
                    i_ = PS[pi][:, 0:TL]
                tk.op('act', lambda e: e.copy(out=o_, in_=i_), reads=[f'ps{pi}'], writes=['us_f'])
            us3 = us_f.rearrange("p (s t) -> p s t", s=nseq)
            if q == 0:
                dbg(f'usf{j}', us_f, ['us_f'])
            tk.op('act', lambda e: e.copy(out=usb[0], in_=us_f), reads=['us_f'], writes=['usb0'])
            tk.op('pool', lambda e: e.tensor_copy(out=usb[1].rearrange("p (s t) -> p s t", s=nseq), in_=us3[:, :, ::-1]),
                  reads=['us_f'], writes=['usb1'])
            res = {}
            for d in range(2):
                u0 = d * 16 + 4 * q
                U4 = slice(u0, u0 + 4)
                wS, Dt, ECs = wS2[d], Dt2[d], ECs2[d]
                w_BT = wS[:, 0:WB].rearrange("p (j c m) -> p j c m", j=SP, c=2)
                w_CL = wS[:, WB:2 * WB].rearrange("p (j c u m) -> p j c u m", j=SP, c=2, u=4)
                w_CC = wS[:, 2 * WB:3 * WB].rearrange("p (j c u m) -> p j c u m", j=SP, c=2, u=4)
                tk.dma('sp', wS[:, 0:WB], S5B[q * 2 + d], writes=[f'wS{d}'], skey=f'wSb{d}')
                tk.dma('sp', wS[:, WB:3 * WB].rearrange("p (x u m) -> p x u m", u=4, m=32),
                       S5C[d].rearrange("p (x u m) -> p x u m", u=16, m=32)[:, :, 4 * q:4 * q + 4, :], writes=[f'wS{d}'], skey=f'wSc{d}')
                tk.op('dve', lambda e: e.tensor_copy(out=Dt, in_=rho[:, U4].unsqueeze(2).broadcast_to([128, 4, TL])), reads=['s5c'], writes=[f'Dt{d}'])
                tk.op('dve', lambda e: e.memset(Dt[:, :, 0::SP], 0.0), writes=[f'Dt{d}'])
                tk.dma('sp', ECs, ECD.rearrange("p c (u n) -> p c u n", n=ECN)[:, :, u0:u0 + 4, :], writes=[f'ECs{d}'], skey=f'ecs{d}')
                res[d] = (u0, U4, ECs[:, 0, :, 0:nch], ECs[:, 1, :, 0:nch], RP3[:, SP, U4], w_BT, w_CL, w_CC, Dt)
            if True:
                items = [(d_, s_, tq_) for d_ in range(2) for s_ in range(nseq) for tq_ in range(tps)]
                ctx = {}

                bank_ctx = {}

                def stage1_mm(it_, pls):
                    d, s, tq = it_
                    u0, U4, ecr, eci, r8b, w_BT, w_CL, w_CC, Dt = res[d]
                    t0 = (s * tps + tq) * TL
                    if it_ not in ctx:
                        ctx[it_] = tcnt[0] % 3
                        tcnt[0] += 1
                    for pl in pls:
                        pa, pb = nps(2, 6), nps(2, 6)
                        bank_ctx[(it_, pl)] = (pa, pb)
                        rows = slice(32 * pl, 32 * pl + 32)
                        for c, pp in ((0, pa), (1, pb)):
                            tk.mm([(PS[pp][:, jj:TL:SP], w_BT[rows, jj, c, :], usb[d][rows, t0 + jj:t0 + TL:SP],
                                    dict(start=True, stop=True, tile_position=(32 * pl, 0))) for jj in range(SP)],
                                  reads=[f'wS{d}', f'usb{d}'], writes=[f'ps{pp}'])

                def stage1_scan(it_, pls, last):
                    d, s, tq = it_
                    u0, U4, ecr, eci, r8b, w_BT, w_CL, w_CC, Dt = res[d]
                    gb = ctx[it_]
                    gRb, gIb = gRb2[gb], gIb2[gb]
                    for pl in pls:
                        pa, pb = bank_ctx[(it_, pl)]
                        tk.op('dve', lambda e: e.tensor_tensor_scan(out=gR[:, pl, :], data0=Dt[:, pl, :], data1=PS[pa][:, 0:TL], initial=0.0,
                                                                     op0=ALU.mult, op1=ALU.add), reads=[f'Dt{d}', f'ps{pa}'], writes=[f'gR{pl}', 'hnl0'])
                        tk.op('dve', lambda e: e.tensor_tensor_scan(out=gI[:, pl, :], data0=Dt[:, pl, :], data1=PS[pb][:, 0:TL], initial=0.0,
                                                                     op0=ALU.mult, op1=ALU.add), reads=[f'Dt{d}', f'ps{pb}'], writes=[f'gI{pl}', 'hnl1'])
                        tk.op('act', lambda e: e.copy(out=gRb[:, pl, :], in_=gR[:, pl, :]), reads=[f'gR{pl}'], writes=[f'gRb{gb}_{pl}'])
                        tk.op('act', lambda e: e.copy(out=gIb[:, pl, :], in_=gI[:, pl, :]), reads=[f'gI{pl}'], writes=[f'gIb{gb}_{pl}'])

                    if last:
                        tk.op('act', lambda e: e.copy(out=Lr2[gb], in_=gR[:, :, SP - 1::SP]), reads=[f'gR{pl}' for pl in range(4)], writes=[f'L{gb}'])
                        tk.op('act', lambda e: e.copy(out=Li2[gb], in_=gI[:, :, SP - 1::SP]), reads=[f'gI{pl}' for pl in range(4)], writes=[f'L{gb}'])

                def stage1(it_):
                    stage1_mm(it_, (0, 1)); stage1_scan(it_, (0, 1), False)
                    stage1_mm(it_, (2, 3)); stage1_scan(it_, (2, 3), True)

                def stage2(it_):
                    d, s, tq = it_
                    u0, U4, ecr, eci, r8b, w_BT, w_CL, w_CC, Dt = res[d]
                    ti = s * tps + tq
                    t0 = ti * TL
                    hb = tq % 2
                    Hr_, Hi_ = HBr[hb], HBi[hb]
                    gb = ctx[it_]
                    gRb, gIb, HBrb, HBib = gRb2[gb], gIb2[gb], HBrb2[gb % 2], HBib2[gb % 2]
                    if tq == 0:
                        if j == 1:
                            dv(lambda e: e.tensor_copy(out=Hr_[:, :, 0], in_=hm1r[:, U4]), ['s5c'], CK)
                            dv(lambda e: e.tensor_copy(out=Hi_[:, :, 0], in_=hm1i[:, U4]), ['s5c'], CK)
                            dv(lambda e: e.tensor_copy(out=Km1r[hb], in_=ini0r[:, U4]), ['s5c'], CK)
                            dv(lambda e: e.tensor_copy(out=Km1i[hb], in_=ini0i[:, U4]), ['s5c'], CK)
                        else:
                            for z_ in (Hr_[:, :, 0], Hi_[:, :, 0], Km1r[hb], Km1i[hb]):
                                dv(lambda e, z_=z_: e.memset(z_, 0.0), [], CK)

                    py = nps(6, 8)
                    GK = [f'L{gb}']
                    Lr, Li = Lr2[gb], Li2[gb]
                    pv(lambda e: e.tensor_tensor(out=X1, in0=ecr, in1=Lr, op=ALU.mult), GK + ['s5c', f'ECs{d}'], CK)
                    pv(lambda e: e.tensor_tensor(out=X2, in0=eci, in1=Li, op=ALU.mult), GK + ['s5c', f'ECs{d}'], CK)
                    pv(lambda e: e.tensor_tensor(out=Xr, in0=X1, in1=X2, op=ALU.subtract), CK, CK)
                    pv(lambda e: e.tensor_tensor(out=X1, in0=ecr, in1=Li, op=ALU.mult), GK + ['s5c', f'ECs{d}'], CK)
                    pv(lambda e: e.tensor_tensor(out=X2, in0=eci, in1=Lr, op=ALU.mult), GK + ['s5c', f'ECs{d}'], CK)
                    pv(lambda e: e.tensor_tensor(out=Xi, in0=X1, in1=X2, op=ALU.add), CK, CK)
                    for pl in range(4):
                        r8 = r8b[:, pl:pl + 1].broadcast_to([128, nch])
                        dv(lambda e: e.tensor_tensor_scan(out=Kr[:, pl, :], data0=r8, data1=Xr[:, pl, :], initial=Km1r[hb][:, pl:pl + 1],
                                                          op0=ALU.mult, op1=ALU.add), CK + ['s5c'], CK)
                        dv(lambda e: e.tensor_tensor_scan(out=Ki[:, pl, :], data0=r8, data1=Xi[:, pl, :], initial=Km1i[hb][:, pl:pl + 1],
                                                          op0=ALU.mult, op1=ALU.add), CK + ['s5c'], CK)
                    pv(lambda e: e.tensor_tensor(out=X1, in0=ecr, in1=Kr, op=ALU.mult), CK, CK)
                    pv(lambda e: e.tensor_tensor(out=X2, in0=eci, in1=Ki, op=ALU.mult), CK, CK)
                    pv(lambda e: e.tensor_tensor(out=Hr_[:, :, 1:nch + 1], in0=X1, in1=X2, op=ALU.add), CK, CK + [f'HB{hb}'])
                    pv(lambda e: e.tensor_tensor(out=X1, in0=ecr, in1=Ki, op=ALU.mult), CK, CK)
                    pv(lambda e: e.tensor_tensor(out=X2, in0=eci, in1=Kr, op=ALU.mult), CK, CK)
                    pv(lambda e: e.tensor_tensor(out=Hi_[:, :, 1:nch + 1], in0=X1, in1=X2, op=ALU.subtract), CK, CK + [f'HB{hb}'])
                    tk.op('act', lambda e: e.copy(out=HBrb, in_=Hr_[:, :, 0:nch]), reads=CK + [f'HB{hb}'], writes=[f'HBb{gb % 2}'])
                    tk.op('act', lambda e: e.copy(out=HBib, in_=Hi_[:, :, 0:nch]), reads=CK + [f'HB{hb}'], writes=[f'HBb{gb % 2}'])
                    hlr, hli = Hr_[:, :, nch], Hi_[:, :, nch]
                    if tq < tps - 1:
                        nb_ = (tq + 1) % 2
                        o_r, o_i = PWr3[:, SP, U4], PWi3[:, SP, U4]
                        pv(lambda e: e.tensor_copy(out=HBr[nb_][:, :, 0], in_=hlr), CK, CK + [f'HB{nb_}'])
                        pv(lambda e: e.tensor_copy(out=HBi[nb_][:, :, 0], in_=hli), CK, CK + [f'HB{nb_}'])
                        pv(lambda e: e.tensor_tensor(out=tq4[0], in0=o_r, in1=hlr, op=ALU.mult), CK + ['s5c'], CK)
                        pv(lambda e: e.tensor_tensor(out=tq4[1], in0=o_i, in1=hli, op=ALU.mult), CK + ['s5c'], CK)
                        pv(lambda e: e.tensor_tensor(out=Km1r[nb_], in0=tq4[0], in1=tq4[1], op=ALU.subtract), CK, CK)
                        pv(lambda e: e.tensor_tensor(out=tq4[0], in0=o_r, in1=hli, op=ALU.mult), CK + ['s5c'], CK)
                        pv(lambda e: e.tensor_tensor(out=tq4[1], in0=o_i, in1=hlr, op=ALU.mult), CK + ['s5c'], CK)
                        pv(lambda e: e.tensor_tensor(out=Km1i[nb_], in0=tq4[0], in1=tq4[1], op=ALU.add), CK, CK)
                    elif j == 0:
                        p7r, p7i = PWr3[:, SP - 1, U4], PWi3[:, SP - 1, U4]
                        fo_r = fins[:, ((0 * 4 + s) * 2 + d) * 16 + 4 * q:((0 * 4 + s) * 2 + d) * 16 + 4 * q + 4]
                        fo_i = fins[:, ((1 * 4 + s) * 2 + d) * 16 + 4 * q:((1 * 4 + s) * 2 + d) * 16 + 4 * q + 4]
                        pv(lambda e: e.tensor_tensor(out=tq4[0], in0=p7r, in1=hlr, op=ALU.mult), CK + ['s5c'], CK)
                        pv(lambda e: e.tensor_tensor(out=tq4[1], in0=p7i, in1=hli, op=ALU.mult), CK + ['s5c'], CK)
                        pv(lambda e: e.tensor_tensor(out=fo_r, in0=tq4[0], in1=tq4[1], op=ALU.subtract), CK, ['fins'])
                        pv(lambda e: e.tensor_tensor(out=tq4[0], in0=p7r, in1=hli, op=ALU.mult), CK + ['s5c'], CK)
                        pv(lambda e: e.tensor_tensor(out=tq4[1], in0=p7i, in1=hlr, op=ALU.mult), CK + ['s5c'], CK)
                        pv(lambda e: e.tensor_tensor(out=fo_i, in0=tq4[0], in1=tq4[1], op=ALU.add), CK, ['fins'])
                    def cmm():
                      for pl in range(4):
                        yq = PS[py][32 * pl:32 * pl + 32, :]
                        mms = []
                        for jj in range(SP):
                            tp_ = (0, 32 * pl)
                            mms.append((yq[:, jj:TL:SP], w_CL[:, jj, 0, pl, :], gRb[:, pl, jj:TL:SP], dict(start=True, stop=False, tile_position=tp_)))
                            mms.append((yq[:, jj:TL:SP], w_CL[:, jj, 1, pl, :], gIb[:, pl, jj:TL:SP], dict(start=False, stop=False, tile_position=tp_)))
                            mms.append((yq[:, jj:TL:SP], w_CC[:, jj, 0, pl, :], HBrb[:, pl, :], dict(start=False, stop=False, tile_position=tp_)))
                            mms.append((yq[:, jj:TL:SP], w_CC[:, jj, 1, pl, :], HBib[:, pl, :], dict(start=False, stop=True, tile_position=tp_)))
                        tk.mm(mms, reads=[f'wS{d}', f'HBb{gb % 2}', f'gRb{gb}_{pl}', f'gIb{gb}_{pl}'], writes=[f'ps{py}'])
                    def evac():
                        if d == 0:
                            tk.op('act', lambda e: e.copy(out=hs_f[:, t0:t0 + TL], in_=PS[py][:, 0:TL]), reads=[f'ps{py}'], writes=['hs_f'])
                        else:
                            lo = s * T + (T - tq * TL - TL)
                            hv = hs_f[:, lo:lo + TL][:, ::-1]
                            tk.op('dve', lambda e: e.tensor_tensor(out=hv, in0=hv, in1=PS[py][:, 0:TL], op=ALU.add),
                                  reads=[f'ps{py}', 'hs_f'], writes=['hs_f'])
                    return cmm, evac

                for it_ in items[0:2]:
                    stage1(it_)
                pend = None
                for ii_, it_ in enumerate(items):
                    nxt = items[ii_ + 2] if ii_ + 2 < len(items) else None
                    if nxt is not None:
                        stage1_mm(nxt, (0, 1))
                    cmm_, ev_ = stage2(it_)
                    if nxt is not None:
                        stage1_scan(nxt, (0, 1), False)
                        stage1_mm(nxt, (2, 3))
                        stage1_scan(nxt, (2, 3), True)
                    cmm_()
                    if pend is not None:
                        pend()
                    pend = ev_
                pend()
            if q == 0:
                dbg(f'hsf{j}', hs_f, ['hs_f'])
            for ti in range(ntile):
                t0 = ti * TL
                b_ = ti % 2
                tk.op('dve', lambda e: e.scalar_tensor_tensor(out=sg[b_], in0=us_f[:, t0:t0 + TL], scalar=col(vecs, 184 + q), in1=hs_f[:, t0:t0 + TL],
                                                              op0=ALU.mult, op1=ALU.add), reads=['us_f', 'hs_f'], writes=[f'sg{b_}'])
                tk.op('act', lambda e: e.activation(out=vso[b_], in_=sg[b_], func=AF.Gelu_apprx_tanh),
                      reads=[f'sg{b_}'], writes=[f'vso{b_}'])
                tk.dma('pool', vsD[j][q * 128:(q + 1) * 128, t0:t0 + TL], vso[b_], reads=[f'vso{b_}'], writes=['vsD'], skey=f'vso{b_}')
        tk.barrier()
        A.off = P3MARK
        wgl = A.bf16(4 * 512).rearrange("p (k n) -> p k n", k=4)
        ysn = A.bf16(4 * NT).rearrange("p (c t) -> p c t", c=4)
        vst = [A.bf16(4 * TL).rearrange("p (c t) -> p c t", c=4) for _ in range(2)]
        vsDk = vsD[j].rearrange("(k p) t -> p k t", p=128)
        tk.dma('pool', wgl, w_glu.rearrange("(k p) n -> p k n", p=128), writes=['wgl'], skey='wgl')
        for ti in range(ntile):
            t0 = ti * TL
            vb_ = ti % 2
            vt = vst[vb_]
            tk.dma('sp', vt, vsDk[:, :, t0:t0 + TL], writes=[f'vst{vb_}'], skey=f'vst{vb_}')
            for oc in range(4):
                pi = nps(0, 6)
                b_ = (ti * 4 + oc) % 2
                tk.mm([(PS[pi][:, 0:TL], wgl[:, k, oc * 128:(oc + 1) * 128], vt[:, k, :], dict(start=(k == 0), stop=(k == 3))) for k in range(4)],
                      reads=['wgl', f'vst{vb_}'], writes=[f'ps{pi}'])
                tk.op('act', lambda e: e.activation(out=sg[b_], in_=PS[pi][:, 0:TL], func=AF.Sigmoid, bias=col(vecs, 188 + oc), scale=1.0),
                      reads=[f'ps{pi}'], writes=[f'sg{b_}'])
                if j == 1:
                    c0 = ti * 8
                    o_ = ysn[:, oc, :].rearrange("p (r c) -> p c r", c=64)[:, c0:c0 + 8, :]
                    a_ = vt[:, oc, :].rearrange("p (c r) -> p c r", r=64)
                    g_ = sg[b_].rearrange("p (c r) -> p c r", r=64)
                else:
                    o_, a_, g_ = ysn[:, oc, t0:t0 + TL], vt[:, oc, :], sg[b_]
                tk.op('dve', lambda e: e.tensor_tensor(out=o_, in0=a_, in1=g_, op=ALU.mult), reads=[f'vst{vb_}', f'sg{b_}'], writes=['ysn'])
        for oc in range(4):
            tk.dma('pool', ysD[j][oc * 128:(oc + 1) * 128, :], ysn[:, oc, :], reads=['ysn'], writes=['ysD'], skey='ysn')
        tk.barrier()

        A.off = PERSIST
        TL = 512
        ntile = NT // TL
        NSL = 4
        ring = [A.bf16(8192) for _ in range(NSL)]
        rc = [0]

        def wload(src_ap, shape_k, ncol):
            sl = rc[0] % NSL
            rc[0] += 1
            v = ring[sl][:, 0:shape_k * ncol].rearrange("p (k n) -> p k n", k=shape_k)
            tk.dma('sp', v, src_ap, writes=[f'ring{sl}'], skey=f'ring{sl}')
            return v, f'ring{sl}'
        xt = A.f32(8 * TL).rearrange("p (k t) -> p k t", k=8)
        R1 = A.bf16(22 * TL)
        hnt = R1[:, 0:8 * TL].rearrange("p (k t) -> p k t", k=8)
        yat = R1[:, 8 * TL:16 * TL].rearrange("p (k t) -> p k t", k=8)
        yst = R1[:, 16 * TL:20 * TL].rearrange("p (k t) -> p k t", k=4)
        hmid = R1.rearrange("p (k t) -> p k t", k=22)
        mt = A.bf16(8 * TL).rearrange("p (k t) -> p k t", k=8)
        mo = A.f32(8 * TL).rearrange("p (k t) -> p k t", k=8)
        sqs = A.bf16(8 * TL).rearrange("p (k t) -> p k t", k=8)
        rstd = A.f32(TL)
        tm4 = [A.f32(TL) for _ in range(4)]
        R1K = ['hnt', 'yat', 'yst', 'hmid']
        for ti in range(ntile if 4 in phases else 0):
            t0 = ti * TL
            wg0, kg0 = wload(kp(WG, 0, 1024), 8, 1024)
            wg1, kg1 = wload(kp(WG, 1024, 2048), 8, 1024)
            wpl, kpl = wload(kp(WPL, 0, 1024), 8, 1024)
            wps, kps = wload(WPS.rearrange("(k p) n -> p k n", p=128), 4, 1024)
            tk.dma('sp', hnt, hnDk[:, :, t0:t0 + TL], writes=['hnt', 'hmid'], skey='hnt')
            tk.dma('sp', yat, yaDk[:, :, t0:t0 + TL], writes=['yat', 'hmid'], skey='yat')
            tk.dma('sp', yst, ysDk[:, :, t0:t0 + TL], writes=['yst', 'hmid'], skey='yst')
            tk.dma('sp', xt, xk[:, :, t0:t0 + TL], writes=['xt', 'xtA', 'xtB'], skey='xt')
            for oc in range(8):
                cs = slice(oc * 128, (oc + 1) * 128)
                p1, p2, p3, p4 = nps(), nps(), nps(), nps()
                tk.mm([(PS[p1][:, 0:TL], wg0[:, k, cs], hnt[:, k, :], dict(start=(k == 0), stop=(k == 7))) for k in range(8)],
                      reads=[kg0, 'hnt'], writes=[f'ps{p1}'])
                tk.mm([(PS[p2][:, 0:TL], wpl[:, k, cs], yat[:, k, :], dict(start=(k == 0), stop=(k == 7))) for k in range(8)],
                      reads=[kpl, 'yat'], writes=[f'ps{p2}'])
                tk.mm([(PS[p3][:, 0:TL], wg1[:, k, cs], hnt[:, k, :], dict(start=(k == 0), stop=(k == 7))) for k in range(8)],
                      reads=[kg1, 'hnt'], writes=[f'ps{p3}'])
                tk.mm([(PS[p4][:, 0:TL], wps[:, k, cs], yst[:, k, :], dict(start=(k == 0), stop=(k == 3))) for k in range(4)],
                      reads=[kps, 'yst'], writes=[f'ps{p4}'])
                tk.op('act', lambda e: e.activation(out=tm4[0], in_=PS[p1][:, 0:TL], func=AF.Sigmoid, bias=col(vecs, 168 + oc), scale=1.0),
                      reads=[f'ps{p1}'], writes=['tm0'])
                tk.op('act', lambda e: e.activation(out=tm4[1], in_=PS[p3][:, 0:TL], func=AF.Sigmoid, bias=col(vecs, 176 + oc), scale=1.0),
                      reads=[f'ps{p3}'], writes=['tm1'])
                tk.op('dve', lambda e: e.tensor_tensor(out=tm4[2], in0=tm4[0], in1=PS[p2][:, 0:TL], op=ALU.mult), reads=['tm0', f'ps{p2}'], writes=['tm2'])
                tk.op('dve', lambda e: e.tensor_tensor(out=tm4[3], in0=tm4[1], in1=PS[p4][:, 0:TL], op=ALU.mult), reads=['tm1', f'ps{p4}'], writes=['tm3'])
                tk.op('dve', lambda e: e.tensor_tensor(out=mt[:, oc, :], in0=tm4[2], in1=tm4[3], op=ALU.add), reads=['tm2', 'tm3'], writes=['mt'])
            wo, ko = wload(kp(WO, 0, 1024), 8, 1024)
            for oc in range(8):
                cs = slice(oc * 128, (oc + 1) * 128)
                p1 = nps()
                tk.mm([(PS[p1][:, 0:TL], wo[:, k, cs], mt[:, k, :], dict(start=(k == 0), stop=(k == 7))) for k in range(8)],
                      reads=[ko, 'mt'], writes=[f'ps{p1}'])
                tk.op('act', lambda e: e.copy(out=mo[:, oc, :], in_=PS[p1][:, 0:TL]), reads=[f'ps{p1}'], writes=['mo'] + [f'mo{k_}' for k_ in range(8)])
                tk.op('act', lambda e: e.activation(out=sqs[:, oc, :], in_=PS[p1][:, 0:TL], func=AF.Square), reads=[f'ps{p1}'], writes=['sqs'])
            rms_rstd(sqs, rstd, 'sqs')

            def residual(cG):
                for k in range(8):
                    tk.op('dve', lambda e: e.scalar_tensor_tensor(out=mo[:, k, :], in0=mo[:, k, :], scalar=cj(cG, k, j), in1=rstd, op0=ALU.mult, op1=ALU.mult),
                          reads=['mo', 'rstd'], writes=[f'mo{k}'])
                tk.op('dve', lambda e: e.tensor_tensor(out=xt[:, 0:5, :], in0=xt[:, 0:5, :], in1=mo[:, 0:5, :], op=ALU.add),
                      reads=[f'mo{k_}' for k_ in range(5)] + ['xt', 'xtA'], writes=['xtA'])
                tk.op('pool', lambda e: e.tensor_tensor(out=xt[:, 5:8, :], in0=xt[:, 5:8, :], in1=mo[:, 5:8, :], op=ALU.add),
                      reads=[f'mo{k_}' for k_ in range(5, 8)] + ['xt', 'xtB'], writes=['xtB'])
            residual(cG1)
            tk.op('act', lambda e: e.activation(out=sqs, in_=xt, func=AF.Square), reads=['xtA', 'xtB'], writes=['sqs'])
            rms_rstd(sqs, rstd, 'sqs')
            for k in range(8):
                tk.op('dve', lambda e: e.scalar_tensor_tensor(out=mo[:, k, :], in0=xt[:, k, :], scalar=cj(cA2, k, j), in1=rstd, op0=ALU.mult, op1=ALU.mult),
                      reads=['xtA', 'xtB', 'rstd'] + [f'mo{k_}' for k_ in range(8)], writes=[f'mo{k}'])
                tk.op('act', lambda e: e.activation(out=mt[:, k, :], in_=mo[:, k, :], func=AF.Identity, bias=cj(cB2, k, j), scale=1.0),
                      reads=[f'mo{k}'], writes=['mt', f'mt{k}'])
            for blk in range(11):
                w1, k1 = wload(kp(WFI, blk * 256, blk * 256 + 256), 8, 256)
                w3, k3 = wload(kp(WFI, 2816 + blk * 256, 2816 + blk * 256 + 256), 8, 256)
                for sub in range(2):
                    cs = slice(sub * 128, (sub + 1) * 128)
                    p1, p3 = nps(), nps()
                    tk.mm([(PS[p1][:, 0:TL], w1[:, k, cs], mt[:, k, :], dict(start=(k == 0), stop=(k == 7))) for k in range(8)],
                          reads=[k1, 'mt'] + [f'mt{k_}' for k_ in range(8)], writes=[f'ps{p1}'])
                    tk.mm([(PS[p3][:, 0:TL], w3[:, k, cs], mt[:, k, :], dict(start=(k == 0), stop=(k == 7))) for k in range(8)],
                          reads=[k3, 'mt'] + [f'mt{k_}' for k_ in range(8)], writes=[f'ps{p3}'])
                    tm = tm4[2 + sub]
                    tk.op('act', lambda e: e.activation(out=tm, in_=PS[p1][:, 0:TL], func=AF.Silu), reads=[f'ps{p1}'], writes=[f'tm{2 + sub}'])
                    tk.op('dve', lambda e: e.tensor_tensor(out=hmid[:, blk * 2 + sub, :], in0=tm, in1=PS[p3][:, 0:TL], op=ALU.mult),
                          reads=[f'tm{2 + sub}', f'ps{p3}'], writes=R1K)
            for oc in range(8):
                wf, kf = wload(WFO[:, oc * 128:(oc + 1) * 128].rearrange("(k p) n -> p k n", p=128), 22, 128)
                p1 = nps()
                tk.mm([(PS[p1][:, 0:TL], wf[:, k, :], hmid[:, k, :], dict(start=(k == 0), stop=(k == 21))) for k in range(22)],
                      reads=[kf, 'hmid'], writes=[f'ps{p1}'])
                tk.op('act', lambda e: e.copy(out=mo[:, oc, :], in_=PS[p1][:, 0:TL]), reads=[f'ps{p1}'], writes=['mo'] + [f'mo{k_}' for k_ in range(8)])
                tk.op('act', lambda e: e.activation(out=sqs[:, oc, :], in_=PS[p1][:, 0:TL], func=AF.Square), reads=[f'ps{p1}'], writes=['sqs'])
            rms_rstd(sqs, rstd, 'sqs')
            residual(cG2)
            tk.dma('pool', yk[:, :, t0:t0 + TL], xt, reads=['xt', 'xtA', 'xtB'], writes=['yT'], skey='yst_out')
        tk.barrier()

    tk.dma('pool', finl_d, finl, reads=['finl'], writes=['finl_d'], skey='fl')
    tk.dma('pool', fins_d, fins, reads=['fins'], writes=['fins_d'], skey='fs')
    tk.barrier()
    return nc


_NC = None


def _host_inputs(inp, c):
    f = np.float32
    g = lambda k: np.asarray(inp[k], dtype=f)

    def v8(vec):
        return np.ascontiguousarray(vec.reshape(-1, 128).T)
    m = {}
    m["xT_p"] = np.ascontiguousarray(g("x_prompt")[4 * c:4 * c + 4].reshape(1024, 1024).T)
    m["xT_s"] = np.ascontiguousarray(g("x_sample")[c].T)
    cv = np.stack([g("c_ctx"), g("c")[c]], axis=1)
    m["cv"] = np.ascontiguousarray(cv.reshape(8, 128, 2).transpose(1, 0, 2).reshape(128, 16))
    cols = [v8(g("g_pre_mix")[0]), v8(g("g_post_mix")[0]), v8(g("g_pre_ffn")[0]), v8(g("g_post_ffn")[0]),
            v8(g("b_mod")[0]), v8(g("conv_w")[0].reshape(-1)), v8(g("conv_b")[0]), v8(g("lru_b_r")[0].reshape(-1)),
            v8(g("lru_b_i")[0].reshape(-1)), v8(g("lru_lambda")[0].reshape(-1)), v8(g("b_gate")[0]),
            v8(g("s5_d")[0]), v8(g("s5_b_glu")[0])]
    m["vecs"] = np.ascontiguousarray(np.concatenate(cols, axis=1))
    assert m["vecs"].shape == (128, NV)
    lw = np.zeros((128, 2, 2, 8, 128), f)
    for gi, key in enumerate(("lru_w_r", "lru_w_i")):
        w = g(key)[0]
        for hh in range(2):
            lw[64 * hh:64 * hh + 64, :, gi, :, 64 * hh:64 * hh + 64] = w[:, hh::2].transpose(2, 0, 1, 3)
    m["lruw"] = lw.reshape(128, 32 * 128)
    m["h0l"] = np.ascontiguousarray(g("state_lru")[c, 0].reshape(2, 8, 128).transpose(2, 0, 1).reshape(128, 16))

    def unit(a):
        sh = a.shape[3:]
        a = a.reshape((2, 16, 2, 64) + sh)
        a = np.moveaxis(a, (2, 3), (0, 1))
        return np.ascontiguousarray(a.reshape((128, 32) + sh))
    ldt = np.broadcast_to(g("s5_log_dt")[0][:, :, None], (2, 32, 64))
    s5p = np.stack([unit(g("s5_a_re")[0]), unit(g("s5_a_im")[0]), unit(ldt),
                    unit(g("state_s5_re")[c, 0]), unit(g("state_s5_im")[c, 0])], axis=1)
    m["s5p"] = np.ascontiguousarray(s5p.reshape(128, 160))
    s5bc = np.stack([unit(g("s5_b_re")[0]), unit(g("s5_b_im")[0]),
                     unit(g("s5_c_re")[0].transpose(0, 1, 3, 2)), unit(g("s5_c_im")[0].transpose(0, 1, 3, 2))], axis=1)
    m["s5bc"] = np.ascontiguousarray(s5bc.reshape(128, 4 * 512))
    m["ident"] = np.eye(128, dtype=f)
    m["w_mod"] = g("w_mod")[0]; m["w_in"] = g("w_in")[0]; m["w_gate"] = g("w_gate")[0]; m["w_out"] = g("w_out")[0]
    m["w_pl"] = g("w_proj_lru")[0]; m["w_ps"] = g("w_proj_s5")[0]; m["w_fi"] = g("w_ff_in")[0]; m["w_fo"] = g("w_ff_out")[0]
    m["w_glu"] = g("s5_w_glu")[0]
    return m


def kernel(**inputs):
    global _NC
    if _NC is None:
        _NC = build_nc()
    nc = _NC
    in_maps = [_host_inputs(inputs, c) for c in range(8)]
    res = run_bass_kernel_spmd(nc, in_maps, core_ids=list(range(8)))
    y_p = np.zeros((32, 256, 1024), np.float32)
    y_s = np.zeros((8, 4096, 1024), np.float32)
    nl = np.zeros((32, 1, 2, 1024), np.float32)
    nre = np.zeros((32, 1, 2, 32, 64), np.float32)
    nim = np.zeros((32, 1, 2, 32, 64), np.float32)
    for c in range(8):
        r = res.results[c]
        y_p[4 * c:4 * c + 4] = r["yT_p"].T.reshape(4, 256, 1024)
        y_s[c] = r["yT_s"].T
        fl = r["fin_lru"].reshape(128, 4, 2, 8)
        nl[4 * c:4 * c + 4, 0] = fl.transpose(1, 2, 3, 0).reshape(4, 2, 1024)
        fs = r["fin_s5"].reshape(2, 64, 2, 4, 2, 16)
        fs = fs.transpose(2, 3, 4, 5, 0, 1).reshape(2, 4, 2, 32, 64)
        nre[4 * c:4 * c + 4, 0] = fs[0]
        nim[4 * c:4 * c + 4, 0] = fs[1]
    return (y_p, y_s, nl, nre, nim)
```

```python
import numpy as np
import concourse.bass as bass
import concourse.mybir as mybir
from concourse.bass_utils import run_bass_kernel_spmd

F32, BF16 = mybir.dt.float32, mybir.dt.bfloat16
AF = mybir.ActivationFunctionType
ALU = mybir.AluOpType
PI = float(np.pi)
NV = 192
SP = 8
LP = 3
ECN = 512 // SP
ARENA_WORDS = 47000


class TK:
    def __init__(s, nc):
        s.nc = nc
        s.eng = {'pe': nc.tensor, 'act': nc.scalar, 'dve': nc.vector, 'pool': nc.gpsimd, 'sp': nc.sync}
        s.sem = {e: nc.alloc_semaphore('sem_' + e) for e in s.eng}
        s.cnt = {e: 0 for e in s.eng}
        s.seen = {e: {} for e in s.eng}
        s.lastw = {}
        s.readers = {}
        s.dsem = {}

    def _wait(s, e, ev):
        name, obj, val = ev
        if name == 'sem_' + e and e == 'pe':
            return
        if s.seen[e].get(name, 0) >= val:
            return
        s.eng[e].wait_ge(obj, val)
        s.seen[e][name] = val

    def deps(s, e, reads, writes):
        for k in reads:
            if k in s.lastw:
                s._wait(e, s.lastw[k])
        for k in writes:
            if k in s.lastw:
                s._wait(e, s.lastw[k])
            for ev in s.readers.get(k, {}).values():
                s._wait(e, ev)

    def commit(s, ev, reads, writes):
        for k in reads:
            s.readers.setdefault(k, {})[ev[0]] = ev
        for k in writes:
            s.lastw[k] = ev
            s.readers[k] = {}

    def op(s, e, fn, reads=(), writes=(), strict=False):
        s.deps(e, reads, writes)
        own = 'sem_' + e
        if not strict and e != 'pe':
            for k in list(reads) + list(writes):
                ev = s.lastw.get(k)
                if ev is not None and ev[0] == own and ev[2] >= s.cnt[e] - 1:
                    strict = True
            for k in writes:
                ev = s.readers.get(k, {}).get(own)
                if ev is not None and ev[2] >= s.cnt[e] - 1:
                    strict = True
        if strict and s.cnt[e] > 0:
            s.eng[e].wait_ge(s.sem[e], s.cnt[e])
        ins = fn(s.eng[e])
        s.cnt[e] += 1
        ins.then_inc(s.sem[e], 1)
        s.commit(('sem_' + e, s.sem[e], s.cnt[e]), reads, writes)

    def mm(s, mms, reads=(), writes=()):
        s.deps('pe', reads, writes)
        ins = None
        for (o, l, r, kw) in mms:
            ins = s.nc.tensor.matmul(o, lhsT=l, rhs=r, **kw)
        s.cnt['pe'] += 1
        ins.then_inc(s.sem['pe'], 1)
        s.commit(('sem_pe', s.sem['pe'], s.cnt['pe']), reads, writes)

    def dma(s, q, out, in_, reads=(), writes=(), skey=None):
        s.deps(q, reads, writes)
        if skey not in s.dsem:
            s.dsem[skey] = [s.nc.alloc_semaphore('d_' + skey), 0]
        d = s.dsem[skey]
        d[1] += 16
        s.eng[q].dma_start(out=out, in_=in_).then_inc(d[0], 16)
        s.commit(('d_' + skey, d[0], d[1]), reads, writes)

    def barrier(s):
        evs = [('sem_' + e, s.sem[e], s.cnt[e]) for e in s.eng if s.cnt[e] > 0]
        evs += [('d_' + k, d[0], d[1]) for k, d in s.dsem.items()]
        for e in s.eng:
            for ev in evs:
                s._wait(e, ev)
        s.lastw.clear()
        s.readers.clear()


class Arena:
    def __init__(s, t, words):
        s.t, s.words, s.off = t, words, 0

    def f32(s, n):
        n = (n + 7) // 8 * 8
        ap = s.t[:, s.off:s.off + n]
        s.off += n
        assert s.off <= s.words, ("arena overflow", s.off, s.words)
        return ap

    def bf16(s, n):
        w = ((n + 1) // 2 + 7) // 8 * 8
        ap = s.t[:, s.off:s.off + w].bitcast(BF16)
        s.off += w
        assert s.off <= s.words, ("arena overflow", s.off, s.words)
        return ap[:, 0:n]


def build_nc(debug=False, jobs_sel=(0, 1), phases=(1, 2, 3, 4)):
    nc = bass.Bass("TRN2", target_bir_lowering=False)
    D = 1024
    dbg_n = [0]

    def dbg(name, ap, keys):
        if not debug:
            return
        shp = [int(x) for x in ap.shape]
        o = nc.dram_tensor("dbg_" + name, shp, ap.dtype, kind="ExternalOutput").ap()
        tk.dma('pool', o, ap, reads=keys, writes=['dbgout'], skey='dbg%d' % (dbg_n[0] % 4))
        dbg_n[0] += 1

    def din(name, shape):
        return nc.dram_tensor(name, list(shape), F32, kind="ExternalInput").ap()

    def dout(name, shape):
        return nc.dram_tensor(name, list(shape), F32, kind="ExternalOutput").ap()

    def dscr(name, shape, dt=BF16):
        return nc.dram_tensor(name, list(shape), dt, kind="Internal").ap()

    xT = [din("xT_p", [D, 1024]), din("xT_s", [D, 4096])]
    yT = [dout("yT_p", [D, 1024]), dout("yT_s", [D, 4096])]
    cv_d = din("cv", [128, 16])
    vecs_d = din("vecs", [128, NV])
    lruw_d = din("lruw", [128, 32 * 128])
    h0l_d = din("h0l", [128, 16])
    s5p_d = din("s5p", [128, 5 * 32])
    s5bc_d = din("s5bc", [128, 4 * 32 * 16])
    ident_d = din("ident", [128, 128])
    w_mod = din("w_mod", [D, 6144])
    w_in = din("w_in", [D, 2560])
    w_gate = din("w_gate", [D, 2048])
    w_out = din("w_out", [D, D])
    w_pl = din("w_pl", [D, D])
    w_ps = din("w_ps", [512, D])
    w_fi = din("w_fi", [D, 5632])
    w_fo = din("w_fo", [2816, D])
    w_glu = din("w_glu", [512, 512])
    finl_d = dout("fin_lru", [128, 64])
    fins_d = dout("fin_s5", [128, 256])

    WG = dscr("WG", [D, 2048]); WO = dscr("WO", [D, D]); WPL = dscr("WPL", [D, D]); WPS = dscr("WPS", [512, D])
    WFI = dscr("WFI", [D, 5632]); WFO = dscr("WFO", [2816, D])
    hnD = [dscr("hnD0", [D, 1024]), dscr("hnD1", [D, 4096])]
    yaD = [dscr("yaD0", [D, 1024]), dscr("yaD1", [D, 4096])]
    vsD = [dscr("vsD0", [512, 1024]), dscr("vsD1", [512, 4096])]
    S5B = dscr("S5B", [8, 128, SP * 256])
    S5C = dscr("S5C", [2, 128, SP * 2048])
    ECD = dscr("ECD", [128, 2, 32 * ECN], F32)
    ysD = [dscr("ysD0", [512, 1024]), dscr("ysD1", [512, 4096])]

    tk = TK(nc)
    arena_t = nc.alloc_sbuf_tensor("arena", [128, ARENA_WORDS], F32)
    A = Arena(arena_t, ARENA_WORDS)
    PS2 = [nc.alloc_psum_tensor(f"pp{i}", [128, 1024], F32) for i in range(4)]
    PS = [PS2[i // 2][:, (i % 2) * 512:(i % 2) * 512 + 512] for i in range(8)]
    psc = [0]

    def nps(lo=0, hi=8):
        i = lo + psc[0] % (hi - lo)
        psc[0] += 1
        return i

    def kp(w, c0, c1):
        return w[:, c0:c1].rearrange("(k p) n -> p k n", p=128)

    vecs = A.f32(NV)
    cv = A.f32(16)
    h0l = A.f32(16)
    ident = A.f32(128)
    s5p = A.f32(160)
    lruw = A.bf16(32 * 128)
    ones_bf = A.bf16(128)
    epsc = A.f32(8)
    modc = A.f32(96)
    cA1 = A.f32(16); cB1 = A.f32(16); cG1 = A.f32(16); cA2 = A.f32(16); cB2 = A.f32(16); cG2 = A.f32(16)
    kco = A.f32(16)
    WKr = A.f32(9 * 32); WKi = A.f32(9 * 32)
    PWr = A.f32(16 * 32); PWi = A.f32(16 * 32)
    RP = A.f32(10 * 32)
    hm1r = A.f32(32); hm1i = A.f32(32)
    cth = A.f32(32); sth = A.f32(32); rho = A.f32(32)
    ini0r = A.f32(32); ini0i = A.f32(32)
    finl = A.f32(64)
    fins = A.f32(256)
    PERSIST = A.off

    def col(ap, i):
        return ap[:, i:i + 1]

    tk.dma('sp', vecs, vecs_d, writes=['vecs'], skey='ld0')
    tk.dma('sp', cv, cv_d, writes=['cv'], skey='ld1')
    tk.dma('sp', h0l, h0l_d, writes=['h0l'], skey='ld2')
    tk.dma('sp', ident, ident_d, writes=['ident'], skey='ld3')
    tk.dma('sp', s5p, s5p_d, writes=['s5p'], skey='ld4')
    tk.dma('pool', lruw, lruw_d, writes=['lruw'], skey='ld5')
    tk.op('dve', lambda e: e.memset(ones_bf, 1.0), writes=['ones'])
    tk.op('dve', lambda e: e.memset(epsc[:, 0:1], 1e-6), writes=['epsc'])
    tk.op('dve', lambda e: e.memset(epsc[:, 1:2], 1.0), writes=['epsc'])
    tk.op('dve', lambda e: e.memset(epsc[:, 2:3], 0.0), writes=['epsc'])
    tk.op('dve', lambda e: e.memset(finl, 0.0), writes=['finl'])
    tk.op('dve', lambda e: e.memset(fins, 0.0), writes=['fins'])
    EPS, ONE, ZERO = epsc[:, 0:1], epsc[:, 1:2], epsc[:, 2:3]

    m0 = A.off
    scb = A.bf16(16)
    tk.op('act', lambda e: e.activation(out=scb, in_=cv, func=AF.Silu), reads=['cv'], writes=['scb'])
    scb3 = scb.rearrange("p (k j) -> p k j", j=2)
    wms = [A.bf16(8 * 512) for _ in range(2)]
    wmf = [A.f32(8 * 512) for _ in range(2)]
    psm = PS[7]
    for blk in range(12):
        wslot = wms[blk % 2].rearrange("p (k n) -> p k n", k=8)
        wf32 = wmf[blk % 2].rearrange("p (k n) -> p k n", k=8)
        tk.dma('sp', wf32, kp(w_mod, blk * 512, blk * 512 + 512), writes=[f'wmf{blk % 2}'], skey=f'wmf{blk % 2}')
        if blk % 2 == 0:
            tk.op('act', lambda e: e.copy(out=wslot, in_=wf32), reads=[f'wmf{blk % 2}'], writes=[f'wm{blk % 2}'])
        else:
            tk.op('dve', lambda e: e.tensor_copy(out=wslot, in_=wf32), reads=[f'wmf{blk % 2}'], writes=[f'wm{blk % 2}'])
        for oc in range(4):
            c = blk * 4 + oc
            tk.mm([(psm[:, 2 * c:2 * c + 2], wslot[:, k, oc * 128:(oc + 1) * 128], scb3[:, k, :],
                    dict(start=(k == 0), stop=(k == 7))) for k in range(8)],
                  reads=[f'wm{blk % 2}', 'scb'], writes=['psm'])
    modc3 = modc.rearrange("p (c j) -> p c j", j=2)
    tk.op('dve', lambda e: e.tensor_tensor(out=modc3, in0=psm[:, 0:96].rearrange("p (c j) -> p c j", j=2),
                                           in1=vecs[:, 32:80].unsqueeze(2).broadcast_to([128, 48, 2]), op=ALU.add),
          reads=['psm', 'vecs'], writes=['modc'])

    def msec(s):
        return modc[:, 16 * s:16 * s + 16].rearrange("p (k j) -> p k j", j=2)

    def vb(c0):
        return vecs[:, c0:c0 + 8].unsqueeze(2).broadcast_to([128, 8, 2])

    def c3(ap):
        return ap.rearrange("p (k j) -> p k j", j=2)
    tk.op('dve', lambda e: e.scalar_tensor_tensor(out=c3(cA1), in0=msec(1), scalar=1.0, in1=vb(0), op0=ALU.add, op1=ALU.mult),
          reads=['modc'], writes=['cA1'])
    tk.op('dve', lambda e: e.tensor_copy(out=c3(cB1), in_=msec(0)), reads=['modc'], writes=['cB1'])
    tk.op('dve', lambda e: e.tensor_tensor(out=c3(cG1), in0=msec(2), in1=vb(8), op=ALU.mult), reads=['modc'], writes=['cG1'])
    tk.op('dve', lambda e: e.scalar_tensor_tensor(out=c3(cA2), in0=msec(4), scalar=1.0, in1=vb(16), op0=ALU.add, op1=ALU.mult),
          reads=['modc'], writes=['cA2'])
    tk.op('dve', lambda e: e.tensor_copy(out=c3(cB2), in_=msec(3)), reads=['modc'], writes=['cB2'])
    tk.op('dve', lambda e: e.tensor_tensor(out=c3(cG2), in0=msec(5), in1=vb(24), op=ALU.mult), reads=['modc'], writes=['cG2'])

    def cj(cst, k, j):
        return cst[:, 2 * k + j:2 * k + j + 1]

    tk.barrier()
    A.off = m0
    for (dst, src, rows, key) in ((WG, w_gate, D, 'cg'), (WPL, w_pl, D, 'cpl'), (WPS, w_ps, 512, 'cps'),
                                  (WO, w_out, D, 'co'), (WFI, w_fi, D, 'cfi'), (WFO, w_fo, 2816, 'cfo')):
        tk.dma('pool', dst.rearrange("(p a) n -> p (a n)", p=128), src.rearrange("(p a) n -> p (a n)", p=128),
               writes=['W' + key], skey=key)


    tl0 = A.f32(16)
    tk.op('act', lambda e: e.activation(out=tl0, in_=vecs[:, 152:168], func=AF.Exp, scale=-1.0), reads=['vecs'], writes=['tl0'])
    tk.op('act', lambda e: e.activation(out=tl0, in_=tl0, func=AF.Ln, bias=ONE, scale=1.0), reads=['epsc'], writes=['tl0'])
    tk.op('dve', lambda e: e.tensor_scalar(out=kco, in0=tl0, scalar1=-8.0, scalar2=None, op0=ALU.mult), reads=['tl0'], writes=['kco'])

    s5bc = A.f32(4 * 512)
    tk.dma('sp', s5bc, s5bc_d, writes=['s5bc'], skey='ld6')
    a_re, a_im, ldt = s5p[:, 0:32], s5p[:, 32:64], s5p[:, 64:96]
    h0r, h0i = s5p[:, 96:128], s5p[:, 128:160]
    T_ = [A.f32(32) for _ in range(12)]
    dt_, th_, r1, r2, nr, den, fre, fim, t8, t9, t10, t11 = T_

    def dv(fn, r, w):
        tk.op('dve', fn, reads=r, writes=w, strict=True)
    S = ['s5c']
    tk.op('act', lambda e: e.activation(out=dt_, in_=ldt, func=AF.Exp), reads=['s5p'], writes=S)
    dv(lambda e: e.tensor_tensor(out=t8, in0=a_re, in1=dt_, op=ALU.mult), S, S)
    tk.op('act', lambda e: e.activation(out=rho, in_=t8, func=AF.Exp), reads=S, writes=S)
    dv(lambda e: e.tensor_tensor(out=th_, in0=a_im, in1=dt_, op=ALU.mult), S, S)
    zi_t = nc.alloc_sbuf_tensor("zi_t", [128, 32], mybir.dt.int32)
    zi_ = zi_t[:, :]
    for (rr, sh) in ((r1, 8.0), (r2, 8.25)):
        dv(lambda e, rr=rr, sh=sh: e.tensor_scalar(out=rr, in0=th_, scalar1=1.0 / (2.0 * PI), scalar2=sh, op0=ALU.mult, op1=ALU.add), S, S)
        dv(lambda e, rr=rr: e.tensor_copy(out=zi_, in_=rr), S, S)
        dv(lambda e: e.tensor_copy(out=t8, in_=zi_), S, S)
        dv(lambda e, rr=rr: e.tensor_tensor(out=rr, in0=rr, in1=t8, op=ALU.subtract), S, S)
        dv(lambda e, rr=rr: e.tensor_scalar(out=t8, in0=rr, scalar1=0.5, scalar2=None, op0=ALU.is_gt), S, S)
        dv(lambda e, rr=rr: e.tensor_tensor(out=rr, in0=rr, in1=t8, op=ALU.subtract), S, S)
        dv(lambda e, rr=rr: e.tensor_scalar(out=rr, in0=rr, scalar1=2.0 * PI, scalar2=None, op0=ALU.mult), S, S)
    tk.op('act', lambda e: e.activation(out=sth, in_=r1, func=AF.Sin), reads=S, writes=S)
    tk.op('act', lambda e: e.activation(out=cth, in_=r2, func=AF.Sin), reads=S, writes=S)
    dv(lambda e: e.tensor_tensor(out=t9, in0=rho, in1=cth, op=ALU.mult), S, S)
    dv(lambda e: e.tensor_tensor(out=t10, in0=rho, in1=sth, op=ALU.mult), S, S)
    dv(lambda e: e.tensor_scalar(out=nr, in0=t9, scalar1=-1.0, scalar2=None, op0=ALU.add), S, S)
    dv(lambda e: e.tensor_tensor(out=den, in0=a_re, in1=a_re, op=ALU.mult), S, S)
    dv(lambda e: e.tensor_tensor(out=t8, in0=a_im, in1=a_im, op=ALU.mult), S, S)
    dv(lambda e: e.tensor_tensor(out=den, in0=den, in1=t8, op=ALU.add), S, S)
    dv(lambda e: e.reciprocal(out=den, in_=den), S, S)
    dv(lambda e: e.tensor_tensor(out=fre, in0=nr, in1=a_re, op=ALU.mult), S, S)
    dv(lambda e: e.tensor_tensor(out=t8, in0=t10, in1=a_im, op=ALU.mult), S, S)
    dv(lambda e: e.tensor_tensor(out=fre, in0=fre, in1=t8, op=ALU.add), S, S)
    dv(lambda e: e.tensor_tensor(out=fre, in0=fre, in1=den, op=ALU.mult), S, S)
    dv(lambda e: e.tensor_tensor(out=fim, in0=t10, in1=a_re, op=ALU.mult), S, S)
    dv(lambda e: e.tensor_tensor(out=t8, in0=nr, in1=a_im, op=ALU.mult), S, S)
    dv(lambda e: e.tensor_tensor(out=fim, in0=fim, in1=t8, op=ALU.subtract), S, S)
    dv(lambda e: e.tensor_tensor(out=fim, in0=fim, in1=den, op=ALU.mult), S, S)
    Bre = s5bc[:, 0:512].rearrange("p (u h) -> p u h", h=16)
    Bim = s5bc[:, 512:1024].rearrange("p (u h) -> p u h", h=16)
    Cre = s5bc[:, 1024:1536].rearrange("p (u h) -> p u h", h=16)
    Cim = s5bc[:, 1536:2048].rearrange("p (u h) -> p u h", h=16)
    bbr = A.f32(512); bbi = A.f32(512); tb = A.f32(512)
    bbr3 = bbr.rearrange("p (u h) -> p u h", h=16); bbi3 = bbi.rearrange("p (u h) -> p u h", h=16)
    tb3 = tb.rearrange("p (u h) -> p u h", h=16)

    def bc16(ap):
        return ap.unsqueeze(2).broadcast_to([128, 32, 16])
    S2 = ['s5c', 's5bc']
    dv(lambda e: e.tensor_tensor(out=bbr3, in0=Bre, in1=bc16(fre), op=ALU.mult), S2, S)
    dv(lambda e: e.tensor_tensor(out=tb3, in0=Bim, in1=bc16(fim), op=ALU.mult), S2, S)
    dv(lambda e: e.tensor_tensor(out=bbr3, in0=bbr3, in1=tb3, op=ALU.subtract), S, S)
    dv(lambda e: e.tensor_tensor(out=bbi3, in0=Bim, in1=bc16(fre), op=ALU.mult), S2, S)
    dv(lambda e: e.tensor_tensor(out=tb3, in0=Bre, in1=bc16(fim), op=ALU.mult), S2, S)
    dv(lambda e: e.tensor_tensor(out=bbi3, in0=bbi3, in1=tb3, op=ALU.add), S, S)
    BZr = A.f32(1024); BZi = A.f32(1024)
    BZr3 = BZr.rearrange("p (u m) -> p u m", m=32); BZi3 = BZi.rearrange("p (u m) -> p u m", m=32)
    C0r = A.f32(1024); C0i = A.f32(1024)
    C0r3 = C0r.rearrange("p (u m) -> p u m", m=32); C0i3 = C0i.rearrange("p (u m) -> p u m", m=32)
    for z_ in (BZr, BZi, C0r, C0i):
        dv(lambda e, z_=z_: e.memset(z_, 0.0), [], S)
    for (lo, hi, c0) in ((0, 64, 0), (64, 128, 16)):
        dv(lambda e, lo=lo, hi=hi, c0=c0: e.tensor_copy(out=BZr3[lo:hi, :, c0:c0 + 16], in_=bbr3[lo:hi]), S, S)
        dv(lambda e, lo=lo, hi=hi, c0=c0: e.tensor_copy(out=BZi3[lo:hi, :, c0:c0 + 16], in_=bbi3[lo:hi]), S, S)
        dv(lambda e, lo=lo, hi=hi, c0=c0: e.tensor_copy(out=C0r3[lo:hi, :, c0:c0 + 16], in_=Cre[lo:hi]), S2, S)
        dv(lambda e, lo=lo, hi=hi, c0=c0: e.tensor_copy(out=C0i3[lo:hi, :, c0:c0 + 16], in_=Cim[lo:hi]), S2, S)
    WKr3 = WKr.rearrange("p (k u) -> p k u", u=32); WKi3 = WKi.rearrange("p (k u) -> p k u", u=32)
    dv(lambda e: e.tensor_copy(out=WKr3[:, 0, :], in_=cth), S, S)
    dv(lambda e: e.tensor_scalar(out=WKi3[:, 0, :], in0=sth, scalar1=-1.0, scalar2=None, op0=ALU.mult), S, S)
    for k in range(8):
        dv(lambda e, k=k: e.tensor_tensor(out=t8, in0=WKr3[:, k, :], in1=WKr3[:, k, :], op=ALU.mult), S, S)
        dv(lambda e, k=k: e.tensor_tensor(out=t9, in0=WKi3[:, k, :], in1=WKi3[:, k, :], op=ALU.mult), S, S)
        dv(lambda e, k=k: e.tensor_tensor(out=WKr3[:, k + 1, :], in0=t8, in1=t9, op=ALU.subtract), S, S)
        dv(lambda e, k=k: e.tensor_tensor(out=t8, in0=WKr3[:, k, :], in1=WKi3[:, k, :], op=ALU.mult), S, S)
        dv(lambda e, k=k: e.tensor_scalar(out=WKi3[:, k + 1, :], in0=t8, scalar1=2.0, scalar2=None, op0=ALU.mult), S, S)
    PWr3 = PWr.rearrange("p (k u) -> p k u", u=32); PWi3 = PWi.rearrange("p (k u) -> p k u", u=32)
    RP3 = RP.rearrange("p (k u) -> p k u", u=32)
    dv(lambda e: e.memset(PWr3[:, 0, :], 1.0), [], S)
    dv(lambda e: e.memset(PWi3[:, 0, :], 0.0), [], S)
    dv(lambda e: e.memset(RP3[:, 0, :], 1.0), [], S)
    for k in range(15):
        dv(lambda e, k=k: e.tensor_tensor(out=t8, in0=PWr3[:, k, :], in1=cth, op=ALU.mult), S, S)
        dv(lambda e, k=k: e.tensor_tensor(out=t9, in0=PWi3[:, k, :], in1=sth, op=ALU.mult), S, S)
        dv(lambda e, k=k: e.tensor_tensor(out=PWr3[:, k + 1, :], in0=t8, in1=t9, op=ALU.subtract), S, S)
        dv(lambda e, k=k: e.tensor_tensor(out=t8, in0=PWr3[:, k, :], in1=sth, op=ALU.mult), S, S)
        dv(lambda e, k=k: e.tensor_tensor(out=t9, in0=PWi3[:, k, :], in1=cth, op=ALU.mult), S, S)
        dv(lambda e, k=k: e.tensor_tensor(out=PWi3[:, k + 1, :], in0=t8, in1=t9, op=ALU.add), S, S)
    for k in range(9):
        dv(lambda e, k=k: e.tensor_tensor(out=RP3[:, k + 1, :], in0=RP3[:, k, :], in1=rho, op=ALU.mult), S, S)
    dv(lambda e: e.tensor_tensor(out=t8, in0=cth, in1=h0r, op=ALU.mult), ['s5c', 's5p'], S)
    dv(lambda e: e.tensor_tensor(out=t9, in0=sth, in1=h0i, op=ALU.mult), ['s5c', 's5p'], S)
    dv(lambda e: e.tensor_tensor(out=ini0r, in0=t8, in1=t9, op=ALU.subtract), S, S)
    dv(lambda e: e.tensor_tensor(out=t8, in0=sth, in1=h0r, op=ALU.mult), ['s5c', 's5p'], S)
    dv(lambda e: e.tensor_tensor(out=t9, in0=cth, in1=h0i, op=ALU.mult), ['s5c', 's5p'], S)
    dv(lambda e: e.tensor_tensor(out=ini0i, in0=t8, in1=t9, op=ALU.add), S, S)
    dv(lambda e: e.tensor_tensor(out=t8, in0=PWr3[:, SP - 1, :], in1=h0r, op=ALU.mult), ['s5c', 's5p'], S)
    dv(lambda e: e.tensor_tensor(out=t9, in0=PWi3[:, SP - 1, :], in1=h0i, op=ALU.mult), ['s5c', 's5p'], S)
    dv(lambda e: e.tensor_tensor(out=hm1r, in0=t8, in1=t9, op=ALU.add), S, S)
    dv(lambda e: e.tensor_tensor(out=t8, in0=PWr3[:, SP - 1, :], in1=h0i, op=ALU.mult), ['s5c', 's5p'], S)
    dv(lambda e: e.tensor_tensor(out=t9, in0=PWi3[:, SP - 1, :], in1=h0r, op=ALU.mult), ['s5c', 's5p'], S)
    dv(lambda e: e.tensor_tensor(out=hm1i, in0=t8, in1=t9, op=ALU.subtract), S, S)
    ECr_t = A.f32(32 * ECN); ECi_t = A.f32(32 * ECN)
    ECr3 = ECr_t.rearrange("p (u c) -> p u c", c=ECN); ECi3 = ECi_t.rearrange("p (u c) -> p u c", c=ECN)
    WKr3 = WKr.rearrange("p (k u) -> p k u", u=32); WKi3 = WKi.rearrange("p (k u) -> p k u", u=32)
    eq1 = A.f32(16 * ECN).rearrange("p (u c) -> p u c", c=ECN // 2)
    eq2 = A.f32(16 * ECN).rearrange("p (u c) -> p u c", c=ECN // 2)
    EK_ = ['s5c']
    dv(lambda e: e.memset(ECr3[:, :, 0:1], 1.0), [], EK_)
    dv(lambda e: e.memset(ECi3[:, :, 0:1], 0.0), [], EK_)
    for k in range(ECN.bit_length() - 1):
        n = 1 << k
        wr = WKr3[:, LP + k, :].unsqueeze(2).broadcast_to([128, 32, n])
        wi = WKi3[:, LP + k, :].unsqueeze(2).broadcast_to([128, 32, n])
        e0r, e0i = ECr3[:, :, 0:n], ECi3[:, :, 0:n]
        q1, q2 = eq1[:, :, 0:n], eq2[:, :, 0:n]
        dv(lambda e: e.tensor_tensor(out=q1, in0=e0r, in1=wr, op=ALU.mult), EK_, EK_)
        dv(lambda e: e.tensor_tensor(out=q2, in0=e0i, in1=wi, op=ALU.mult), EK_, EK_)
        dv(lambda e: e.tensor_tensor(out=ECr3[:, :, n:2 * n], in0=q1, in1=q2, op=ALU.subtract), EK_, EK_)
        dv(lambda e: e.tensor_tensor(out=q1, in0=e0r, in1=wi, op=ALU.mult), EK_, EK_)
        dv(lambda e: e.tensor_tensor(out=q2, in0=e0i, in1=wr, op=ALU.mult), EK_, EK_)
        dv(lambda e: e.tensor_tensor(out=ECi3[:, :, n:2 * n], in0=q1, in1=q2, op=ALU.add), EK_, EK_)
    tk.dma('pool', ECD[:, 0, :], ECr_t, reads=['s5c'], writes=['ECD'], skey='ecd0')
    tk.dma('pool', ECD[:, 1, :], ECi_t, reads=['s5c'], writes=['ECD'], skey='ecd1')
    stB = A.bf16(8 * SP * 256).rearrange("p (x j c m) -> p x j c m", x=8, j=SP, c=2)
    stC = A.bf16(SP * 2048).rearrange("p (t j c u m) -> p t j c u m", t=2, j=SP, c=2, u=16)
    ZA = [[A.f32(1024), A.f32(1024)] for _ in range(1)]
    zt = [eq1.rearrange("p u c -> p (u c)")[:, 0:1024], eq2.rearrange("p u c -> p (u c)")[:, 0:1024]]
    pt_ = [A.f32(1024) for _ in range(2)]
    ff = A.f32(64)
    ffr, ffi = ff[:, 0:32], ff[:, 32:64]

    def v33(ap):
        return ap.rearrange("p (u m) -> p u m", m=32)

    def pl_(fn, r, w):
        tk.op('pool', fn, reads=r, writes=w)

    def dv2(fn, r, w):
        tk.op('dve', fn, reads=r, writes=w)

    def b32(ap):
        return ap.unsqueeze(2).broadcast_to([128, 32, 32])
    for jj in range(SP):
        pr, pi_ = b32(PWr3[:, jj, :]), b32(PWi3[:, jj, :])
        zb = 0
        Zr_, Zi_ = ZA[zb]
        ZK = [f'Z{zb}']
        z0, z1 = v33(zt[0]), v33(zt[1])
        dv2(lambda e: e.tensor_tensor(out=z0, in0=BZr3, in1=pr, op=ALU.mult), S, ['zt0'])
        dv2(lambda e: e.tensor_tensor(out=z1, in0=BZi3, in1=pi_, op=ALU.mult), S, ['zt1'])
        dv2(lambda e: e.tensor_tensor(out=v33(Zr_), in0=z0, in1=z1, op=ALU.add), ['zt0', 'zt1'], ZK)
        dv2(lambda e: e.tensor_tensor(out=z0, in0=BZi3, in1=pr, op=ALU.mult), S, ['zt0'])
        dv2(lambda e: e.tensor_tensor(out=z1, in0=BZr3, in1=pi_, op=ALU.mult), S, ['zt1'])
        dv2(lambda e: e.tensor_tensor(out=v33(Zi_), in0=z0, in1=z1, op=ALU.subtract), ['zt0', 'zt1'], ZK)
        for c, Z_ in enumerate((Zr_, Zi_)):
            for q in range(4):
                for d in range(2):
                    u0 = d * 16 + 4 * q
                    pi = nps(0, 8)
                    tk.deps('pe', ZK + ['ident'], [f'ps{pi}'])
                    ins = nc.tensor.transpose(PS[pi][:, 0:128], Z_[:, u0 * 32:u0 * 32 + 128], ident)
                    tk.cnt['pe'] += 1
                    ins.then_inc(tk.sem['pe'], 1)
                    tk.commit(('sem_pe', tk.sem['pe'], tk.cnt['pe']), ZK + ['ident'], [f'ps{pi}'])
                    tk.op('act', lambda e, pi=pi, c=c, q=q, d=d: e.copy(out=stB[:, q * 2 + d, jj, c, :], in_=PS[pi][:, 0:128]),
                          reads=[f'ps{pi}'], writes=['stB'])
    for d in range(2):
        U16 = slice(d * 16, d * 16 + 16)
        for jj in range(SP):
            pr, pi_ = b32(PWr3[:, jj, :]), b32(PWi3[:, jj, :])
            z0, z1 = v33(zt[0]), v33(zt[1])
            pl_(lambda e: e.tensor_tensor(out=ffr, in0=RP3[:, jj + 1, :], in1=PWr3[:, jj + SP, :], op=ALU.mult), S, ['ff'])
            pl_(lambda e: e.tensor_tensor(out=ffi, in0=RP3[:, jj + 1, :], in1=PWi3[:, jj + SP, :], op=ALU.mult), S, ['ff'])
            fr_, fi_ = b32(ffr), b32(ffi)
            for (tsel, xr, xi, rk, fn_, t0_, t1_, tk0, tk1) in ((0, pr, pi_, S, dv2, z0, z1, 'zt0', 'zt1'),
                                                               (1, fr_, fi_, ['ff'] + S, pl_, v33(pt_[0]), v33(pt_[1]), 'pt0', 'pt1')):
                dre = stC[:, tsel, jj, 0, :, :]
                dim = stC[:, tsel, jj, 1, :, :]
                fn_(lambda e: e.tensor_tensor(out=t0_[:, U16, :], in0=C0r3[:, U16, :], in1=xr[:, U16, :], op=ALU.mult), rk, [tk0])
                fn_(lambda e: e.tensor_tensor(out=t1_[:, U16, :], in0=C0i3[:, U16, :], in1=xi[:, U16, :], op=ALU.mult), rk, [tk1])
                fn_(lambda e: e.tensor_tensor(out=dre, in0=t0_[:, U16, :], in1=t1_[:, U16, :], op=ALU.subtract), [tk0, tk1], ['stC'])
                fn_(lambda e: e.tensor_tensor(out=t0_[:, U16, :], in0=C0r3[:, U16, :], in1=xi[:, U16, :], op=ALU.mult), rk, [tk0])
                fn_(lambda e: e.tensor_tensor(out=t1_[:, U16, :], in0=C0i3[:, U16, :], in1=xr[:, U16, :], op=ALU.mult), rk, [tk1])
                fn_(lambda e: e.tensor_tensor(out=t0_[:, U16, :], in0=t0_[:, U16, :], in1=t1_[:, U16, :], op=ALU.add), [tk0, tk1], [tk0])
                fn_(lambda e: e.tensor_scalar(out=dim, in0=t0_[:, U16, :], scalar1=-1.0, scalar2=None, op0=ALU.mult), [tk0], ['stC'])
        tk.dma('pool', S5C[d], stC.rearrange("p t j c u m -> p (t j c u m)"), reads=['stC'], writes=['S5C'], skey='stC')
    for x in range(8):
        tk.dma('pool', S5B[x], stB[:, x].rearrange("p j c m -> p (j c m)"), reads=['stB'], writes=['S5B'], skey='stB')
    for nm, ap_ in (('rho', rho), ('cth', cth), ('sth', sth), ('PWr', PWr), ('PWi', PWi), ('RP', RP),
                    ('hm1r', hm1r), ('ini0r', ini0r)):
        dbg(nm, ap_, ['s5c'])
    tk.barrier()
    A.off = PERSIST

    jobs = [dict(j=0, nseq=4, T=256, TL=256), dict(j=1, nseq=1, T=4096, TL=512)]

    def rms_rstd(sq3, rstd, keyin):
        pi = nps()
        TL = rstd.shape[1]
        tk.mm([(PS[pi][:, 0:TL], ones_bf, sq3[:, k, :], dict(start=(k == 0), stop=(k == 7))) for k in range(8)],
              reads=[keyin, 'ones'], writes=[f'ps{pi}'])
        tk.op('act', lambda e: e.activation(out=rstd, in_=PS[pi][:, 0:TL], func=AF.Sqrt, bias=EPS, scale=1.0 / D),
              reads=[f'ps{pi}', 'epsc'], writes=['rstd'])
        tk.op('dve', lambda e: e.reciprocal(out=rstd, in_=rstd), reads=['rstd'], writes=['rstd'])

    for job in jobs:
        if job['j'] not in jobs_sel:
            continue
        j, nseq, T, TL = job['j'], job['nseq'], job['T'], job['TL']
        NT = nseq * T
        ntile = NT // TL
        tps = T // TL
        xTj, yTj = xT[j], yT[j]
        hnDk = hnD[j].rearrange("(k p) t -> p k t", p=128)
        yaDk = yaD[j].rearrange("(k p) t -> p k t", p=128)
        ysDk = ysD[j].rearrange("(k p) t -> p k t", p=128)
        xk = xTj.rearrange("(k p) t -> p k t", p=128)
        yk = yTj.rearrange("(k p) t -> p k t", p=128)

        A.off = PERSIST
        xs = [A.f32(8 * TL).rearrange("p (k t) -> p k t", k=8) for _ in range(2)]
        sqs = A.bf16(8 * TL).rearrange("p (k t) -> p k t", k=8)
        hno = [A.bf16(8 * TL).rearrange("p (k t) -> p k t", k=8) for _ in range(2)]
        rstd = A.f32(TL)
        tmp = [A.f32(TL) for _ in range(2)]
        for ti in range(ntile):
            t0 = ti * TL
            sl = ti % 2
            tk.dma('sp', xs[sl], xk[:, :, t0:t0 + TL], writes=[f'x{sl}'], skey=f'x{sl}')
            tk.op('act', lambda e: e.activation(out=sqs, in_=xs[sl], func=AF.Square), reads=[f'x{sl}'], writes=['sqs'])
            rms_rstd(sqs, rstd, 'sqs')
            for k in range(8):
                tm = tmp[k % 2]
                tk.op('dve', lambda e: e.scalar_tensor_tensor(out=tm, in0=xs[sl][:, k, :], scalar=cj(cA1, k, j), in1=rstd,
                                                              op0=ALU.mult, op1=ALU.mult),
                      reads=[f'x{sl}', 'rstd'], writes=[f'tmp{k % 2}'])
                tk.op('act', lambda e: e.activation(out=hno[sl][:, k, :], in_=tm, func=AF.Identity, bias=cj(cB1, k, j), scale=1.0),
                      reads=[f'tmp{k % 2}'], writes=[f'hno{sl}'])
            tk.dma('pool', hnDk[:, :, t0:t0 + TL], hno[sl], reads=[f'hno{sl}'], writes=['hnD'], skey=f'hno{sl}')
        tk.barrier()

        A.off = PERSIST
        G = 1024
        ngrp = NT // G
        SEG = min(T, G)
        spg = G // SEG
        xa_pad = A.f32(nseq * (T + 3)).rearrange("p (s t) -> p s t", s=nseq)
        u_f = A.f32(NT)
        u_b = A.bf16(NT)
        hf = A.f32(NT)
        u3 = u_f.rearrange("p (s t) -> p s t", s=nseq)
        wxa = [A.bf16(8 * 128).rearrange("p (k n) -> p k n", k=8) for _ in range(2)]
        wga = [A.bf16(8 * 128).rearrange("p (k n) -> p k n", k=8) for _ in range(2)]
        hnl = [A.bf16(8 * 512).rearrange("p (k t) -> p k t", k=8) for _ in range(3)]
        NB = 2
        tr = [[A.f32(G) for _ in range(NB)] for _ in range(5)]
        cb = A.f32(8)
        cbc = [0]
        yao = [A.bf16(G) for _ in range(2)]
        lruw3 = lruw.rearrange("p (i m) -> p i m", m=128)
        hnc = [0]

        def load_hn(t0):
            sl = hnc[0] % 3
            hnc[0] += 1
            tk.dma('sp', hnl[sl], hnDk[:, :, t0:t0 + 512], writes=[f'hnl{sl}'], skey=f'hnl{sl}')
            return sl
        tk.op('dve', lambda e: e.memset(xa_pad, 0.0), writes=['xa_pad'])
        it = [0]
        psR, psI = PS2[0][:, :], PS2[1][:, :]
        def conv_grp(g, q):
            c0 = g * 1024
            uo = u_f[:, c0:c0 + 1024]
            tk.op('dve', lambda e: e.tensor_scalar(out=uo, in0=xa_pad[:, 0, c0:c0 + 1024], scalar1=col(vecs, 80 + q), scalar2=col(vecs, 112 + q),
                                                   op0=ALU.mult, op1=ALU.add), reads=['xa_pad', 'vecs'], writes=['u_f'])
            for tap in range(1, 4):
                tk.op('dve', lambda e: e.scalar_tensor_tensor(out=uo, in0=xa_pad[:, 0, c0 + tap:c0 + tap + 1024], scalar=col(vecs, 80 + 8 * tap + q),
                                                              in1=uo, op0=ALU.mult, op1=ALU.add), reads=['xa_pad', 'u_f'], writes=['u_f'])
            tk.op('act', lambda e: e.copy(out=u_b[:, c0:c0 + 1024], in_=uo), reads=['u_f'], writes=['u_b'])

        for q in range(8 if 2 in phases else 0):
            ws = q % 2
            tk.dma('pool', wxa[ws], kp(w_in, q * 128, q * 128 + 128), writes=[f'wxa{ws}'], skey=f'wxa{ws}')
            tk.dma('pool', wga[ws], kp(w_in, 1024 + q * 128, 1024 + q * 128 + 128), writes=[f'wga{ws}'], skey=f'wga{ws}')
            for ti in range(NT // 512):
                t0 = ti * 512
                sl = load_hn(t0)
                pi = nps(4, 8)
                tk.mm([(PS[pi], wxa[ws][:, k, :], hnl[sl][:, k, :], dict(start=(k == 0), stop=(k == 7))) for k in range(8)],
                      reads=[f'wxa{ws}', f'hnl{sl}'], writes=[f'ps{pi}'])
                if T >= 512:
                    s_, tt = t0 // T, t0 % T
                    o_, i_ = xa_pad[:, s_, 2 + tt:2 + tt + 512], PS[pi]
                else:
                    ns_ = 512 // T
                    s_ = t0 // T
                    o_, i_ = xa_pad[:, s_:s_ + ns_, 2:2 + T], PS[pi].rearrange("p (s t) -> p s t", s=ns_)
                tk.op('dve', lambda e: e.tensor_copy(out=o_, in_=i_), reads=[f'ps{pi}'], writes=['xa_pad'])
                if T >= 2048 and ti >= 2 and ti % 2 == 0:
                    conv_grp((ti - 2) // 2, q)
            if T >= 2048:
                conv_grp(NT // 1024 - 1, q)
            else:
                tk.op('dve', lambda e: e.tensor_scalar(out=u3, in0=xa_pad[:, :, 0:T], scalar1=col(vecs, 80 + q), scalar2=col(vecs, 112 + q),
                                                       op0=ALU.mult, op1=ALU.add), reads=['xa_pad', 'vecs'], writes=['u_f'])
                for tap in range(1, 4):
                    tk.op('dve', lambda e: e.scalar_tensor_tensor(out=u3, in0=xa_pad[:, :, tap:tap + T], scalar=col(vecs, 80 + 8 * tap + q),
                                                                  in1=u3, op0=ALU.mult, op1=ALU.add), reads=['xa_pad', 'u_f'], writes=['u_f'])
                tk.op('act', lambda e: e.copy(out=u_b, in_=u_f), reads=['u_f'], writes=['u_b'])
            for d in range(2):
                carry = (col(h0l, d * 8 + q) if j == 1 else ZERO)
                ckey = 'h0l' if j == 1 else 'epsc'
                order = list(range(ngrp)) if d == 0 else list(range(ngrp - 1, -1, -1))
                for r0_ in range(0, len(order), 2):
                    rnd = order[r0_:r0_ + 2]
                    ctxs = []
                    for g in rnd:
                        g0 = g * G
                        b_ = it[0] % NB
                        it[0] += 1
                        r_, i_, a_, a2_, iu_ = [tr[x][b_] for x in range(5)]
                        ctxs.append((g, g0, b_, r_, i_, a_, a2_, iu_))
                        K = lambda n, b_=b_: f'{n}{b_}'
                        for h in range(2):
                            ub = u_b[:, g0 + h * 512:g0 + (h + 1) * 512]
                            tk.mm([(psR[:, h * 512:(h + 1) * 512], lruw3[:, (d * 2 + 0) * 8 + q, :], ub, dict(start=True, stop=True))],
                                  reads=['lruw', 'u_b'], writes=['psR'])
                            tk.mm([(psI[:, h * 512:(h + 1) * 512], lruw3[:, (d * 2 + 1) * 8 + q, :], ub, dict(start=True, stop=True))],
                                  reads=['lruw', 'u_b'], writes=['psI'])
                        tk.op('act', lambda e: e.activation(out=r_, in_=psR, func=AF.Sigmoid, bias=col(vecs, 120 + d * 8 + q), scale=1.0),
                              reads=['psR'], writes=[K('r')])
                        tk.op('act', lambda e: e.activation(out=i_, in_=psI, func=AF.Sigmoid, bias=col(vecs, 136 + d * 8 + q), scale=1.0),
                              reads=['psI'], writes=[K('i')])
                    for (g, g0, b_, r_, i_, a_, a2_, iu_) in ctxs:
                        K = lambda n, b_=b_: f'{n}{b_}'
                        tk.op('act', lambda e: e.activation(out=a_, in_=r_, func=AF.Exp, scale=col(kco, d * 8 + q)),
                              reads=[K('r'), 'kco'], writes=[K('a')])
                        tk.op('dve', lambda e: e.tensor_tensor(out=a2_, in0=a_, in1=a_, op=ALU.mult), reads=[K('a')], writes=[K('a2')])
                        tk.op('pool', lambda e: e.tensor_tensor(out=iu_, in0=i_, in1=u_f[:, g0:g0 + G], op=ALU.mult),
                              reads=[K('i'), 'u_f'], writes=[K('iu')])
                    for (g, g0, b_, r_, i_, a_, a2_, iu_) in ctxs:
                        K = lambda n, b_=b_: f'{n}{b_}'
                        tk.op('act', lambda e: e.activation(out=a2_, in_=a2_, func=AF.Sqrt, bias=ONE, scale=-1.0),
                              reads=[K('a2'), 'epsc'], writes=[K('a2')])
                        tk.op('dve', lambda e: e.tensor_tensor(out=iu_, in0=a2_, in1=iu_, op=ALU.mult), reads=[K('a2'), K('iu')], writes=[K('iu')])
                    gel = []
                    for (g, g0, b_, r_, i_, a_, a2_, iu_) in ctxs:
                        K = lambda n, b_=b_: f'{n}{b_}'
                        bb_, hb_ = iu_, r_
                        for ss in range(spg):
                            lo = ss * SEG
                            gs = g0 + lo
                            sq_ = gs // T
                            if j == 0:
                                carry, ckey = ZERO, 'epsc'
                            if d == 0:
                                tk.op('dve', lambda e: e.tensor_tensor_scan(out=hf[:, gs:gs + SEG], data0=a_[:, lo:lo + SEG], data1=bb_[:, lo:lo + SEG],
                                                                             initial=carry, op0=ALU.mult, op1=ALU.add),
                                      reads=[K('a'), K('iu'), ckey], writes=['hf'])
                                carry, ckey = hf[:, gs + SEG - 1:gs + SEG], 'hf'
                                if j == 0:
                                    tk.op('pool', lambda e: e.tensor_copy(out=col(finl, (sq_ * 2 + 0) * 8 + q), in_=carry), reads=['hf'], writes=['finl'])
                            else:
                                tk.op('dve', lambda e: e.tensor_tensor_scan(out=hb_[:, lo:lo + SEG][:, ::-1], data0=a_[:, lo:lo + SEG][:, ::-1],
                                                                             data1=bb_[:, lo:lo + SEG][:, ::-1], initial=carry, op0=ALU.mult, op1=ALU.add),
                                      reads=[K('a'), K('iu'), ckey], writes=[K('r')])
                                cbi = cbc[0] % 8
                                cbc[0] += 1
                                tk.op('dve', lambda e: e.tensor_copy(out=cb[:, cbi:cbi + 1], in_=hb_[:, lo:lo + 1]), reads=[K('r')], writes=['cb'])
                                carry, ckey = cb[:, cbi:cbi + 1], 'cb'
                                if j == 0:
                                    tk.op('pool', lambda e: e.tensor_copy(out=col(finl, (sq_ * 2 + 1) * 8 + q), in_=carry), reads=[ckey], writes=['finl'])
                        if d == 1:
                            tk.op('dve', lambda e: e.tensor_tensor(out=i_, in0=hb_, in1=hf[:, g0:g0 + G], op=ALU.add),
                                  reads=[K('r'), 'hf'], writes=[K('i')])
                            pgi = b_
                            pg2 = PS2[2 + pgi][:, :]
                            pgb = [f'ps{4 + 2 * pgi}', f'ps{5 + 2 * pgi}']
                            for h in range(2):
                                sl = load_hn(g0 + h * 512)
                                tk.mm([(pg2[:, h * 512:(h + 1) * 512], wga[ws][:, k, :], hnl[sl][:, k, :], dict(start=(k == 0), stop=(k == 7))) for k in range(8)],
                                      reads=[f'wga{ws}', f'hnl{sl}'], writes=pgb)
                            gel.append((g0, b_, i_, a_, pg2, pgb))
                    for (g0, b_, i_, a_, pg2, pgb) in gel:
                        K = lambda n, b_=b_: f'{n}{b_}'
                        tk.op('act', lambda e: e.activation(out=a_, in_=pg2, func=AF.Gelu_apprx_tanh), reads=pgb + [K('a')], writes=[K('a')])
                        tk.op('dve', lambda e: e.tensor_tensor(out=yao[b_], in0=a_, in1=i_, op=ALU.mult),
                              reads=[K('a'), K('i')], writes=[f'yao{b_}'])
                        tk.dma('pool', yaD[j][q * 128:(q + 1) * 128, g0:g0 + G], yao[b_], reads=[f'yao{b_}'], writes=['yaD'], skey=f'yao{b_}')
        tk.barrier()

        A.off = PERSIST
        nch = TL // SP
        sg = [A.f32(TL) for _ in range(2)]
        vso = [A.bf16(TL) for _ in range(2)]
        P3MARK = A.off
        us_f = A.f32(NT)
        hs_f = A.f32(NT)
        usb = [A.bf16(NT) for _ in range(2)]
        wus = A.bf16(8 * 128).rearrange("p (k n) -> p k n", k=8)
        WB = SP * 256
        ECs2 = [A.f32(2 * 4 * ECN).rearrange("p (c u n) -> p c u n", c=2, u=4) for _ in range(2)]
        wS2 = [A.bf16(3 * WB) for _ in range(2)]
        Dt2 = [A.f32(4 * TL).rearrange("p (u t) -> p u t", u=4) for _ in range(2)]
        gR = A.f32(4 * TL).rearrange("p (u t) -> p u t", u=4)
        gI = A.f32(4 * TL).rearrange("p (u t) -> p u t", u=4)
        hnl = [gR.rearrange("p u t -> p (u t)").bitcast(BF16)[:, 0:8 * TL].rearrange("p (k t) -> p k t", k=8),
               gI.rearrange("p u t -> p (u t)").bitcast(BF16)[:, 0:8 * TL].rearrange("p (k t) -> p k t", k=8)]
        GA = [[f'gR{pl}' for pl in range(4)], [f'gI{pl}' for pl in range(4)]]
        gRb2 = [A.bf16(4 * TL).rearrange("p (u t) -> p u t", u=4) for _ in range(3)]
        gIb2 = [A.bf16(4 * TL).rearrange("p (u t) -> p u t", u=4) for _ in range(3)]

        def c4(n=nch):
            return A.f32(4 * n).rearrange("p (u c) -> p u c", u=4)
        X1, X2, Xr, Xi, Kr, Ki = [c4() for _ in range(6)]
        Lr2 = [c4() for _ in range(3)]
        Li2 = [c4() for _ in range(3)]
        HBr = [c4(nch + 1) for _ in range(2)]
        HBi = [c4(nch + 1) for _ in range(2)]
        HBrb2 = [A.bf16(4 * nch).rearrange("p (u c) -> p u c", u=4) for _ in range(2)]
        HBib2 = [A.bf16(4 * nch).rearrange("p (u c) -> p u c", u=4) for _ in range(2)]
        Km1r = [A.f32(8)[:, 0:4] for _ in range(2)]
        Km1i = [A.f32(8)[:, 0:4] for _ in range(2)]
        tq4 = [A.f32(8)[:, 0:4] for _ in range(2)]
        PWr3 = PWr.rearrange("p (k u) -> p k u", u=32); PWi3 = PWi.rearrange("p (k u) -> p k u", u=32)
        RP3 = RP.rearrange("p (k u) -> p k u", u=32)
        CK = ['chunk']

        def pv(fn, r, w):
            tk.op('pool', fn, reads=r, writes=w)
        tcnt = [0]
        for q in range(4):
            tk.dma('pool', wus, kp(w_in, 2048 + q * 128, 2048 + q * 128 + 128), writes=['wus'], skey='wus')
            for ti in range(ntile):
                hs_ = ti % 2
                tk.dma('sp', hnl[hs_], hnDk[:, :, ti * TL:(ti + 1) * TL], writes=[f'hnl{hs_}'] + GA[hs_], skey=f'hnl{hs_}')
                pi = nps(0, 2)
                tk.mm([(PS[pi][:, 0:TL], wus[:, k, :], hnl[hs_][:, k, :], dict(start=(k == 0), stop=(k == 7))) for k in range(8)],
                      reads=['wus', f'hnl{hs_}'], writes=[f'ps{pi}'])
                if j == 1:
                    r0 = ti * 8
                    o_ = us_f.rearrange("p (c r) -> p r c", r=64)[:, r0:r0 + 8, :]
                    i_ = PS[pi][:, 0:512].rearrange("p (r c) -> p r c", c=64)
                else:
                    o_ = us_f[:, ti * TL:(ti + 1) * TL]
                    i_ = PS[pi][:, 0:TL]
                tk.op('act', lambda e: e.copy(out=o_, in_=i_), reads=[f'ps{pi}'], writes=['us_f'])
            us3 = us_f.rearrange("p (s t) -> p s t", s=nseq)
            if q == 0:
                dbg(f'usf{j}', us_f, ['us_f'])
            tk.op('act', lambda e: e.copy(out=usb[0], in_=us_f), reads=['us_f'], writes=['usb0'])
            tk.op('pool', lambda e: e.tensor_copy(out=usb[1].rearrange("p (s t) -> p s t", s=nseq), in_=us3[:, :, ::-1]),
                  reads=['us_f'], writes=['usb1'])
            res = {}
            for d in range(2):
                u0 = d * 16 + 4 * q
                U4 = slice(u0, u0 + 4)
                wS, Dt, ECs = wS2[d], Dt2[d], ECs2[d]
                w_BT = wS[:, 0:WB].rearrange("p (j c m) -> p j c m", j=SP, c=2)
                w_CL = wS[:, WB:2 * WB].rearrange("p (j c u m) -> p j c u m", j=SP, c=2, u=4)
                w_CC = wS[:, 2 * WB:3 * WB].rearrange("p (j c u m) -> p j c u m", j=SP, c=2, u=4)
                tk.dma('sp', wS[:, 0:WB], S5B[q * 2 + d], writes=[f'wS{d}'], skey=f'wSb{d}')
                tk.dma('sp', wS[:, WB:3 * WB].rearrange("p (x u m) -> p x u m", u=4, m=32),
                       S5C[d].rearrange("p (x u m) -> p x u m", u=16, m=32)[:, :, 4 * q:4 * q + 4, :], writes=[f'wS{d}'], skey=f'wSc{d}')
                tk.op('dve', lambda e: e.tensor_copy(out=Dt, in_=rho[:, U4].unsqueeze(2).broadcast_to([128, 4, TL])), reads=['s5c'], writes=[f'Dt{d}'])
                tk.op('dve', lambda e: e.memset(Dt[:, :, 0::SP], 0.0), writes=[f'Dt{d}'])
                tk.dma('sp', ECs, ECD.rearrange("p c (u n) -> p c u n", n=ECN)[:, :, u0:u0 + 4, :], writes=[f'ECs{d}'], skey=f'ecs{d}')
                res[d] = (u0, U4, ECs[:, 0, :, 0:nch], ECs[:, 1, :, 0:nch], RP3[:, SP, U4], w_BT, w_CL, w_CC, Dt)
            if True:
                items = [(d_, s_, tq_) for d_ in range(2) for s_ in range(nseq) for tq_ in range(tps)]
                ctx = {}

                bank_ctx = {}

                def stage1_mm(it_, pls):
                    d, s, tq = it_
                    u0, U4, ecr, eci, r8b, w_BT, w_CL, w_CC, Dt = res[d]
                    t0 = (s * tps + tq) * TL
                    if it_ not in ctx:
                        ctx[it_] = tcnt[0] % 3
                        tcnt[0] += 1
                    for pl in pls:
                        pa, pb = nps(2, 6), nps(2, 6)
                        bank_ctx[(it_, pl)] = (pa, pb)
                        rows = slice(32 * pl, 32 * pl + 32)
                        for c, pp in ((0, pa), (1, pb)):
                            tk.mm([(PS[pp][:, jj:TL:SP], w_BT[rows, jj, c, :], usb[d][rows, t0 + jj:t0 + TL:SP],
                                    dict(start=True, stop=True, tile_position=(32 * pl, 0))) for jj in range(SP)],
                                  reads=[f'wS{d}', f'usb{d}'], writes=[f'ps{pp}'])

                def stage1_scan(it_, pls, last):
                    d, s, tq = it_
                    u0, U4, ecr, eci, r8b, w_BT, w_CL, w_CC, Dt = res[d]
                    gb = ctx[it_]
                    gRb, gIb = gRb2[gb], gIb2[gb]
                    for pl in pls:
                        pa, pb = bank_ctx[(it_, pl)]
                        tk.op('dve', lambda e: e.tensor_tensor_scan(out=gR[:, pl, :], data0=Dt[:, pl, :], data1=PS[pa][:, 0:TL], initial=0.0,
                                                                     op0=ALU.mult, op1=ALU.add), reads=[f'Dt{d}', f'ps{pa}'], writes=[f'gR{pl}', 'hnl0'])
                        tk.op('dve', lambda e: e.tensor_tensor_scan(out=gI[:, pl, :], data0=Dt[:, pl, :], data1=PS[pb][:, 0:TL], initial=0.0,
                                                                     op0=ALU.mult, op1=ALU.add), reads=[f'Dt{d}', f'ps{pb}'], writes=[f'gI{pl}', 'hnl1'])
                        tk.op('act', lambda e: e.copy(out=gRb[:, pl, :], in_=gR[:, pl, :]), reads=[f'gR{pl}'], writes=[f'gRb{gb}_{pl}'])
                        tk.op('act', lambda e: e.copy(out=gIb[:, pl, :], in_=gI[:, pl, :]), reads=[f'gI{pl}'], writes=[f'gIb{gb}_{pl}'])

                    if last:
                        tk.op('act', lambda e: e.copy(out=Lr2[gb], in_=gR[:, :, SP - 1::SP]), reads=[f'gR{pl}' for pl in range(4)], writes=[f'L{gb}'])
                        tk.op('act', lambda e: e.copy(out=Li2[gb], in_=gI[:, :, SP - 1::SP]), reads=[f'gI{pl}' for pl in range(4)], writes=[f'L{gb}'])

                def stage1(it_):
                    stage1_mm(it_, (0, 1)); stage1_scan(it_, (0, 1), False)
                    stage1_mm(it_, (2, 3)); stage1_scan(it_, (2, 3), True)

                def stage2(it_):
                    d, s, tq = it_
                    u0, U4, ecr, eci, r8b, w_BT, w_CL, w_CC, Dt = res[d]
                    ti = s * tps + tq
                    t0 = ti * TL
                    hb = tq % 2
                    Hr_, Hi_ = HBr[hb], HBi[hb]
                    gb = ctx[it_]
                    gRb, gIb, HBrb, HBib = gRb2[gb], gIb2[gb], HBrb2[gb % 2], HBib2[gb % 2]
                    if tq == 0:
                        if j == 1:
                            dv(lambda e: e.tensor_copy(out=Hr_[:, :, 0], in_=hm1r[:, U4]), ['s5c'], CK)
                            dv(lambda e: e.tensor_copy(out=Hi_[:, :, 0], in_=hm1i[:, U4]), ['s5c'], CK)
                            dv(lambda e: e.tensor_copy(out=Km1r[hb], in_=ini0r[:, U4]), ['s5c'], CK)
                            dv(lambda e: e.tensor_copy(out=Km1i[hb], in_=ini0i[:, U4]), ['s5c'], CK)
                        else:
                            for z_ in (Hr_[:, :, 0], Hi_[:, :, 0], Km1r[hb], Km1i[hb]):
                                dv(lambda e, z_=z_: e.memset(z_, 0.0), [], CK)

                    py = nps(6, 8)
                    GK = [f'L{gb}']
                    Lr, Li = Lr2[gb], Li2[gb]
                    pv(lambda e: e.tensor_tensor(out=X1, in0=ecr, in1=Lr, op=ALU.mult), GK + ['s5c', f'ECs{d}'], CK)
                    pv(lambda e: e.tensor_tensor(out=X2, in0=eci, in1=Li, op=ALU.mult), GK + ['s5c', f'ECs{d}'], CK)
                    pv(lambda e: e.tensor_tensor(out=Xr, in0=X1, in1=X2, op=ALU.subtract), CK, CK)
                    pv(lambda e: e.tensor_tensor(out=X1, in0=ecr, in1=Li, op=ALU.mult), GK + ['s5c', f'ECs{d}'], CK)
                    pv(lambda e: e.tensor_tensor(out=X2, in0=eci, in1=Lr, op=ALU.mult), GK + ['s5c', f'ECs{d}'], CK)
                    pv(lambda e: e.tensor_tensor(out=Xi, in0=X1, in1=X2, op=ALU.add), CK, CK)
                    for pl in range(4):
                        r8 = r8b[:, pl:pl + 1].broadcast_to([128, nch])
                        dv(lambda e: e.tensor_tensor_scan(out=Kr[:, pl, :], data0=r8, data1=Xr[:, pl, :], initial=Km1r[hb][:, pl:pl + 1],
                                                          op0=ALU.mult, op1=ALU.add), CK + ['s5c'], CK)
                        dv(lambda e: e.tensor_tensor_scan(out=Ki[:, pl, :], data0=r8, data1=Xi[:, pl, :], initial=Km1i[hb][:, pl:pl + 1],
                                                          op0=ALU.mult, op1=ALU.add), CK + ['s5c'], CK)
                    pv(lambda e: e.tensor_tensor(out=X1, in0=ecr, in1=Kr, op=ALU.mult), CK, CK)
                    pv(lambda e: e.tensor_tensor(out=X2, in0=eci, in1=Ki, op=ALU.mult), CK, CK)
                    pv(lambda e: e.tensor_tensor(out=Hr_[:, :, 1:nch + 1], in0=X1, in1=X2, op=ALU.add), CK, CK + [f'HB{hb}'])
                    pv(lambda e: e.tensor_tensor(out=X1, in0=ecr, in1=Ki, op=ALU.mult), CK, CK)
                    pv(lambda e: e.tensor_tensor(out=X2, in0=eci, in1=Kr, op=ALU.mult), CK, CK)
                    pv(lambda e: e.tensor_tensor(out=Hi_[:, :, 1:nch + 1], in0=X1, in1=X2, op=ALU.subtract), CK, CK + [f'HB{hb}'])
                    tk.op('act', lambda e: e.copy(out=HBrb, in_=Hr_[:, :, 0:nch]), reads=CK + [f'HB{hb}'], writes=[f'HBb{gb % 2}'])
                    tk.op('act', lambda e: e.copy(out=HBib, in_=Hi_[:, :, 0:nch]), reads=CK + [f'HB{hb}'], writes=[f'HBb{gb % 2}'])
                    hlr, hli = Hr_[:, :, nch], Hi_[:, :, nch]
                    if tq < tps - 1:
                        nb_ = (tq + 1) % 2
                        o_r, o_i = PWr3[:, SP, U4], PWi3[:, SP, U4]
                        pv(lambda e: e.tensor_copy(out=HBr[nb_][:, :, 0], in_=hlr), CK, CK + [f'HB{nb_}'])
                        pv(lambda e: e.tensor_copy(out=HBi[nb_][:, :, 0], in_=hli), CK, CK + [f'HB{nb_}'])
                        pv(lambda e: e.tensor_tensor(out=tq4[0], in0=o_r, in1=hlr, op=ALU.mult), CK + ['s5c'], CK)
                        pv(lambda e: e.tensor_tensor(out=tq4[1], in0=o_i, in1=hli, op=ALU.mult), CK + ['s5c'], CK)
                        pv(lambda e: e.tensor_tensor(out=Km1r[nb_], in0=tq4[0], in1=tq4[1], op=ALU.subtract), CK, CK)
                        pv(lambda e: e.tensor_tensor(out=tq4[0], in0=o_r, in1=hli, op=ALU.mult), CK + ['s5c'], CK)
                        pv(lambda e: e.tensor_tensor(out=tq4[1], in0=o_i, in1=hlr, op=ALU.mult), CK + ['s5c'], CK)
                        pv(lambda e: e.tensor_tensor(out=Km1i[nb_], in0=tq4[0], in1=tq4[1], op=ALU.add), CK, CK)
                    elif j == 0:
                        p7r, p7i = PWr3[:, SP - 1, U4], PWi3[:, SP - 1, U4]
                        fo_r = fins[:, ((0 * 4 + s) * 2 + d) * 16 + 4 * q:((0 * 4 + s) * 2 + d) * 16 + 4 * q + 4]
                        fo_i = fins[:, ((1 * 4 + s) * 2 + d) * 16 + 4 * q:((1 * 4 + s) * 2 + d) * 16 + 4 * q + 4]
                        pv(lambda e: e.tensor_tensor(out=tq4[0], in0=p7r, in1=hlr, op=ALU.mult), CK + ['s5c'], CK)
                        pv(lambda e: e.tensor_tensor(out=tq4[1], in0=p7i, in1=hli, op=ALU.mult), CK + ['s5c'], CK)
                        pv(lambda e: e.tensor_tensor(out=fo_r, in0=tq4[0], in1=tq4[1], op=ALU.subtract), CK, ['fins'])
                        pv(lambda e: e.tensor_tensor(out=tq4[0], in0=p7r, in1=hli, op=ALU.mult), CK + ['s5c'], CK)
                        pv(lambda e: e.tensor_tensor(out=tq4[1], in0=p7i, in1=hlr, op=ALU.mult), CK + ['s5c'], CK)
                        pv(lambda e: e.tensor_tensor(out=fo_i, in0=tq4[0], in1=tq4[1], op=ALU.add), CK, ['fins'])
                    def cmm():
                      for pl in range(4):
                        yq = PS[py][32 * pl:32 * pl + 32, :]
                        mms = []
                        for jj in range(SP):
                            tp_ = (0, 32 * pl)
                            mms.append((yq[:, jj:TL:SP], w_CL[:, jj, 0, pl, :], gRb[:, pl, jj:TL:SP], dict(start=True, stop=False, tile_position=tp_)))
                            mms.append((yq[:, jj:TL:SP], w_CL[:, jj, 1, pl, :], gIb[:, pl, jj:TL:SP], dict(start=False, stop=False, tile_position=tp_)))
                            mms.append((yq[:, jj:TL:SP], w_CC[:, jj, 0, pl, :], HBrb[:, pl, :], dict(start=False, stop=False, tile_position=tp_)))
                            mms.append((yq[:, jj:TL:SP], w_CC[:, jj, 1, pl, :], HBib[:, pl, :], dict(start=False, stop=True, tile_position=tp_)))
                        tk.mm(mms, reads=[f'wS{d}', f'HBb{gb % 2}', f'gRb{gb}_{pl}', f'gIb{gb}_{pl}'], writes=[f'ps{py}'])
                    def evac():
                        if d == 0:
                            tk.op('act', lambda e: e.copy(out=hs_f[:, t0:t0 + TL], in_=PS[py][:, 0:TL]), reads=[f'ps{py}'], writes=['hs_f'])
                        else:
                            lo = s * T + (T - tq * TL - TL)
                            hv = hs_f[:, lo:lo + TL][:, ::-1]
                            tk.op('dve', lambda e: e.tensor_tensor(out=hv, in0=hv, in1=PS[py][:, 0:TL], op=ALU.add),
                                  reads=[f'ps{py}', 'hs_f'], writes=['hs_f'])
                    return cmm, evac

                for it_ in items[0:2]:
                    stage1(it_)
                pend = None
                for ii_, it_ in enumerate(items):
                    nxt = items[ii_ + 2] if ii_ + 2 < len(items) else None
                    if nxt is not None:
                        stage1_mm(nxt, (0, 1))
                    cmm_, ev_ = stage2(it_)
                    if nxt is not None:
                        stage1_scan(nxt, (0, 1), False)
                        stage1_mm(nxt, (2, 3))
                        stage1_scan(nxt, (2, 3), True)
                    cmm_()
                    if pend is not None:
                        pend()
                    pend = ev_
                pend()
            if q == 0:
                dbg(f'hsf{j}', hs_f, ['hs_f'])
            for ti in range(ntile):
                t0 = ti * TL
                b_ = ti % 2
                tk.op('dve', lambda e: e.scalar_tensor_tensor(out=sg[b_], in0=us_f[:, t0:t0 + TL], scalar=col(vecs, 184 + q), in1=hs_f[:, t0:t0 + TL],
                                                              op0=ALU.mult, op1=ALU.add), reads=['us_f', 'hs_f'], writes=[f'sg{b_}'])
                tk.op('act', lambda e: e.activation(out=vso[b_], in_=sg[b_], func=AF.Gelu_apprx_tanh),
                      reads=[f'sg{b_}'], writes=[f'vso{b_}'])
                tk.dma('pool', vsD[j][q * 128:(q + 1) * 128, t0:t0 + TL], vso[b_], reads=[f'vso{b_}'], writes=['vsD'], skey=f'vso{b_}')
        tk.barrier()
        A.off = P3MARK
        wgl = A.bf16(4 * 512).rearrange("p (k n) -> p k n", k=4)
        ysn = A.bf16(4 * NT).rearrange("p (c t) -> p c t", c=4)
        vst = [A.bf16(4 * TL).rearrange("p (c t) -> p c t", c=4) for _ in range(2)]
        vsDk = vsD[j].rearrange("(k p) t -> p k t", p=128)
        tk.dma('pool', wgl, w_glu.rearrange("(k p) n -> p k n", p=128), writes=['wgl'], skey='wgl')
        for ti in range(ntile):
            t0 = ti * TL
            vb_ = ti % 2
            vt = vst[vb_]
            tk.dma('sp', vt, vsDk[:, :, t0:t0 + TL], writes=[f'vst{vb_}'], skey=f'vst{vb_}')
            for oc in range(4):
                pi = nps(0, 6)
                b_ = (ti * 4 + oc) % 2
                tk.mm([(PS[pi][:, 0:TL], wgl[:, k, oc * 128:(oc + 1) * 128], vt[:, k, :], dict(start=(k == 0), stop=(k == 3))) for k in range(4)],
                      reads=['wgl', f'vst{vb_}'], writes=[f'ps{pi}'])
                tk.op('act', lambda e: e.activation(out=sg[b_], in_=PS[pi][:, 0:TL], func=AF.Sigmoid, bias=col(vecs, 188 + oc), scale=1.0),
                      reads=[f'ps{pi}'], writes=[f'sg{b_}'])
                if j == 1:
                    c0 = ti * 8
                    o_ = ysn[:, oc, :].rearrange("p (r c) -> p c r", c=64)[:, c0:c0 + 8, :]
                    a_ = vt[:, oc, :].rearrange("p (c r) -> p c r", r=64)
                    g_ = sg[b_].rearrange("p (c r) -> p c r", r=64)
                else:
                    o_, a_, g_ = ysn[:, oc, t0:t0 + TL], vt[:, oc, :], sg[b_]
                tk.op('dve', lambda e: e.tensor_tensor(out=o_, in0=a_, in1=g_, op=ALU.mult), reads=[f'vst{vb_}', f'sg{b_}'], writes=['ysn'])
        for oc in range(4):
            tk.dma('pool', ysD[j][oc * 128:(oc + 1) * 128, :], ysn[:, oc, :], reads=['ysn'], writes=['ysD'], skey='ysn')
        tk.barrier()

        A.off = PERSIST
        TL = 512
        ntile = NT // TL
        NSL = 4
        ring = [A.bf16(8192) for _ in range(NSL)]
        rc = [0]

        def wload(src_ap, shape_k, ncol):
            sl = rc[0] % NSL
            rc[0] += 1
            v = ring[sl][:, 0:shape_k * ncol].rearrange("p (k n) -> p k n", k=shape_k)
            tk.dma('sp', v, src_ap, writes=[f'ring{sl}'], skey=f'ring{sl}')
            return v, f'ring{sl}'
        xt = A.f32(8 * TL).rearrange("p (k t) -> p k t", k=8)
        R1 = A.bf16(22 * TL)
        hnt = R1[:, 0:8 * TL].rearrange("p (k t) -> p k t", k=8)
        yat = R1[:, 8 * TL:16 * TL].rearrange("p (k t) -> p k t", k=8)
        yst = R1[:, 16 * TL:20 * TL].rearrange("p (k t) -> p k t", k=4)
        hmid = R1.rearrange("p (k t) -> p k t", k=22)
        mt = A.bf16(8 * TL).rearrange("p (k t) -> p k t", k=8)
        mo = A.f32(8 * TL).rearrange("p (k t) -> p k t", k=8)
        sqs = A.bf16(8 * TL).rearrange("p (k t) -> p k t", k=8)
        rstd = A.f32(TL)
        tm4 = [A.f32(TL) for _ in range(4)]
        R1K = ['hnt', 'yat', 'yst', 'hmid']
        for ti in range(ntile if 4 in phases else 0):
            t0 = ti * TL
            wg0, kg0 = wload(kp(WG, 0, 1024), 8, 1024)
            wg1, kg1 = wload(kp(WG, 1024, 2048), 8, 1024)
            wpl, kpl = wload(kp(WPL, 0, 1024), 8, 1024)
            wps, kps = wload(WPS.rearrange("(k p) n -> p k n", p=128), 4, 1024)
            tk.dma('sp', hnt, hnDk[:, :, t0:t0 + TL], writes=['hnt', 'hmid'], skey='hnt')
            tk.dma('sp', yat, yaDk[:, :, t0:t0 + TL], writes=['yat', 'hmid'], skey='yat')
            tk.dma('sp', yst, ysDk[:, :, t0:t0 + TL], writes=['yst', 'hmid'], skey='yst')
            tk.dma('sp', xt, xk[:, :, t0:t0 + TL], writes=['xt', 'xtA', 'xtB'], skey='xt')
            for oc in range(8):
                cs = slice(oc * 128, (oc + 1) * 128)
                p1, p2, p3, p4 = nps(), nps(), nps(), nps()
                tk.mm([(PS[p1][:, 0:TL], wg0[:, k, cs], hnt[:, k, :], dict(start=(k == 0), stop=(k == 7))) for k in range(8)],
                      reads=[kg0, 'hnt'], writes=[f'ps{p1}'])
                tk.mm([(PS[p2][:, 0:TL], wpl[:, k, cs], yat[:, k, :], dict(start=(k == 0), stop=(k == 7))) for k in range(8)],
                      reads=[kpl, 'yat'], writes=[f'ps{p2}'])
                tk.mm([(PS[p3][:, 0:TL], wg1[:, k, cs], hnt[:, k, :], dict(start=(k == 0), stop=(k == 7))) for k in range(8)],
                      reads=[kg1, 'hnt'], writes=[f'ps{p3}'])
                tk.mm([(PS[p4][:, 0:TL], wps[:, k, cs], yst[:, k, :], dict(start=(k == 0), stop=(k == 3))) for k in range(4)],
                      reads=[kps, 'yst'], writes=[f'ps{p4}'])
                tk.op('act', lambda e: e.activation(out=tm4[0], in_=PS[p1][:, 0:TL], func=AF.Sigmoid, bias=col(vecs, 168 + oc), scale=1.0),
                      reads=[f'ps{p1}'], writes=['tm0'])
                tk.op('act', lambda e: e.activation(out=tm4[1], in_=PS[p3][:, 0:TL], func=AF.Sigmoid, bias=col(vecs, 176 + oc), scale=1.0),
                      reads=[f'ps{p3}'], writes=['tm1'])
                tk.op('dve', lambda e: e.tensor_tensor(out=tm4[2], in0=tm4[0], in1=PS[p2][:, 0:TL], op=ALU.mult), reads=['tm0', f'ps{p2}'], writes=['tm2'])
                tk.op('dve', lambda e: e.tensor_tensor(out=tm4[3], in0=tm4[1], in1=PS[p4][:, 0:TL], op=ALU.mult), reads=['tm1', f'ps{p4}'], writes=['tm3'])
                tk.op('dve', lambda e: e.tensor_tensor(out=mt[:, oc, :], in0=tm4[2], in1=tm4[3], op=ALU.add), reads=['tm2', 'tm3'], writes=['mt'])
            wo, ko = wload(kp(WO, 0, 1024), 8, 1024)
            for oc in range(8):
                cs = slice(oc * 128, (oc + 1) * 128)
                p1 = nps()
                tk.mm([(PS[p1][:, 0:TL], wo[:, k, cs], mt[:, k, :], dict(start=(k == 0), stop=(k == 7))) for k in range(8)],
                      reads=[ko, 'mt'], writes=[f'ps{p1}'])
                tk.op('act', lambda e: e.copy(out=mo[:, oc, :], in_=PS[p1][:, 0:TL]), reads=[f'ps{p1}'], writes=['mo'] + [f'mo{k_}' for k_ in range(8)])
                tk.op('act', lambda e: e.activation(out=sqs[:, oc, :], in_=PS[p1][:, 0:TL], func=AF.Square), reads=[f'ps{p1}'], writes=['sqs'])
            rms_rstd(sqs, rstd, 'sqs')

            def residual(cG):
                for k in range(8):
                    tk.op('dve', lambda e: e.scalar_tensor_tensor(out=mo[:, k, :], in0=mo[:, k, :], scalar=cj(cG, k, j), in1=rstd, op0=ALU.mult, op1=ALU.mult),
                          reads=['mo', 'rstd'], writes=[f'mo{k}'])
                tk.op('dve', lambda e: e.tensor_tensor(out=xt[:, 0:5, :], in0=xt[:, 0:5, :], in1=mo[:, 0:5, :], op=ALU.add),
                      reads=[f'mo{k_}' for k_ in range(5)] + ['xt', 'xtA'], writes=['xtA'])
                tk.op('pool', lambda e: e.tensor_tensor(out=xt[:, 5:8, :], in0=xt[:, 5:8, :], in1=mo[:, 5:8, :], op=ALU.add),
                      reads=[f'mo{k_}' for k_ in range(5, 8)] + ['xt', 'xtB'], writes=['xtB'])
            residual(cG1)
            tk.op('act', lambda e: e.activation(out=sqs, in_=xt, func=AF.Square), reads=['xtA', 'xtB'], writes=['sqs'])
            rms_rstd(sqs, rstd, 'sqs')
            for k in range(8):
                tk.op('dve', lambda e: e.scalar_tensor_tensor(out=mo[:, k, :], in0=xt[:, k, :], scalar=cj(cA2, k, j), in1=rstd, op0=ALU.mult, op1=ALU.mult),
                      reads=['xtA', 'xtB', 'rstd'] + [f'mo{k_}' for k_ in range(8)], writes=[f'mo{k}'])
                tk.op('act', lambda e: e.activation(out=mt[:, k, :], in_=mo[:, k, :], func=AF.Identity, bias=cj(cB2, k, j), scale=1.0),
                      reads=[f'mo{k}'], writes=['mt', f'mt{k}'])
            for blk in range(11):
                w1, k1 = wload(kp(WFI, blk * 256, blk * 256 + 256), 8, 256)
                w3, k3 = wload(kp(WFI, 2816 + blk * 256, 2816 + blk * 256 + 256), 8, 256)
                for sub in range(2):
                    cs = slice(sub * 128, (sub + 1) * 128)
                    p1, p3 = nps(), nps()
                    tk.mm([(PS[p1][:, 0:TL], w1[:, k, cs], mt[:, k, :], dict(start=(k == 0), stop=(k == 7))) for k in range(8)],
                          reads=[k1, 'mt'] + [f'mt{k_}' for k_ in range(8)], writes=[f'ps{p1}'])
                    tk.mm([(PS[p3][:, 0:TL], w3[:, k, cs], mt[:, k, :], dict(start=(k == 0), stop=(k == 7))) for k in range(8)],
                          reads=[k3, 'mt'] + [f'mt{k_}' for k_ in range(8)], writes=[f'ps{p3}'])
                    tm = tm4[2 + sub]
                    tk.op('act', lambda e: e.activation(out=tm, in_=PS[p1][:, 0:TL], func=AF.Silu), reads=[f'ps{p1}'], writes=[f'tm{2 + sub}'])
                    tk.op('dve', lambda e: e.tensor_tensor(out=hmid[:, blk * 2 + sub, :], in0=tm, in1=PS[p3][:, 0:TL], op=ALU.mult),
                          reads=[f'tm{2 + sub}', f'ps{p3}'], writes=R1K)
            for oc in range(8):
                wf, kf = wload(WFO[:, oc * 128:(oc + 1) * 128].rearrange("(k p) n -> p k n", p=128), 22, 128)
                p1 = nps()
                tk.mm([(PS[p1][:, 0:TL], wf[:, k, :], hmid[:, k, :], dict(start=(k == 0), stop=(k == 21))) for k in range(22)],
                      reads=[kf, 'hmid'], writes=[f'ps{p1}'])
                tk.op('act', lambda e: e.copy(out=mo[:, oc, :], in_=PS[p1][:, 0:TL]), reads=[f'ps{p1}'], writes=['mo'] + [f'mo{k_}' for k_ in range(8)])
                tk.op('act', lambda e: e.activation(out=sqs[:, oc, :], in_=PS[p1][:, 0:TL], func=AF.Square), reads=[f'ps{p1}'], writes=['sqs'])
            rms_rstd(sqs, rstd, 'sqs')
            residual(cG2)
            tk.dma('pool', yk[:, :, t0:t0 + TL], xt, reads=['xt', 'xtA', 'xtB'], writes=['yT'], skey='yst_out')
        tk.barrier()

    tk.dma('pool', finl_d, finl, reads=['finl'], writes=['finl_d'], skey='fl')
    tk.dma('pool', fins_d, fins, reads=['fins'], writes=['fins_d'], skey='fs')
    tk.barrier()
    return nc


_NC = None


def _host_inputs(inp, c):
    f = np.float32
    g = lambda k: np.asarray(inp[k], dtype=f)

    def v8(vec):
        return np.ascontiguousarray(vec.reshape(-1, 128).T)
    m = {}
    m["xT_p"] = np.ascontiguousarray(g("x_prompt")[4 * c:4 * c + 4].reshape(1024, 1024).T)
    m["xT_s"] = np.ascontiguousarray(g("x_sample")[c].T)
    cv = np.stack([g("c_ctx"), g("c")[c]], axis=1)
    m["cv"] = np.ascontiguousarray(cv.reshape(8, 128, 2).transpose(1, 0, 2).reshape(128, 16))
    cols = [v8(g("g_pre_mix")[0]), v8(g("g_post_mix")[0]), v8(g("g_pre_ffn")[0]), v8(g("g_post_ffn")[0]),
            v8(g("b_mod")[0]), v8(g("conv_w")[0].reshape(-1)), v8(g("conv_b")[0]), v8(g("lru_b_r")[0].reshape(-1)),
            v8(g("lru_b_i")[0].reshape(-1)), v8(g("lru_lambda")[0].reshape(-1)), v8(g("b_gate")[0]),
            v8(g("s5_d")[0]), v8(g("s5_b_glu")[0])]
    m["vecs"] = np.ascontiguousarray(np.concatenate(cols, axis=1))
    assert m["vecs"].shape == (128, NV)
    lw = np.zeros((128, 2, 2, 8, 128), f)
    for gi, key in enumerate(("lru_w_r", "lru_w_i")):
        w = g(key)[0]
        for hh in range(2):
            lw[64 * hh:64 * hh + 64, :, gi, :, 64 * hh:64 * hh + 64] = w[:, hh::2].transpose(2, 0, 1, 3)
    m["lruw"] = lw.reshape(128, 32 * 128)
    m["h0l"] = np.ascontiguousarray(g("state_lru")[c, 0].reshape(2, 8, 128).transpose(2, 0, 1).reshape(128, 16))

    def unit(a):
        sh = a.shape[3:]
        a = a.reshape((2, 16, 2, 64) + sh)
        a = np.moveaxis(a, (2, 3), (0, 1))
        return np.ascontiguousarray(a.reshape((128, 32) + sh))
    ldt = np.broadcast_to(g("s5_log_dt")[0][:, :, None], (2, 32, 64))
    s5p = np.stack([unit(g("s5_a_re")[0]), unit(g("s5_a_im")[0]), unit(ldt),
                    unit(g("state_s5_re")[c, 0]), unit(g("state_s5_im")[c, 0])], axis=1)
    m["s5p"] = np.ascontiguousarray(s5p.reshape(128, 160))
    s5bc = np.stack([unit(g("s5_b_re")[0]), unit(g("s5_b_im")[0]),
                     unit(g("s5_c_re")[0].transpose(0, 1, 3, 2)), unit(g("s5_c_im")[0].transpose(0, 1, 3, 2))], axis=1)
    m["s5bc"] = np.ascontiguousarray(s5bc.reshape(128, 4 * 512))
    m["ident"] = np.eye(128, dtype=f)
    m["w_mod"] = g("w_mod")[0]; m["w_in"] = g("w_in")[0]; m["w_gate"] = g("w_gate")[0]; m["w_out"] = g("w_out")[0]
    m["w_pl"] = g("w_proj_lru")[0]; m["w_ps"] = g("w_proj_s5")[0]; m["w_fi"] = g("w_ff_in")[0]; m["w_fo"] = g("w_ff_out")[0]
    m["w_glu"] = g("s5_w_glu")[0]
    return m


def kernel(**inputs):
    global _NC
    if _NC is None:
        _NC = build_nc()
    nc = _NC
    in_maps = [_host_inputs(inputs, c) for c in range(8)]
    res = run_bass_kernel_spmd(nc, in_maps, core_ids=list(range(8)))
    y_p = np.zeros((32, 256, 1024), np.float32)
    y_s = np.zeros((8, 4096, 1024), np.float32)
    nl = np.zeros((32, 1, 2, 1024), np.float32)
    nre = np.zeros((32, 1, 2, 32, 64), np.float32)
    nim = np.zeros((32, 1, 2, 32, 64), np.float32)
    for c in range(8):
        r = res.results[c]
        y_p[4 * c:4 * c + 4] = r["yT_p"].T.reshape(4, 256, 1024)
        y_s[c] = r["yT_s"].T
        fl = r["fin_lru"].reshape(128, 4, 2, 8)
        nl[4 * c:4 * c + 4, 0] = fl.transpose(1, 2, 3, 0).reshape(4, 2, 1024)
        fs = r["fin_s5"].reshape(2, 64, 2, 4, 2, 16)
        fs = fs.transpose(2, 3, 4, 5, 0, 1).reshape(2, 4, 2, 32, 64)
        nre[4 * c:4 * c + 4, 0] = fs[0]
        nim[4 * c:4 * c + 4, 0] = fs[1]
    return (y_p, y_s, nl, nre, nim)
```

```python
import numpy as np
import concourse.bass as bass
import concourse.mybir as mybir
from concourse.bass_utils import run_bass_kernel_spmd

F32, BF16 = mybir.dt.float32, mybir.dt.bfloat16
AF = mybir.ActivationFunctionType
ALU = mybir.AluOpType
PI = float(np.pi)
NV = 192
SP = 8
LP = 3
ECN = 512 // SP
ARENA_WORDS = 47000


class TK:
    def __init__(s, nc):
        s.nc = nc
        s.eng = {'pe': nc.tensor, 'act': nc.scalar, 'dve': nc.vector, 'pool': nc.gpsimd, 'sp': nc.sync}
        s.sem = {e: nc.alloc_semaphore('sem_' + e) for e in s.eng}
        s.cnt = {e: 0 for e in s.eng}
        s.seen = {e: {} for e in s.eng}
        s.lastw = {}
        s.readers = {}
        s.dsem = {}

    def _wait(s, e, ev):
        name, obj, val = ev
        if name == 'sem_' + e and e == 'pe':
            return
        if s.seen[e].get(name, 0) >= val:
            return
        s.eng[e].wait_ge(obj, val)
        s.seen[e][name] = val

    def deps(s, e, reads, writes):
        for k in reads:
            if k in s.lastw:
                s._wait(e, s.lastw[k])
        for k in writes:
            if k in s.lastw:
                s._wait(e, s.lastw[k])
            for ev in s.readers.get(k, {}).values():
                s._wait(e, ev)

    def commit(s, ev, reads, writes):
        for k in reads:
            s.readers.setdefault(k, {})[ev[0]] = ev
        for k in writes:
            s.lastw[k] = ev
            s.readers[k] = {}

    def op(s, e, fn, reads=(), writes=(), strict=False):
        s.deps(e, reads, writes)
        own = 'sem_' + e
        if not strict and e != 'pe':
            for k in list(reads) + list(writes):
                ev = s.lastw.get(k)
                if ev is not None and ev[0] == own and ev[2] >= s.cnt[e] - 1:
                    strict = True
            for k in writes:
                ev = s.readers.get(k, {}).get(own)
                if ev is not None and ev[2] >= s.cnt[e] - 1:
                    strict = True
        if strict and s.cnt[e] > 0:
            s.eng[e].wait_ge(s.sem[e], s.cnt[e])
        ins = fn(s.eng[e])
        s.cnt[e] += 1
        ins.then_inc(s.sem[e], 1)
        s.commit(('sem_' + e, s.sem[e], s.cnt[e]), reads, writes)

    def mm(s, mms, reads=(), writes=()):
        s.deps('pe', reads, writes)
        ins = None
        for (o, l, r, kw) in mms:
            ins = s.nc.tensor.matmul(o, lhsT=l, rhs=r, **kw)
        s.cnt['pe'] += 1
        ins.then_inc(s.sem['pe'], 1)
        s.commit(('sem_pe', s.sem['pe'], s.cnt['pe']), reads, writes)

    def dma(s, q, out, in_, reads=(), writes=(), skey=None):
        s.deps(q, reads, writes)
        if skey not in s.dsem:
            s.dsem[skey] = [s.nc.alloc_semaphore('d_' + skey), 0]
        d = s.dsem[skey]
        d[1] += 16
        s.eng[q].dma_start(out=out, in_=in_).then_inc(d[0], 16)
        s.commit(('d_' + skey, d[0], d[1]), reads, writes)

    def barrier(s):
        evs = [('sem_' + e, s.sem[e], s.cnt[e]) for e in s.eng if s.cnt[e] > 0]
        evs += [('d_' + k, d[0], d[1]) for k, d in s.dsem.items()]
        for e in s.eng:
            for ev in evs:
                s._wait(e, ev)
        s.lastw.clear()
        s.readers.clear()


class Arena:
    def __init__(s, t, words):
        s.t, s.words, s.off = t, words, 0

    def f32(s, n):
        n = (n + 7) // 8 * 8
        ap = s.t[:, s.off:s.off + n]
        s.off += n
        assert s.off <= s.words, ("arena overflow", s.off, s.words)
        return ap

    def bf16(s, n):
        w = ((n + 1) // 2 + 7) // 8 * 8
        ap = s.t[:, s.off:s.off + w].bitcast(BF16)
        s.off += w
        assert s.off <= s.words, ("arena overflow", s.off, s.words)
        return ap[:, 0:n]


def build_nc(debug=False, jobs_sel=(0, 1), phases=(1, 2, 3, 4)):
    nc = bass.Bass("TRN2", target_bir_lowering=False)
    D = 1024
    dbg_n = [0]

    def dbg(name, ap, keys):
        if not debug:
            return
        shp = [int(x) for x in ap.shape]
        o = nc.dram_tensor("dbg_" + name, shp, ap.dtype, kind="ExternalOutput").ap()
        tk.dma('pool', o, ap, reads=keys, writes=['dbgout'], skey='dbg%d' % (dbg_n[0] % 4))
        dbg_n[0] += 1

    def din(name, shape):
        return nc.dram_tensor(name, list(shape), F32, kind="ExternalInput").ap()

    def dout(name, shape):
        return nc.dram_tensor(name, list(shape), F32, kind="ExternalOutput").ap()

    def dscr(name, shape, dt=BF16):
        return nc.dram_tensor(name, list(shape), dt, kind="Internal").ap()

    xT = [din("xT_p", [D, 1024]), din("xT_s", [D, 4096])]
    yT = [dout("yT_p", [D, 1024]), dout("yT_s", [D, 4096])]
    cv_d = din("cv", [128, 16])
    vecs_d = din("vecs", [128, NV])
    lruw_d = din("lruw", [128, 32 * 128])
    h0l_d = din("h0l", [128, 16])
    s5p_d = din("s5p", [128, 5 * 32])
    s5bc_d = din("s5bc", [128, 4 * 32 * 16])
    ident_d = din("ident", [128, 128])
    w_mod = din("w_mod", [D, 6144])
    w_in = din("w_in", [D, 2560])
    w_gate = din("w_gate", [D, 2048])
    w_out = din("w_out", [D, D])
    w_pl = din("w_pl", [D, D])
    w_ps = din("w_ps", [512, D])
    w_fi = din("w_fi", [D, 5632])
    w_fo = din("w_fo", [2816, D])
    w_glu = din("w_glu", [512, 512])
    finl_d = dout("fin_lru", [128, 64])
    fins_d = dout("fin_s5", [128, 256])

    WG = dscr("WG", [D, 2048]); WO = dscr("WO", [D, D]); WPL = dscr("WPL", [D, D]); WPS = dscr("WPS", [512, D])
    WFI = dscr("WFI", [D, 5632]); WFO = dscr("WFO", [2816, D])
    hnD = [dscr("hnD0", [D, 1024]), dscr("hnD1", [D, 4096])]
    yaD = [dscr("yaD0", [D, 1024]), dscr("yaD1", [D, 4096])]
    vsD = [dscr("vsD0", [512, 1024]), dscr("vsD1", [512, 4096])]
    S5B = dscr("S5B", [8, 128, SP * 256])
    S5C = dscr("S5C", [2, 128, SP * 2048])
    ECD = dscr("ECD", [128, 2, 32 * ECN], F32)
    ysD = [dscr("ysD0", [512, 1024]), dscr("ysD1", [512, 4096])]

    tk = TK(nc)
    arena_t = nc.alloc_sbuf_tensor("arena", [128, ARENA_WORDS], F32)
    A = Arena(arena_t, ARENA_WORDS)
    PS2 = [nc.alloc_psum_tensor(f"pp{i}", [128, 1024], F32) for i in range(4)]
    PS = [PS2[i // 2][:, (i % 2) * 512:(i % 2) * 512 + 512] for i in range(8)]
    psc = [0]

    def nps(lo=0, hi=8):
        i = lo + psc[0] % (hi - lo)
        psc[0] += 1
        return i

    def kp(w, c0, c1):
        return w[:, c0:c1].rearrange("(k p) n -> p k n", p=128)

    vecs = A.f32(NV)
    cv = A.f32(16)
    h0l = A.f32(16)
    ident = A.f32(128)
    s5p = A.f32(160)
    lruw = A.bf16(32 * 128)
    ones_bf = A.bf16(128)
    epsc = A.f32(8)
    modc = A.f32(96)
    cA1 = A.f32(16); cB1 = A.f32(16); cG1 = A.f32(16); cA2 = A.f32(16); cB2 = A.f32(16); cG2 = A.f32(16)
    kco = A.f32(16)
    WKr = A.f32(9 * 32); WKi = A.f32(9 * 32)
    PWr = A.f32(16 * 32); PWi = A.f32(16 * 32)
    RP = A.f32(10 * 32)
    hm1r = A.f32(32); hm1i = A.f32(32)
    cth = A.f32(32); sth = A.f32(32); rho = A.f32(32)
    ini0r = A.f32(32); ini0i = A.f32(32)
    finl = A.f32(64)
    fins = A.f32(256)
    PERSIST = A.off

    def col(ap, i):
        return ap[:, i:i + 1]

    tk.dma('sp', vecs, vecs_d, writes=['vecs'], skey='ld0')
    tk.dma('sp', cv, cv_d, writes=['cv'], skey='ld1')
    tk.dma('sp', h0l, h0l_d, writes=['h0l'], skey='ld2')
    tk.dma('sp', ident, ident_d, writes=['ident'], skey='ld3')
    tk.dma('sp', s5p, s5p_d, writes=['s5p'], skey='ld4')
    tk.dma('pool', lruw, lruw_d, writes=['lruw'], skey='ld5')
    tk.op('dve', lambda e: e.memset(ones_bf, 1.0), writes=['ones'])
    tk.op('dve', lambda e: e.memset(epsc[:, 0:1], 1e-6), writes=['epsc'])
    tk.op('dve', lambda e: e.memset(epsc[:, 1:2], 1.0), writes=['epsc'])
    tk.op('dve', lambda e: e.memset(epsc[:, 2:3], 0.0), writes=['epsc'])
    tk.op('dve', lambda e: e.memset(finl, 0.0), writes=['finl'])
    tk.op('dve', lambda e: e.memset(fins, 0.0), writes=['fins'])
    EPS, ONE, ZERO = epsc[:, 0:1], epsc[:, 1:2], epsc[:, 2:3]

    m0 = A.off
    scb = A.bf16(16)
    tk.op('act', lambda e: e.activation(out=scb, in_=cv, func=AF.Silu), reads=['cv'], writes=['scb'])
    scb3 = scb.rearrange("p (k j) -> p k j", j=2)
    wms = [A.bf16(8 * 512) for _ in range(2)]
    wmf = [A.f32(8 * 512) for _ in range(2)]
    psm = PS[7]
    for blk in range(12):
        wslot = wms[blk % 2].rearrange("p (k n) -> p k n", k=8)
        wf32 = wmf[blk % 2].rearrange("p (k n) -> p k n", k=8)
        tk.dma('sp', wf32, kp(w_mod, blk * 512, blk * 512 + 512), writes=[f'wmf{blk % 2}'], skey=f'wmf{blk % 2}')
        if blk % 2 == 0:
            tk.op('act', lambda e: e.copy(out=wslot, in_=wf32), reads=[f'wmf{blk % 2}'], writes=[f'wm{blk % 2}'])
        else:
            tk.op('dve', lambda e: e.tensor_copy(out=wslot, in_=wf32), reads=[f'wmf{blk % 2}'], writes=[f'wm{blk % 2}'])
        for oc in range(4):
            c = blk * 4 + oc
            tk.mm([(psm[:, 2 * c:2 * c + 2], wslot[:, k, oc * 128:(oc + 1) * 128], scb3[:, k, :],
                    dict(start=(k == 0), stop=(k == 7))) for k in range(8)],
                  reads=[f'wm{blk % 2}', 'scb'], writes=['psm'])
    modc3 = modc.rearrange("p (c j) -> p c j", j=2)
    tk.op('dve', lambda e: e.tensor_tensor(out=modc3, in0=psm[:, 0:96].rearrange("p (c j) -> p c j", j=2),
                                           in1=vecs[:, 32:80].unsqueeze(2).broadcast_to([128, 48, 2]), op=ALU.add),
          reads=['psm', 'vecs'], writes=['modc'])

    def msec(s):
        return modc[:, 16 * s:16 * s + 16].rearrange("p (k j) -> p k j", j=2)

    def vb(c0):
        return vecs[:, c0:c0 + 8].unsqueeze(2).broadcast_to([128, 8, 2])

    def c3(ap):
        return ap.rearrange("p (k j) -> p k j", j=2)
    tk.op('dve', lambda e: e.scalar_tensor_tensor(out=c3(cA1), in0=msec(1), scalar=1.0, in1=vb(0), op0=ALU.add, op1=ALU.mult),
          reads=['modc'], writes=['cA1'])
    tk.op('dve', lambda e: e.tensor_copy(out=c3(cB1), in_=msec(0)), reads=['modc'], writes=['cB1'])
    tk.op('dve', lambda e: e.tensor_tensor(out=c3(cG1), in0=msec(2), in1=vb(8), op=ALU.mult), reads=['modc'], writes=['cG1'])
    tk.op('dve', lambda e: e.scalar_tensor_tensor(out=c3(cA2), in0=msec(4), scalar=1.0, in1=vb(16), op0=ALU.add, op1=ALU.mult),
          reads=['modc'], writes=['cA2'])
    tk.op('dve', lambda e: e.tensor_copy(out=c3(cB2), in_=msec(3)), reads=['modc'], writes=['cB2'])
    tk.op('dve', lambda e: e.tensor_tensor(out=c3(cG2), in0=msec(5), in1=vb(24), op=ALU.mult), reads=['modc'], writes=['cG2'])

    def cj(cst, k, j):
        return cst[:, 2 * k + j:2 * k + j + 1]

    tk.barrier()
    A.off = m0
    for (dst, src, rows, key) in ((WG, w_gate, D, 'cg'), (WPL, w_pl, D, 'cpl'), (WPS, w_ps, 512, 'cps'),
                                  (WO, w_out, D, 'co'), (WFI, w_fi, D, 'cfi'), (WFO, w_fo, 2816, 'cfo')):
        tk.dma('pool', dst.rearrange("(p a) n -> p (a n)", p=128), src.rearrange("(p a) n -> p (a n)", p=128),
               writes=['W' + key], skey=key)


    tl0 = A.f32(16)
    tk.op('act', lambda e: e.activation(out=tl0, in_=vecs[:, 152:168], func=AF.Exp, scale=-1.0), reads=['vecs'], writes=['tl0'])
    tk.op('act', lambda e: e.activation(out=tl0, in_=tl0, func=AF.Ln, bias=ONE, scale=1.0), reads=['epsc'], writes=['tl0'])
    tk.op('dve', lambda e: e.tensor_scalar(out=kco, in0=tl0, scalar1=-8.0, scalar2=None, op0=ALU.mult), reads=['tl0'], writes=['kco'])

    s5bc = A.f32(4 * 512)
    tk.dma('sp', s5bc, s5bc_d, writes=['s5bc'], skey='ld6')
    a_re, a_im, ldt = s5p[:, 0:32], s5p[:, 32:64], s5p[:, 64:96]
    h0r, h0i = s5p[:, 96:128], s5p[:, 128:160]
    T_ = [A.f32(32) for _ in range(12)]
    dt_, th_, r1, r2, nr, den, fre, fim, t8, t9, t10, t11 = T_

    def dv(fn, r, w):
        tk.op('dve', fn, reads=r, writes=w, strict=True)
    S = ['s5c']
    tk.op('act', lambda e: e.activation(out=dt_, in_=ldt, func=AF.Exp), reads=['s5p'], writes=S)
    dv(lambda e: e.tensor_tensor(out=t8, in0=a_re, in1=dt_, op=ALU.mult), S, S)
    tk.op('act', lambda e: e.activation(out=rho, in_=t8, func=AF.Exp), reads=S, writes=S)
    dv(lambda e: e.tensor_tensor(out=th_, in0=a_im, in1=dt_, op=ALU.mult), S, S)
    zi_t = nc.alloc_sbuf_tensor("zi_t", [128, 32], mybir.dt.int32)
    zi_ = zi_t[:, :]
    for (rr, sh) in ((r1, 8.0), (r2, 8.25)):
        dv(lambda e, rr=rr, sh=sh: e.tensor_scalar(out=rr, in0=th_, scalar1=1.0 / (2.0 * PI), scalar2=sh, op0=ALU.mult, op1=ALU.add), S, S)
        dv(lambda e, rr=rr: e.tensor_copy(out=zi_, in_=rr), S, S)
        dv(lambda e: e.tensor_copy(out=t8, in_=zi_), S, S)
        dv(lambda e, rr=rr: e.tensor_tensor(out=rr, in0=rr, in1=t8, op=ALU.subtract), S, S)
        dv(lambda e, rr=rr: e.tensor_scalar(out=t8, in0=rr, scalar1=0.5, scalar2=None, op0=ALU.is_gt), S, S)
        dv(lambda e, rr=rr: e.tensor_tensor(out=rr, in0=rr, in1=t8, op=ALU.subtract), S, S)
        dv(lambda e, rr=rr: e.tensor_scalar(out=rr, in0=rr, scalar1=2.0 * PI, scalar2=None, op0=ALU.mult), S, S)
    tk.op('act', lambda e: e.activation(out=sth, in_=r1, func=AF.Sin), reads=S, writes=S)
    tk.op('act', lambda e: e.activation(out=cth, in_=r2, func=AF.Sin), reads=S, writes=S)
    dv(lambda e: e.tensor_tensor(out=t9, in0=rho, in1=cth, op=ALU.mult), S, S)
    dv(lambda e: e.tensor_tensor(out=t10, in0=rho, in1=sth, op=ALU.mult), S, S)
    dv(lambda e: e.tensor_scalar(out=nr, in0=t9, scalar1=-1.0, scalar2=None, op0=ALU.add), S, S)
    dv(lambda e: e.tensor_tensor(out=den, in0=a_re, in1=a_re, op=ALU.mult), S, S)
    dv(lambda e: e.tensor_tensor(out=t8, in0=a_im, in1=a_im, op=ALU.mult), S, S)
    dv(lambda e: e.tensor_tensor(out=den, in0=den, in1=t8, op=ALU.add), S, S)
    dv(lambda e: e.reciprocal(out=den, in_=den), S, S)
    dv(lambda e: e.tensor_tensor(out=fre, in0=nr, in1=a_re, op=ALU.mult), S, S)
    dv(lambda e: e.tensor_tensor(out=t8, in0=t10, in1=a_im, op=ALU.mult), S, S)
    dv(lambda e: e.tensor_tensor(out=fre, in0=fre, in1=t8, op=ALU.add), S, S)
    dv(lambda e: e.tensor_tensor(out=fre, in0=fre, in1=den, op=ALU.mult), S, S)
    dv(lambda e: e.tensor_tensor(out=fim, in0=t10, in1=a_re, op=ALU.mult), S, S)
    dv(lambda e: e.tensor_tensor(out=t8, in0=nr, in1=a_im, op=ALU.mult), S, S)
    dv(lambda e: e.tensor_tensor(out=fim, in0=fim, in1=t8, op=ALU.subtract), S, S)
    dv(lambda e: e.tensor_tensor(out=fim, in0=fim, in1=den, op=ALU.mult), S, S)
    Bre = s5bc[:, 0:512].rearrange("p (u h) -> p u h", h=16)
    Bim = s5bc[:, 512:1024].rearrange("p (u h) -> p u h", h=16)
    Cre = s5bc[:, 1024:1536].rearrange("p (u h) -> p u h", h=16)
    Cim = s5bc[:, 1536:2048].rearrange("p (u h) -> p u h", h=16)
    bbr = A.f32(512); bbi = A.f32(512); tb = A.f32(512)
    bbr3 = bbr.rearrange("p (u h) -> p u h", h=16); bbi3 = bbi.rearrange("p (u h) -> p u h", h=16)
    tb3 = tb.rearrange("p (u h) -> p u h", h=16)

    def bc16(ap):
        return ap.unsqueeze(2).broadcast_to([128, 32, 16])
    S2 = ['s5c', 's5bc']
    dv(lambda e: e.tensor_tensor(out=bbr3, in0=Bre, in1=bc16(fre), op=ALU.mult), S2, S)
    dv(lambda e: e.tensor_tensor(out=tb3, in0=Bim, in1=bc16(fim), op=ALU.mult), S2, S)
    dv(lambda e: e.tensor_tensor(out=bbr3, in0=bbr3, in1=tb3, op=ALU.subtract), S, S)
    dv(lambda e: e.tensor_tensor(out=bbi3, in0=Bim, in1=bc16(fre), op=ALU.mult), S2, S)
    dv(lambda e: e.tensor_tensor(out=tb3, in0=Bre, in1=bc16(fim), op=ALU.mult), S2, S)
    dv(lambda e: e.tensor_tensor(out=bbi3, in0=bbi3, in1=tb3, op=ALU.add), S, S)
    BZr = A.f32(1024); BZi = A.f32(1024)
    BZr3 = BZr.rearrange("p (u m) -> p u m", m=32); BZi3 = BZi.rearrange("p (u m) -> p u m", m=32)
    C0r = A.f32(1024); C0i = A.f32(1024)
    C0r3 = C0r.rearrange("p (u m) -> p u m", m=32); C0i3 = C0i.rearrange("p (u m) -> p u m", m=32)
    for z_ in (BZr, BZi, C0r, C0i):
        dv(lambda e, z_=z_: e.memset(z_, 0.0), [], S)
    for (lo, hi, c0) in ((0, 64, 0), (64, 128, 16)):
        dv(lambda e, lo=lo, hi=hi, c0=c0: e.tensor_copy(out=BZr3[lo:hi, :, c0:c0 + 16], in_=bbr3[lo:hi]), S, S)
        dv(lambda e, lo=lo, hi=hi, c0=c0: e.tensor_copy(out=BZi3[lo:hi, :, c0:c0 + 16], in_=bbi3[lo:hi]), S, S)
        dv(lambda e, lo=lo, hi=hi, c0=c0: e.tensor_copy(out=C0r3[lo:hi, :, c0:c0 + 16], in_=Cre[lo:hi]), S2, S)
        dv(lambda e, lo=lo, hi=hi, c0=c0: e.tensor_copy(out=C0i3[lo:hi, :, c0:c0 + 16], in_=Cim[lo:hi]), S2, S)
    WKr3 = WKr.rearrange("p (k u) -> p k u", u=32); WKi3 = WKi.rearrange("p (k u) -> p k u", u=32)
    dv(lambda e: e.tensor_copy(out=WKr3[:, 0, :], in_=cth), S, S)
    dv(lambda e: e.tensor_scalar(out=WKi3[:, 0, :], in0=sth, scalar1=-1.0, scalar2=None, op0=ALU.mult), S, S)
    for k in range(8):
        dv(lambda e, k=k: e.tensor_tensor(out=t8, in0=WKr3[:, k, :], in1=WKr3[:, k, :], op=ALU.mult), S, S)
        dv(lambda e, k=k: e.tensor_tensor(out=t9, in0=WKi3[:, k, :], in1=WKi3[:, k, :], op=ALU.mult), S, S)
        dv(lambda e, k=k: e.tensor_tensor(out=WKr3[:, k + 1, :], in0=t8, in1=t9, op=ALU.subtract), S, S)
        dv(lambda e, k=k: e.tensor_tensor(out=t8, in0=WKr3[:, k, :], in1=WKi3[:, k, :], op=ALU.mult), S, S)
        dv(lambda e, k=k: e.tensor_scalar(out=WKi3[:, k + 1, :], in0=t8, scalar1=2.0, scalar2=None, op0=ALU.mult), S, S)
    PWr3 = PWr.rearrange("p (k u) -> p k u", u=32); PWi3 = PWi.rearrange("p (k u) -> p k u", u=32)
    RP3 = RP.rearrange("p (k u) -> p k u", u=32)
    dv(lambda e: e.memset(PWr3[:, 0, :], 1.0), [], S)
    dv(lambda e: e.memset(PWi3[:, 0, :], 0.0), [], S)
    dv(lambda e: e.memset(RP3[:, 0, :], 1.0), [], S)
    for k in range(15):
        dv(lambda e, k=k: e.tensor_tensor(out=t8, in0=PWr3[:, k, :], in1=cth, op=ALU.mult), S, S)
        dv(lambda e, k=k: e.tensor_tensor(out=t9, in0=PWi3[:, k, :], in1=sth, op=ALU.mult), S, S)
        dv(lambda e, k=k: e.tensor_tensor(out=PWr3[:, k + 1, :], in0=t8, in1=t9, op=ALU.subtract), S, S)
        dv(lambda e, k=k: e.tensor_tensor(out=t8, in0=PWr3[:, k, :], in1=sth, op=ALU.mult), S, S)
        dv(lambda e, k=k: e.tensor_tensor(out=t9, in0=PWi3[:, k, :], in1=cth, op=ALU.mult), S, S)
        dv(lambda e, k=k: e.tensor_tensor(out=PWi3[:, k + 1, :], in0=t8, in1=t9, op=ALU.add), S, S)
    for k in range(9):
        dv(lambda e, k=k: e.tensor_tensor(out=RP3[:, k + 1, :], in0=RP3[:, k, :], in1=rho, op=ALU.mult), S, S)
    dv(lambda e: e.tensor_tensor(out=t8, in0=cth, in1=h0r, op=ALU.mult), ['s5c', 's5p'], S)
    dv(lambda e: e.tensor_tensor(out=t9, in0=sth, in1=h0i, op=ALU.mult), ['s5c', 's5p'], S)
    dv(lambda e: e.tensor_tensor(out=ini0r, in0=t8, in1=t9, op=ALU.subtract), S, S)
    dv(lambda e: e.tensor_tensor(out=t8, in0=sth, in1=h0r, op=ALU.mult), ['s5c', 's5p'], S)
    dv(lambda e: e.tensor_tensor(out=t9, in0=cth, in1=h0i, op=ALU.mult), ['s5c', 's5p'], S)
    dv(lambda e: e.tensor_tensor(out=ini0i, in0=t8, in1=t9, op=ALU.add), S, S)
    dv(lambda e: e.tensor_tensor(out=t8, in0=PWr3[:, SP - 1, :], in1=h0r, op=ALU.mult), ['s5c', 's5p'], S)
    dv(lambda e: e.tensor_tensor(out=t9, in0=PWi3[:, SP - 1, :], in1=h0i, op=ALU.mult), ['s5c', 's5p'], S)
    dv(lambda e: e.tensor_tensor(out=hm1r, in0=t8, in1=t9, op=ALU.add), S, S)
    dv(lambda e: e.tensor_tensor(out=t8, in0=PWr3[:, SP - 1, :], in1=h0i, op=ALU.mult), ['s5c', 's5p'], S)
    dv(lambda e: e.tensor_tensor(out=t9, in0=PWi3[:, SP - 1, :], in1=h0r, op=ALU.mult), ['s5c', 's5p'], S)
    dv(lambda e: e.tensor_tensor(out=hm1i, in0=t8, in1=t9, op=ALU.subtract), S, S)
    ECr_t = A.f32(32 * ECN); ECi_t = A.f32(32 * ECN)
    ECr3 = ECr_t.rearrange("p (u c) -> p u c", c=ECN); ECi3 = ECi_t.rearrange("p (u c) -> p u c", c=ECN)
    WKr3 = WKr.rearrange("p (k u) -> p k u", u=32); WKi3 = WKi.rearrange("p (k u) -> p k u", u=32)
    eq1 = A.f32(16 * ECN).rearrange("p (u c) -> p u c", c=ECN // 2)
    eq2 = A.f32(16 * ECN).rearrange("p (u c) -> p u c", c=ECN // 2)
    EK_ = ['s5c']
    dv(lambda e: e.memset(ECr3[:, :, 0:1], 1.0), [], EK_)
    dv(lambda e: e.memset(ECi3[:, :, 0:1], 0.0), [], EK_)
    for k in range(ECN.bit_length() - 1):
        n = 1 << k
        wr = WKr3[:, LP + k, :].unsqueeze(2).broadcast_to([128, 32, n])
        wi = WKi3[:, LP + k, :].unsqueeze(2).broadcast_to([128, 32, n])
        e0r, e0i = ECr3[:, :, 0:n], ECi3[:, :, 0:n]
        q1, q2 = eq1[:, :, 0:n], eq2[:, :, 0:n]
        dv(lambda e: e.tensor_tensor(out=q1, in0=e0r, in1=wr, op=ALU.mult), EK_, EK_)
        dv(lambda e: e.tensor_tensor(out=q2, in0=e0i, in1=wi, op=ALU.mult), EK_, EK_)
        dv(lambda e: e.tensor_tensor(out=ECr3[:, :, n:2 * n], in0=q1, in1=q2, op=ALU.subtract), EK_, EK_)
        dv(lambda e: e.tensor_tensor(out=q1, in0=e0r, in1=wi, op=ALU.mult), EK_, EK_)
        dv(lambda e: e.tensor_tensor(out=q2, in0=e0i, in1=wr, op=ALU.mult), EK_, EK_)
        dv(lambda e: e.tensor_tensor(out=ECi3[:, :, n:2 * n], in0=q1, in1=q2, op=ALU.add), EK_, EK_)
    tk.dma('pool', ECD[:, 0, :], ECr_t, reads=['s5c'], writes=['ECD'], skey='ecd0')
    tk.dma('pool', ECD[:, 1, :], ECi_t, reads=['s5c'], writes=['ECD'], skey='ecd1')
    stB = A.bf16(8 * SP * 256).rearrange("p (x j c m) -> p x j c m", x=8, j=SP, c=2)
    stC = A.bf16(SP * 2048).rearrange("p (t j c u m) -> p t j c u m", t=2, j=SP, c=2, u=16)
    ZA = [[A.f32(1024), A.f32(1024)] for _ in range(1)]
    zt = [eq1.rearrange("p u c -> p (u c)")[:, 0:1024], eq2.rearrange("p u c -> p (u c)")[:, 0:1024]]
    pt_ = [A.f32(1024) for _ in range(2)]
    ff = A.f32(64)
    ffr, ffi = ff[:, 0:32], ff[:, 32:64]

    def v33(ap):
        return ap.rearrange("p (u m) -> p u m", m=32)

    def pl_(fn, r, w):
        tk.op('pool', fn, reads=r, writes=w)

    def dv2(fn, r, w):
        tk.op('dve', fn, reads=r, writes=w)

    def b32(ap):
        return ap.unsqueeze(2).broadcast_to([128, 32, 32])
    for jj in range(SP):
        pr, pi_ = b32(PWr3[:, jj, :]), b32(PWi3[:, jj, :])
        zb = 0
        Zr_, Zi_ = ZA[zb]
        ZK = [f'Z{zb}']
        z0, z1 = v33(zt[0]), v33(zt[1])
        dv2(lambda e: e.tensor_tensor(out=z0, in0=BZr3, in1=pr, op=ALU.mult), S, ['zt0'])
        dv2(lambda e: e.tensor_tensor(out=z1, in0=BZi3, in1=pi_, op=ALU.mult), S, ['zt1'])
        dv2(lambda e: e.tensor_tensor(out=v33(Zr_), in0=z0, in1=z1, op=ALU.add), ['zt0', 'zt1'], ZK)
        dv2(lambda e: e.tensor_tensor(out=z0, in0=BZi3, in1=pr, op=ALU.mult), S, ['zt0'])
        dv2(lambda e: e.tensor_tensor(out=z1, in0=BZr3, in1=pi_, op=ALU.mult), S, ['zt1'])
        dv2(lambda e: e.tensor_tensor(out=v33(Zi_), in0=z0, in1=z1, op=ALU.subtract), ['zt0', 'zt1'], ZK)
        for c, Z_ in enumerate((Zr_, Zi_)):
            for q in range(4):
                for d in range(2):
                    u0 = d * 16 + 4 * q
                    pi = nps(0, 8)
                    tk.deps('pe', ZK + ['ident'], [f'ps{pi}'])
                    ins = nc.tensor.transpose(PS[pi][:, 0:128], Z_[:, u0 * 32:u0 * 32 + 128], ident)
                    tk.cnt['pe'] += 1
                    ins.then_inc(tk.sem['pe'], 1)
                    tk.commit(('sem_pe', tk.sem['pe'], tk.cnt['pe']), ZK + ['ident'], [f'ps{pi}'])
                    tk.op('act', lambda e, pi=pi, c=c, q=q, d=d: e.copy(out=stB[:, q * 2 + d, jj, c, :], in_=PS[pi][:, 0:128]),
                          reads=[f'ps{pi}'], writes=['stB'])
    for d in range(2):
        U16 = slice(d * 16, d * 16 + 16)
        for jj in range(SP):
            pr, pi_ = b32(PWr3[:, jj, :]), b32(PWi3[:, jj, :])
            z0, z1 = v33(zt[0]), v33(zt[1])
            ccf = dv2 if jj < 5 else pl_
            ccf(lambda e: e.tensor_tensor(out=ffr, in0=RP3[:, jj + 1, :], in1=PWr3[:, jj + SP, :], op=ALU.mult), S, ['ff'])
            ccf(lambda e: e.tensor_tensor(out=ffi, in0=RP3[:, jj + 1, :], in1=PWi3[:, jj + SP, :], op=ALU.mult), S, ['ff'])
            fr_, fi_ = b32(ffr), b32(ffi)
            cc_spec = ((1, fr_, fi_, ['ff'] + S, dv2, z0, z1, 'zt0', 'zt1') if jj < 5 else
                       (1, fr_, fi_, ['ff'] + S, pl_, v33(pt_[0]), v33(pt_[1]), 'pt0', 'pt1'))
            for (tsel, xr, xi, rk, fn_, t0_, t1_, tk0, tk1) in ((0, pr, pi_, S, dv2, z0, z1, 'zt0', 'zt1'), cc_spec):
                dre = stC[:, tsel, jj, 0, :, :]
                dim = stC[:, tsel, jj, 1, :, :]
                fn_(lambda e: e.tensor_tensor(out=t0_[:, U16, :], in0=C0r3[:, U16, :], in1=xr[:, U16, :], op=ALU.mult), rk, [tk0])
                fn_(lambda e: e.tensor_tensor(out=t1_[:, U16, :], in0=C0i3[:, U16, :], in1=xi[:, U16, :], op=ALU.mult), rk, [tk1])
                fn_(lambda e: e.tensor_tensor(out=dre, in0=t0_[:, U16, :], in1=t1_[:, U16, :], op=ALU.subtract), [tk0, tk1], ['stC'])
                fn_(lambda e: e.tensor_tensor(out=t0_[:, U16, :], in0=C0r3[:, U16, :], in1=xi[:, U16, :], op=ALU.mult), rk, [tk0])
                fn_(lambda e: e.tensor_tensor(out=t1_[:, U16, :], in0=C0i3[:, U16, :], in1=xr[:, U16, :], op=ALU.mult), rk, [tk1])
                fn_(lambda e: e.tensor_tensor(out=t0_[:, U16, :], in0=t0_[:, U16, :], in1=t1_[:, U16, :], op=ALU.add), [tk0, tk1], [tk0])
                fn_(lambda e: e.tensor_scalar(out=dim, in0=t0_[:, U16, :], scalar1=-1.0, scalar2=None, op0=ALU.mult), [tk0], ['stC'])
        tk.dma('pool', S5C[d], stC.rearrange("p t j c u m -> p (t j c u m)"), reads=['stC'], writes=['S5C'], skey='stC')
    for x in range(8):
        tk.dma('pool', S5B[x], stB[:, x].rearrange("p j c m -> p (j c m)"), reads=['stB'], writes=['S5B'], skey='stB')
    for nm, ap_ in (('rho', rho), ('cth', cth), ('sth', sth), ('PWr', PWr), ('PWi', PWi), ('RP', RP),
                    ('hm1r', hm1r), ('ini0r', ini0r)):
        dbg(nm, ap_, ['s5c'])
    tk.barrier()
    A.off = PERSIST

    jobs = [dict(j=0, nseq=4, T=256, TL=256), dict(j=1, nseq=1, T=4096, TL=512)]

    def rms_rstd(sq3, rstd, keyin):
        pi = nps()
        TL = rstd.shape[1]
        tk.mm([(PS[pi][:, 0:TL], ones_bf, sq3[:, k, :], dict(start=(k == 0), stop=(k == 7))) for k in range(8)],
              reads=[keyin, 'ones'], writes=[f'ps{pi}'])
        tk.op('act', lambda e: e.activation(out=rstd, in_=PS[pi][:, 0:TL], func=AF.Sqrt, bias=EPS, scale=1.0 / D),
              reads=[f'ps{pi}', 'epsc'], writes=['rstd'])
        tk.op('dve', lambda e: e.reciprocal(out=rstd, in_=rstd), reads=['rstd'], writes=['rstd'])

    for job in jobs:
        if job['j'] not in jobs_sel:
            continue
        j, nseq, T, TL = job['j'], job['nseq'], job['T'], job['TL']
        NT = nseq * T
        ntile = NT // TL
        tps = T // TL
        xTj, yTj = xT[j], yT[j]
        hnDk = hnD[j].rearrange("(k p) t -> p k t", p=128)
        yaDk = yaD[j].rearrange("(k p) t -> p k t", p=128)
        ysDk = ysD[j].rearrange("(k p) t -> p k t", p=128)
        xk = xTj.rearrange("(k p) t -> p k t", p=128)
        yk = yTj.rearrange("(k p) t -> p k t", p=128)

        A.off = PERSIST
        xs = [A.f32(8 * TL).rearrange("p (k t) -> p k t", k=8) for _ in range(2)]
        sqs = A.bf16(8 * TL).rearrange("p (k t) -> p k t", k=8)
        hno = [A.bf16(8 * TL).rearrange("p (k t) -> p k t", k=8) for _ in range(2)]
        rstd = A.f32(TL)
        tmp = [A.f32(TL) for _ in range(2)]
        for ti in range(ntile):
            t0 = ti * TL
            sl = ti % 2
            tk.dma('sp', xs[sl], xk[:, :, t0:t0 + TL], writes=[f'x{sl}'], skey=f'x{sl}')
            tk.op('act', lambda e: e.activation(out=sqs, in_=xs[sl], func=AF.Square), reads=[f'x{sl}'], writes=['sqs'])
            rms_rstd(sqs, rstd, 'sqs')
            for k in range(8):
                tm = tmp[k % 2]
                tk.op('dve', lambda e: e.scalar_tensor_tensor(out=tm, in0=xs[sl][:, k, :], scalar=cj(cA1, k, j), in1=rstd,
                                                              op0=ALU.mult, op1=ALU.mult),
                      reads=[f'x{sl}', 'rstd'], writes=[f'tmp{k % 2}'])
                tk.op('act', lambda e: e.activation(out=hno[sl][:, k, :], in_=tm, func=AF.Identity, bias=cj(cB1, k, j), scale=1.0),
                      reads=[f'tmp{k % 2}'], writes=[f'hno{sl}'])
            tk.dma('pool', hnDk[:, :, t0:t0 + TL], hno[sl], reads=[f'hno{sl}'], writes=['hnD'], skey=f'hno{sl}')
        tk.barrier()

        A.off = PERSIST
        G = 1024
        ngrp = NT // G
        SEG = min(T, G)
        spg = G // SEG
        xa_pad = A.f32(nseq * (T + 3)).rearrange("p (s t) -> p s t", s=nseq)
        u_f = A.f32(NT)
        u_b = A.bf16(NT)
        hf = A.f32(NT)
        u3 = u_f.rearrange("p (s t) -> p s t", s=nseq)
        wxa = [A.bf16(8 * 128).rearrange("p (k n) -> p k n", k=8) for _ in range(2)]
        wga = [A.bf16(8 * 128).rearrange("p (k n) -> p k n", k=8) for _ in range(2)]
        hnl = [A.bf16(8 * 512).rearrange("p (k t) -> p k t", k=8) for _ in range(3)]
        NB = 2
        tr = [[A.f32(G) for _ in range(NB)] for _ in range(5)]
        cb = A.f32(8)
        cbc = [0]
        yao = [A.bf16(G) for _ in range(2)]
        lruw3 = lruw.rearrange("p (i m) -> p i m", m=128)
        hnc = [0]

        def load_hn(t0):
            sl = hnc[0] % 3
            hnc[0] += 1
            tk.dma('sp', hnl[sl], hnDk[:, :, t0:t0 + 512], writes=[f'hnl{sl}'], skey=f'hnl{sl}')
            return sl
        tk.op('dve', lambda e: e.memset(xa_pad, 0.0), writes=['xa_pad'])
        it = [0]
        psR, psI = PS2[0][:, :], PS2[1][:, :]
        def conv_grp(g, q):
            c0 = g * 1024
            uo = u_f[:, c0:c0 + 1024]
            tk.op('dve', lambda e: e.tensor_scalar(out=uo, in0=xa_pad[:, 0, c0:c0 + 1024], scalar1=col(vecs, 80 + q), scalar2=col(vecs, 112 + q),
                                                   op0=ALU.mult, op1=ALU.add), reads=['xa_pad', 'vecs'], writes=['u_f'])
            for tap in range(1, 4):
                tk.op('dve', lambda e: e.scalar_tensor_tensor(out=uo, in0=xa_pad[:, 0, c0 + tap:c0 + tap + 1024], scalar=col(vecs, 80 + 8 * tap + q),
                                                              in1=uo, op0=ALU.mult, op1=ALU.add), reads=['xa_pad', 'u_f'], writes=['u_f'])
            tk.op('act', lambda e: e.copy(out=u_b[:, c0:c0 + 1024], in_=uo), reads=['u_f'], writes=['u_b'])

        for q in range(8 if 2 in phases else 0):
            ws = q % 2
            tk.dma('pool', wxa[ws], kp(w_in, q * 128, q * 128 + 128), writes=[f'wxa{ws}'], skey=f'wxa{ws}')
            tk.dma('pool', wga[ws], kp(w_in, 1024 + q * 128, 1024 + q * 128 + 128), writes=[f'wga{ws}'], skey=f'wga{ws}')
            for ti in range(NT // 512):
                t0 = ti * 512
                sl = load_hn(t0)
                pi = nps(4, 8)
                tk.mm([(PS[pi], wxa[ws][:, k, :], hnl[sl][:, k, :], dict(start=(k == 0), stop=(k == 7))) for k in range(8)],
                      reads=[f'wxa{ws}', f'hnl{sl}'], writes=[f'ps{pi}'])
                if T >= 512:
                    s_, tt = t0 // T, t0 % T
                    o_, i_ = xa_pad[:, s_, 2 + tt:2 + tt + 512], PS[pi]
                else:
                    ns_ = 512 // T
                    s_ = t0 // T
                    o_, i_ = xa_pad[:, s_:s_ + ns_, 2:2 + T], PS[pi].rearrange("p (s t) -> p s t", s=ns_)
                tk.op('dve', lambda e: e.tensor_copy(out=o_, in_=i_), reads=[f'ps{pi}'], writes=['xa_pad'])
                if T >= 2048 and ti >= 2 and ti % 2 == 0:
                    conv_grp((ti - 2) // 2, q)
            if T >= 2048:
                conv_grp(NT // 1024 - 1, q)
            else:
                tk.op('dve', lambda e: e.tensor_scalar(out=u3, in0=xa_pad[:, :, 0:T], scalar1=col(vecs, 80 + q), scalar2=col(vecs, 112 + q),
                                                       op0=ALU.mult, op1=ALU.add), reads=['xa_pad', 'vecs'], writes=['u_f'])
                for tap in range(1, 4):
                    tk.op('dve', lambda e: e.scalar_tensor_tensor(out=u3, in0=xa_pad[:, :, tap:tap + T], scalar=col(vecs, 80 + 8 * tap + q),
                                                                  in1=u3, op0=ALU.mult, op1=ALU.add), reads=['xa_pad', 'u_f'], writes=['u_f'])
                tk.op('act', lambda e: e.copy(out=u_b, in_=u_f), reads=['u_f'], writes=['u_b'])
            for d in range(2):
                carry = (col(h0l, d * 8 + q) if j == 1 else ZERO)
                ckey = 'h0l' if j == 1 else 'epsc'
                order = list(range(ngrp)) if d == 0 else list(range(ngrp - 1, -1, -1))
                for r0_ in range(0, len(order), 2):
                    rnd = order[r0_:r0_ + 2]
                    ctxs = []
                    for g in rnd:
                        g0 = g * G
                        b_ = it[0] % NB
                        it[0] += 1
                        r_, i_, a_, a2_, iu_ = [tr[x][b_] for x in range(5)]
                        ctxs.append((g, g0, b_, r_, i_, a_, a2_, iu_))
                        K = lambda n, b_=b_: f'{n}{b_}'
                        for h in range(2):
                            ub = u_b[:, g0 + h * 512:g0 + (h + 1) * 512]
                            tk.mm([(psR[:, h * 512:(h + 1) * 512], lruw3[:, (d * 2 + 0) * 8 + q, :], ub, dict(start=True, stop=True))],
                                  reads=['lruw', 'u_b'], writes=['psR'])
                            tk.mm([(psI[:, h * 512:(h + 1) * 512], lruw3[:, (d * 2 + 1) * 8 + q, :], ub, dict(start=True, stop=True))],
                                  reads=['lruw', 'u_b'], writes=['psI'])
                        tk.op('act', lambda e: e.activation(out=r_, in_=psR, func=AF.Sigmoid, bias=col(vecs, 120 + d * 8 + q), scale=1.0),
                              reads=['psR'], writes=[K('r')])
                        tk.op('act', lambda e: e.activation(out=i_, in_=psI, func=AF.Sigmoid, bias=col(vecs, 136 + d * 8 + q), scale=1.0),
                              reads=['psI'], writes=[K('i')])
                    for (g, g0, b_, r_, i_, a_, a2_, iu_) in ctxs:
                        K = lambda n, b_=b_: f'{n}{b_}'
                        tk.op('act', lambda e: e.activation(out=a_, in_=r_, func=AF.Exp, scale=col(kco, d * 8 + q)),
                              reads=[K('r'), 'kco'], writes=[K('a')])
                        tk.op('dve', lambda e: e.tensor_tensor(out=a2_, in0=a_, in1=a_, op=ALU.mult), reads=[K('a')], writes=[K('a2')])
                        tk.op('pool', lambda e: e.tensor_tensor(out=iu_, in0=i_, in1=u_f[:, g0:g0 + G], op=ALU.mult),
                              reads=[K('i'), 'u_f'], writes=[K('iu')])
                    for (g, g0, b_, r_, i_, a_, a2_, iu_) in ctxs:
                        K = lambda n, b_=b_: f'{n}{b_}'
                        tk.op('act', lambda e: e.activation(out=a2_, in_=a2_, func=AF.Sqrt, bias=ONE, scale=-1.0),
                              reads=[K('a2'), 'epsc'], writes=[K('a2')])
                        tk.op('dve', lambda e: e.tensor_tensor(out=iu_, in0=a2_, in1=iu_, op=ALU.mult), reads=[K('a2'), K('iu')], writes=[K('iu')])
                    gel = []
                    for (g, g0, b_, r_, i_, a_, a2_, iu_) in ctxs:
                        K = lambda n, b_=b_: f'{n}{b_}'
                        bb_, hb_ = iu_, r_
                        for ss in range(spg):
                            lo = ss * SEG
                            gs = g0 + lo
                            sq_ = gs // T
                            if j == 0:
                                carry, ckey = ZERO, 'epsc'
                            if d == 0:
                                tk.op('dve', lambda e: e.tensor_tensor_scan(out=hf[:, gs:gs + SEG], data0=a_[:, lo:lo + SEG], data1=bb_[:, lo:lo + SEG],
                                                                             initial=carry, op0=ALU.mult, op1=ALU.add),
                                      reads=[K('a'), K('iu'), ckey], writes=['hf'])
                                carry, ckey = hf[:, gs + SEG - 1:gs + SEG], 'hf'
                                if j == 0:
                                    tk.op('pool', lambda e: e.tensor_copy(out=col(finl, (sq_ * 2 + 0) * 8 + q), in_=carry), reads=['hf'], writes=['finl'])
                            else:
                                tk.op('dve', lambda e: e.tensor_tensor_scan(out=hb_[:, lo:lo + SEG][:, ::-1], data0=a_[:, lo:lo + SEG][:, ::-1],
                                                                             data1=bb_[:, lo:lo + SEG][:, ::-1], initial=carry, op0=ALU.mult, op1=ALU.add),
                                      reads=[K('a'), K('iu'), ckey], writes=[K('r')])
                                cbi = cbc[0] % 8
                                cbc[0] += 1
                                tk.op('dve', lambda e: e.tensor_copy(out=cb[:, cbi:cbi + 1], in_=hb_[:, lo:lo + 1]), reads=[K('r')], writes=['cb'])
                                carry, ckey = cb[:, cbi:cbi + 1], 'cb'
                                if j == 0:
                                    tk.op('pool', lambda e: e.tensor_copy(out=col(finl, (sq_ * 2 + 1) * 8 + q), in_=carry), reads=[ckey], writes=['finl'])
                        if d == 1:
                            tk.op('dve', lambda e: e.tensor_tensor(out=i_, in0=hb_, in1=hf[:, g0:g0 + G], op=ALU.add),
                                  reads=[K('r'), 'hf'], writes=[K('i')])
                            pgi = b_
                            pg2 = PS2[2 + pgi][:, :]
                            pgb = [f'ps{4 + 2 * pgi}', f'ps{5 + 2 * pgi}']
                            for h in range(2):
                                sl = load_hn(g0 + h * 512)
                                tk.mm([(pg2[:, h * 512:(h + 1) * 512], wga[ws][:, k, :], hnl[sl][:, k, :], dict(start=(k == 0), stop=(k == 7))) for k in range(8)],
                                      reads=[f'wga{ws}', f'hnl{sl}'], writes=pgb)
                            gel.append((g0, b_, i_, a_, pg2, pgb))
                    for (g0, b_, i_, a_, pg2, pgb) in gel:
                        K = lambda n, b_=b_: f'{n}{b_}'
                        tk.op('act', lambda e: e.activation(out=a_, in_=pg2, func=AF.Gelu_apprx_tanh), reads=pgb + [K('a')], writes=[K('a')])
                        tk.op('dve', lambda e: e.tensor_tensor(out=yao[b_], in0=a_, in1=i_, op=ALU.mult),
                              reads=[K('a'), K('i')], writes=[f'yao{b_}'])
                        tk.dma('pool', yaD[j][q * 128:(q + 1) * 128, g0:g0 + G], yao[b_], reads=[f'yao{b_}'], writes=['yaD'], skey=f'yao{b_}')
        tk.barrier()

        A.off = PERSIST
        nch = TL // SP
        sg = [A.f32(TL) for _ in range(2)]
        vso = [A.bf16(TL) for _ in range(2)]
        P3MARK = A.off
        us_f = A.f32(NT)
        hs_f = A.f32(NT)
        usb = [A.bf16(NT) for _ in range(2)]
        wus = A.bf16(8 * 128).rearrange("p (k n) -> p k n", k=8)
        WB = SP * 256
        ECs2 = [A.f32(2 * 4 * ECN).rearrange("p (c u n) -> p c u n", c=2, u=4) for _ in range(2)]
        wS2 = [A.bf16(3 * WB) for _ in range(2)]
        Dt2 = [A.f32(4 * TL).rearrange("p (u t) -> p u t", u=4) for _ in range(2)]
        gR = A.f32(4 * TL).rearrange("p (u t) -> p u t", u=4)
        gI = A.f32(4 * TL).rearrange("p (u t) -> p u t", u=4)
        hnl = [gR.rearrange("p u t -> p (u t)").bitcast(BF16)[:, 0:8 * TL].rearrange("p (k t) -> p k t", k=8),
               gI.rearrange("p u t -> p (u t)").bitcast(BF16)[:, 0:8 * TL].rearrange("p (k t) -> p k t", k=8)]
        GA = [[f'gR{pl}' for pl in range(4)], [f'gI{pl}' for pl in range(4)]]
        gRb2 = [A.bf16(4 * TL).rearrange("p (u t) -> p u t", u=4) for _ in range(3)]
        gIb2 = [A.bf16(4 * TL).rearrange("p (u t) -> p u t", u=4) for _ in range(3)]

        def c4(n=nch):
            return A.f32(4 * n).rearrange("p (u c) -> p u c", u=4)
        X1, X2, Xr, Xi, Kr, Ki = [c4() for _ in range(6)]
        Lr2 = [c4() for _ in range(3)]
        Li2 = [c4() for _ in range(3)]
        HBr = [c4(nch + 1) for _ in range(2)]
        HBi = [c4(nch + 1) for _ in range(2)]
        HBrb2 = [A.bf16(4 * nch).rearrange("p (u c) -> p u c", u=4) for _ in range(2)]
        HBib2 = [A.bf16(4 * nch).rearrange("p (u c) -> p u c", u=4) for _ in range(2)]
        Km1r = [A.f32(8)[:, 0:4] for _ in range(2)]
        Km1i = [A.f32(8)[:, 0:4] for _ in range(2)]
        tq4 = [A.f32(8)[:, 0:4] for _ in range(2)]
        PWr3 = PWr.rearrange("p (k u) -> p k u", u=32); PWi3 = PWi.rearrange("p (k u) -> p k u", u=32)
        RP3 = RP.rearrange("p (k u) -> p k u", u=32)
        CK = ['chunk']

        def pv(fn, r, w):
            tk.op('pool', fn, reads=r, writes=w)
        tcnt = [0]
        for q in range(4):
            tk.dma('pool', wus, kp(w_in, 2048 + q * 128, 2048 + q * 128 + 128), writes=['wus'], skey='wus')
            for ti in range(ntile):
                hs_ = ti % 2
                tk.dma('sp', hnl[hs_], hnDk[:, :, ti * TL:(ti + 1) * TL], writes=[f'hnl{hs_}'] + GA[hs_], skey=f'hnl{hs_}')
                pi = nps(0, 2)
                tk.mm([(PS[pi][:, 0:TL], wus[:, k, :], hnl[hs_][:, k, :], dict(start=(k == 0), stop=(k == 7))) for k in range(8)],
                      reads=['wus', f'hnl{hs_}'], writes=[f'ps{pi}'])
                if j == 1:
                    r0 = ti * 8
                    o_ = us_f.rearrange("p (c r) -> p r c", r=64)[:, r0:r0 + 8, :]
                    i_ = PS[pi][:, 0:512].rearrange("p (r c) -> p r c", c=64)
                else:
                    o_ = us_f[:, ti * TL:(ti + 1) * TL]
                    i_ = PS[pi][:, 0:TL]
                tk.op('act', lambda e: e.copy(out=o_, in_=i_), reads=[f'ps{pi}'], writes=['us_f'])
            us3 = us_f.rearrange("p (s t) -> p s t", s=nseq)
            if q == 0:
                dbg(f'usf{j}', us_f, ['us_f'])
            tk.op('act', lambda e: e.copy(out=usb[0], in_=us_f), reads=['us_f'], writes=['usb0'])
            tk.op('pool', lambda e: e.tensor_copy(out=usb[1].rearrange("p (s t) -> p s t", s=nseq), in_=us3[:, :, ::-1]),
                  reads=['us_f'], writes=['usb1'])
            res = {}
            for d in range(2):
                u0 = d * 16 + 4 * q
                U4 = slice(u0, u0 + 4)
                wS, Dt, ECs = wS2[d], Dt2[d], ECs2[d]
                w_BT = wS[:, 0:WB].rearrange("p (j c m) -> p j c m", j=SP, c=2)
                w_CL = wS[:, WB:2 * WB].rearrange("p (j c u m) -> p j c u m", j=SP, c=2, u=4)
                w_CC = wS[:, 2 * WB:3 * WB].rearrange("p (j c u m) -> p j c u m", j=SP, c=2, u=4)
                tk.dma('sp', wS[:, 0:WB], S5B[q * 2 + d], writes=[f'wS{d}'], skey=f'wSb{d}')
                tk.dma('sp', wS[:, WB:3 * WB].rearrange("p (x u m) -> p x u m", u=4, m=32),
                       S5C[d].rearrange("p (x u m) -> p x u m", u=16, m=32)[:, :, 4 * q:4 * q + 4, :], writes=[f'wS{d}'], skey=f'wSc{d}')
                tk.op('dve', lambda e: e.tensor_copy(out=Dt, in_=rho[:, U4].unsqueeze(2).broadcast_to([128, 4, TL])), reads=['s5c'], writes=[f'Dt{d}'])
                tk.op('dve', lambda e: e.memset(Dt[:, :, 0::SP], 0.0), writes=[f'Dt{d}'])
                tk.dma('sp', ECs, ECD.rearrange("p c (u n) -> p c u n", n=ECN)[:, :, u0:u0 + 4, :], writes=[f'ECs{d}'], skey=f'ecs{d}')
                res[d] = (u0, U4, ECs[:, 0, :, 0:nch], ECs[:, 1, :, 0:nch], RP3[:, SP, U4], w_BT, w_CL, w_CC, Dt)
            if True:
                items = [(d_, s_, tq_) for d_ in range(2) for s_ in range(nseq) for tq_ in range(tps)]
                ctx = {}

                bank_ctx = {}

                def stage1_mm(it_, pls):
                    d, s, tq = it_
                    u0, U4, ecr, eci, r8b, w_BT, w_CL, w_CC, Dt = res[d]
                    t0 = (s * tps + tq) * TL
                    if it_ not in ctx:
                        ctx[it_] = tcnt[0] % 3
                        tcnt[0] += 1
                    for pl in pls:
                        pa, pb = nps(2, 6), nps(2, 6)
                        bank_ctx[(it_, pl)] = (pa, pb)
                        rows = slice(32 * pl, 32 * pl + 32)
                        for c, pp in ((0, pa), (1, pb)):
                            tk.mm([(PS[pp][:, jj:TL:SP], w_BT[rows, jj, c, :], usb[d][rows, t0 + jj:t0 + TL:SP],
                                    dict(start=True, stop=True, tile_position=(32 * pl, 0))) for jj in range(SP)],
                                  reads=[f'wS{d}', f'usb{d}'], writes=[f'ps{pp}'])

                def stage1_scan(it_, pls, last):
                    d, s, tq = it_
                    u0, U4, ecr, eci, r8b, w_BT, w_CL, w_CC, Dt = res[d]
                    gb = ctx[it_]
                    gRb, gIb = gRb2[gb], gIb2[gb]
                    for pl in pls:
                        pa, pb = bank_ctx[(it_, pl)]
                        tk.op('dve', lambda e: e.tensor_tensor_scan(out=gR[:, pl, :], data0=Dt[:, pl, :], data1=PS[pa][:, 0:TL], initial=0.0,
                                                                     op0=ALU.mult, op1=ALU.add), reads=[f'Dt{d}', f'ps{pa}'], writes=[f'gR{pl}', 'hnl0'])
                        tk.op('dve', lambda e: e.tensor_tensor_scan(out=gI[:, pl, :], data0=Dt[:, pl, :], data1=PS[pb][:, 0:TL], initial=0.0,
                                                                     op0=ALU.mult, op1=ALU.add), reads=[f'Dt{d}', f'ps{pb}'], writes=[f'gI{pl}', 'hnl1'])
                        tk.op('act', lambda e: e.copy(out=gRb[:, pl, :], in_=gR[:, pl, :]), reads=[f'gR{pl}'], writes=[f'gRb{gb}_{pl}'])
                        tk.op('act', lambda e: e.copy(out=gIb[:, pl, :], in_=gI[:, pl, :]), reads=[f'gI{pl}'], writes=[f'gIb{gb}_{pl}'])

                    if last:
                        tk.op('act', lambda e: e.copy(out=Lr2[gb], in_=gR[:, :, SP - 1::SP]), reads=[f'gR{pl}' for pl in range(4)], writes=[f'L{gb}'])
                        tk.op('act', lambda e: e.copy(out=Li2[gb], in_=gI[:, :, SP - 1::SP]), reads=[f'gI{pl}' for pl in range(4)], writes=[f'L{gb}'])

                def stage1(it_):
                    stage1_mm(it_, (0, 1)); stage1_scan(it_, (0, 1), False)
                    stage1_mm(it_, (2, 3)); stage1_scan(it_, (2, 3), True)

                def stage2(it_):
                    d, s, tq = it_
                    u0, U4, ecr, eci, r8b, w_BT, w_CL, w_CC, Dt = res[d]
                    ti = s * tps + tq
                    t0 = ti * TL
                    hb = tq % 2
                    Hr_, Hi_ = HBr[hb], HBi[hb]
                    gb = ctx[it_]
                    gRb, gIb, HBrb, HBib = gRb2[gb], gIb2[gb], HBrb2[gb % 2], HBib2[gb % 2]
                    if tq == 0:
                        if j == 1:
                            dv(lambda e: e.tensor_copy(out=Hr_[:, :, 0], in_=hm1r[:, U4]), ['s5c'], CK)
                            dv(lambda e: e.tensor_copy(out=Hi_[:, :, 0], in_=hm1i[:, U4]), ['s5c'], CK)
                            dv(lambda e: e.tensor_copy(out=Km1r[hb], in_=ini0r[:, U4]), ['s5c'], CK)
                            dv(lambda e: e.tensor_copy(out=Km1i[hb], in_=ini0i[:, U4]), ['s5c'], CK)
                        else:
                            for z_ in (Hr_[:, :, 0], Hi_[:, :, 0], Km1r[hb], Km1i[hb]):
                                dv(lambda e, z_=z_: e.memset(z_, 0.0), [], CK)

                    py = nps(6, 8)
                    GK = [f'L{gb}']
                    Lr, Li = Lr2[gb], Li2[gb]
                    pv(lambda e: e.tensor_tensor(out=X1, in0=ecr, in1=Lr, op=ALU.mult), GK + ['s5c', f'ECs{d}'], CK)
                    pv(lambda e: e.tensor_tensor(out=X2, in0=eci, in1=Li, op=ALU.mult), GK + ['s5c', f'ECs{d}'], CK)
                    pv(lambda e: e.tensor_tensor(out=Xr, in0=X1, in1=X2, op=ALU.subtract), CK, CK)
                    pv(lambda e: e.tensor_tensor(out=X1, in0=ecr, in1=Li, op=ALU.mult), GK + ['s5c', f'ECs{d}'], CK)
                    pv(lambda e: e.tensor_tensor(out=X2, in0=eci, in1=Lr, op=ALU.mult), GK + ['s5c', f'ECs{d}'], CK)
                    pv(lambda e: e.tensor_tensor(out=Xi, in0=X1, in1=X2, op=ALU.add), CK, CK)
                    for pl in range(4):
                        r8 = r8b[:, pl:pl + 1].broadcast_to([128, nch])
                        dv(lambda e: e.tensor_tensor_scan(out=Kr[:, pl, :], data0=r8, data1=Xr[:, pl, :], initial=Km1r[hb][:, pl:pl + 1],
                                                          op0=ALU.mult, op1=ALU.add), CK + ['s5c'], CK)
                        dv(lambda e: e.tensor_tensor_scan(out=Ki[:, pl, :], data0=r8, data1=Xi[:, pl, :], initial=Km1i[hb][:, pl:pl + 1],
                                                          op0=ALU.mult, op1=ALU.add), CK + ['s5c'], CK)
                    pv(lambda e: e.tensor_tensor(out=X1, in0=ecr, in1=Kr, op=ALU.mult), CK, CK)
                    pv(lambda e: e.tensor_tensor(out=X2, in0=eci, in1=Ki, op=ALU.mult), CK, CK)
                    pv(lambda e: e.tensor_tensor(out=Hr_[:, :, 1:nch + 1], in0=X1, in1=X2, op=ALU.add), CK, CK + [f'HB{hb}'])
                    pv(lambda e: e.tensor_tensor(out=X1, in0=ecr, in1=Ki, op=ALU.mult), CK, CK)
                    pv(lambda e: e.tensor_tensor(out=X2, in0=eci, in1=Kr, op=ALU.mult), CK, CK)
                    pv(lambda e: e.tensor_tensor(out=Hi_[:, :, 1:nch + 1], in0=X1, in1=X2, op=ALU.subtract), CK, CK + [f'HB{hb}'])
                    tk.op('act', lambda e: e.copy(out=HBrb, in_=Hr_[:, :, 0:nch]), reads=CK + [f'HB{hb}'], writes=[f'HBb{gb % 2}'])
                    tk.op('act', lambda e: e.copy(out=HBib, in_=Hi_[:, :, 0:nch]), reads=CK + [f'HB{hb}'], writes=[f'HBb{gb % 2}'])
                    hlr, hli = Hr_[:, :, nch], Hi_[:, :, nch]
                    if tq < tps - 1:
                        nb_ = (tq + 1) % 2
                        o_r, o_i = PWr3[:, SP, U4], PWi3[:, SP, U4]
                        pv(lambda e: e.tensor_copy(out=HBr[nb_][:, :, 0], in_=hlr), CK, CK + [f'HB{nb_}'])
                        pv(lambda e: e.tensor_copy(out=HBi[nb_][:, :, 0], in_=hli), CK, CK + [f'HB{nb_}'])
                        pv(lambda e: e.tensor_tensor(out=tq4[0], in0=o_r, in1=hlr, op=ALU.mult), CK + ['s5c'], CK)
                        pv(lambda e: e.tensor_tensor(out=tq4[1], in0=o_i, in1=hli, op=ALU.mult), CK + ['s5c'], CK)
                        pv(lambda e: e.tensor_tensor(out=Km1r[nb_], in0=tq4[0], in1=tq4[1], op=ALU.subtract), CK, CK)
                        pv(lambda e: e.tensor_tensor(out=tq4[0], in0=o_r, in1=hli, op=ALU.mult), CK + ['s5c'], CK)
                        pv(lambda e: e.tensor_tensor(out=tq4[1], in0=o_i, in1=hlr, op=ALU.mult), CK + ['s5c'], CK)
                        pv(lambda e: e.tensor_tensor(out=Km1i[nb_], in0=tq4[0], in1=tq4[1], op=ALU.add), CK, CK)
                    elif j == 0:
                        p7r, p7i = PWr3[:, SP - 1, U4], PWi3[:, SP - 1, U4]
                        fo_r = fins[:, ((0 * 4 + s) * 2 + d) * 16 + 4 * q:((0 * 4 + s) * 2 + d) * 16 + 4 * q + 4]
                        fo_i = fins[:, ((1 * 4 + s) * 2 + d) * 16 + 4 * q:((1 * 4 + s) * 2 + d) * 16 + 4 * q + 4]
                        pv(lambda e: e.tensor_tensor(out=tq4[0], in0=p7r, in1=hlr, op=ALU.mult), CK + ['s5c'], CK)
                        pv(lambda e: e.tensor_tensor(out=tq4[1], in0=p7i, in1=hli, op=ALU.mult), CK + ['s5c'], CK)
                        pv(lambda e: e.tensor_tensor(out=fo_r, in0=tq4[0], in1=tq4[1], op=ALU.subtract), CK, ['fins'])
                        pv(lambda e: e.tensor_tensor(out=tq4[0], in0=p7r, in1=hli, op=ALU.mult), CK + ['s5c'], CK)
                        pv(lambda e: e.tensor_tensor(out=tq4[1], in0=p7i, in1=hlr, op=ALU.mult), CK + ['s5c'], CK)
                        pv(lambda e: e.tensor_tensor(out=fo_i, in0=tq4[0], in1=tq4[1], op=ALU.add), CK, ['fins'])
                    def cmm():
                      for pl in range(4):
                        yq = PS[py][32 * pl:32 * pl + 32, :]
                        mms = []
                        for jj in range(SP):
                            tp_ = (0, 32 * pl)
                            mms.append((yq[:, jj:TL:SP], w_CL[:, jj, 0, pl, :], gRb[:, pl, jj:TL:SP], dict(start=True, stop=False, tile_position=tp_)))
                            mms.append((yq[:, jj:TL:SP], w_CL[:, jj, 1, pl, :], gIb[:, pl, jj:TL:SP], dict(start=False, stop=False, tile_position=tp_)))
                            mms.append((yq[:, jj:TL:SP], w_CC[:, jj, 0, pl, :], HBrb[:, pl, :], dict(start=False, stop=False, tile_position=tp_)))
                            mms.append((yq[:, jj:TL:SP], w_CC[:, jj, 1, pl, :], HBib[:, pl, :], dict(start=False, stop=True, tile_position=tp_)))
                        tk.mm(mms, reads=[f'wS{d}', f'HBb{gb % 2}', f'gRb{gb}_{pl}', f'gIb{gb}_{pl}'], writes=[f'ps{py}'])
                    def evac():
                        if d == 0:
                            tk.op('act', lambda e: e.copy(out=hs_f[:, t0:t0 + TL], in_=PS[py][:, 0:TL]), reads=[f'ps{py}'], writes=['hs_f'])
                        else:
                            lo = s * T + (T - tq * TL - TL)
                            hv = hs_f[:, lo:lo + TL][:, ::-1]
                            tk.op('dve', lambda e: e.tensor_tensor(out=hv, in0=hv, in1=PS[py][:, 0:TL], op=ALU.add),
                                  reads=[f'ps{py}', 'hs_f'], writes=['hs_f'])
                    return cmm, evac

                for it_ in items[0:2]:
                    stage1(it_)
                pend = None
                for ii_, it_ in enumerate(items):
                    nxt = items[ii_ + 2] if ii_ + 2 < len(items) else None
                    if nxt is not None:
                        stage1_mm(nxt, (0, 1))
                    cmm_, ev_ = stage2(it_)
                    if nxt is not None:
                        stage1_scan(nxt, (0, 1), False)
                        stage1_mm(nxt, (2, 3))
                        stage1_scan(nxt, (2, 3), True)
                    cmm_()
                    if pend is not None:
                        pend()
                    pend = ev_
                pend()
            if q == 0:
                dbg(f'hsf{j}', hs_f, ['hs_f'])
            for ti in range(ntile):
                t0 = ti * TL
                b_ = ti % 2
                tk.op('dve', lambda e: e.scalar_tensor_tensor(out=sg[b_], in0=us_f[:, t0:t0 + TL], scalar=col(vecs, 184 + q), in1=hs_f[:, t0:t0 + TL],
                                                              op0=ALU.mult, op1=ALU.add), reads=['us_f', 'hs_f'], writes=[f'sg{b_}'])
                tk.op('act', lambda e: e.activation(out=vso[b_], in_=sg[b_], func=AF.Gelu_apprx_tanh),
                      reads=[f'sg{b_}'], writes=[f'vso{b_}'])
                tk.dma('pool', vsD[j][q * 128:(q + 1) * 128, t0:t0 + TL], vso[b_], reads=[f'vso{b_}'], writes=['vsD'], skey=f'vso{b_}')
        tk.barrier()
        A.off = P3MARK
        wgl = A.bf16(4 * 512).rearrange("p (k n) -> p k n", k=4)
        ysn = A.bf16(4 * NT).rearrange("p (c t) -> p c t", c=4)
        vst = [A.bf16(4 * TL).rearrange("p (c t) -> p c t", c=4) for _ in range(2)]
        vsDk = vsD[j].rearrange("(k p) t -> p k t", p=128)
        tk.dma('pool', wgl, w_glu.rearrange("(k p) n -> p k n", p=128), writes=['wgl'], skey='wgl')
        for ti in range(ntile):
            t0 = ti * TL
            vb_ = ti % 2
            vt = vst[vb_]
            tk.dma('sp', vt, vsDk[:, :, t0:t0 + TL], writes=[f'vst{vb_}'], skey=f'vst{vb_}')
            for oc in range(4):
                pi = nps(0, 6)
                b_ = (ti * 4 + oc) % 2
                tk.mm([(PS[pi][:, 0:TL], wgl[:, k, oc * 128:(oc + 1) * 128], vt[:, k, :], dict(start=(k == 0), stop=(k == 3))) for k in range(4)],
                      reads=['wgl', f'vst{vb_}'], writes=[f'ps{pi}'])
                tk.op('act', lambda e: e.activation(out=sg[b_], in_=PS[pi][:, 0:TL], func=AF.Sigmoid, bias=col(vecs, 188 + oc), scale=1.0),
                      reads=[f'ps{pi}'], writes=[f'sg{b_}'])
                if j == 1:
                    c0 = ti * 8
                    o_ = ysn[:, oc, :].rearrange("p (r c) -> p c r", c=64)[:, c0:c0 + 8, :]
                    a_ = vt[:, oc, :].rearrange("p (c r) -> p c r", r=64)
                    g_ = sg[b_].rearrange("p (c r) -> p c r", r=64)
                else:
                    o_, a_, g_ = ysn[:, oc, t0:t0 + TL], vt[:, oc, :], sg[b_]
                tk.op('dve', lambda e: e.tensor_tensor(out=o_, in0=a_, in1=g_, op=ALU.mult), reads=[f'vst{vb_}', f'sg{b_}'], writes=['ysn'])
        for oc in range(4):
            tk.dma('pool', ysD[j][oc * 128:(oc + 1) * 128, :], ysn[:, oc, :], reads=['ysn'], writes=['ysD'], skey='ysn')
        tk.barrier()

        A.off = PERSIST
        TL = 512
        ntile = NT // TL
        NSL = 4
        ring = [A.bf16(8192) for _ in range(NSL)]
        rc = [0]

        def wload(src_ap, shape_k, ncol):
            sl = rc[0] % NSL
            rc[0] += 1
            v = ring[sl][:, 0:shape_k * ncol].rearrange("p (k n) -> p k n", k=shape_k)
            tk.dma('sp', v, src_ap, writes=[f'ring{sl}'], skey=f'ring{sl}')
            return v, f'ring{sl}'
        xt = A.f32(8 * TL).rearrange("p (k t) -> p k t", k=8)
        R1 = A.bf16(22 * TL)
        hnt = R1[:, 0:8 * TL].rearrange("p (k t) -> p k t", k=8)
        yat = R1[:, 8 * TL:16 * TL].rearrange("p (k t) -> p k t", k=8)
        yst = R1[:, 16 * TL:20 * TL].rearrange("p (k t) -> p k t", k=4)
        hmid = R1.rearrange("p (k t) -> p k t", k=22)
        mt = A.bf16(8 * TL).rearrange("p (k t) -> p k t", k=8)
        mo = A.f32(8 * TL).rearrange("p (k t) -> p k t", k=8)
        sqs = A.bf16(8 * TL).rearrange("p (k t) -> p k t", k=8)
        rstd = A.f32(TL)
        tm4 = [A.f32(TL) for _ in range(4)]
        R1K = ['hnt', 'yat', 'yst', 'hmid']
        for ti in range(ntile if 4 in phases else 0):
            t0 = ti * TL
            wg0, kg0 = wload(kp(WG, 0, 1024), 8, 1024)
            wg1, kg1 = wload(kp(WG, 1024, 2048), 8, 1024)
            wpl, kpl = wload(kp(WPL, 0, 1024), 8, 1024)
            wps, kps = wload(WPS.rearrange("(k p) n -> p k n", p=128), 4, 1024)
            tk.dma('sp', hnt, hnDk[:, :, t0:t0 + TL], writes=['hnt', 'hmid'], skey='hnt')
            tk.dma('sp', yat, yaDk[:, :, t0:t0 + TL], writes=['yat', 'hmid'], skey='yat')
            tk.dma('sp', yst, ysDk[:, :, t0:t0 + TL], writes=['yst', 'hmid'], skey='yst')
            tk.dma('sp', xt, xk[:, :, t0:t0 + TL], writes=['xt', 'xtA', 'xtB'], skey='xt')
            for oc in range(8):
                cs = slice(oc * 128, (oc + 1) * 128)
                p1, p2, p3, p4 = nps(), nps(), nps(), nps()
                tk.mm([(PS[p1][:, 0:TL], wg0[:, k, cs], hnt[:, k, :], dict(start=(k == 0), stop=(k == 7))) for k in range(8)],
                      reads=[kg0, 'hnt'], writes=[f'ps{p1}'])
                tk.mm([(PS[p2][:, 0:TL], wpl[:, k, cs], yat[:, k, :], dict(start=(k == 0), stop=(k == 7))) for k in range(8)],
                      reads=[kpl, 'yat'], writes=[f'ps{p2}'])
                tk.mm([(PS[p3][:, 0:TL], wg1[:, k, cs], hnt[:, k, :], dict(start=(k == 0), stop=(k == 7))) for k in range(8)],
                      reads=[kg1, 'hnt'], writes=[f'ps{p3}'])
                tk.mm([(PS[p4][:, 0:TL], wps[:, k, cs], yst[:, k, :], dict(start=(k == 0), stop=(k == 3))) for k in range(4)],
                      reads=[kps, 'yst'], writes=[f'ps{p4}'])
                tk.op('act', lambda e: e.activation(out=tm4[0], in_=PS[p1][:, 0:TL], func=AF.Sigmoid, bias=col(vecs, 168 + oc), scale=1.0),
                      reads=[f'ps{p1}'], writes=['tm0'])
                tk.op('act', lambda e: e.activation(out=tm4[1], in_=PS[p3][:, 0:TL], func=AF.Sigmoid, bias=col(vecs, 176 + oc), scale=1.0),
                      reads=[f'ps{p3}'], writes=['tm1'])
                tk.op('dve', lambda e: e.tensor_tensor(out=tm4[2], in0=tm4[0], in1=PS[p2][:, 0:TL], op=ALU.mult), reads=['tm0', f'ps{p2}'], writes=['tm2'])
                tk.op('dve', lambda e: e.tensor_tensor(out=tm4[3], in0=tm4[1], in1=PS[p4][:, 0:TL], op=ALU.mult), reads=['tm1', f'ps{p4}'], writes=['tm3'])
                tk.op('dve', lambda e: e.tensor_tensor(out=mt[:, oc, :], in0=tm4[2], in1=tm4[3], op=ALU.add), reads=['tm2', 'tm3'], writes=['mt'])
            wo, ko = wload(kp(WO, 0, 1024), 8, 1024)
            for oc in range(8):
                cs = slice(oc * 128, (oc + 1) * 128)
                p1 = nps()
                tk.mm([(PS[p1][:, 0:TL], wo[:, k, cs], mt[:, k, :], dict(start=(k == 0), stop=(k == 7))) for k in range(8)],
                      reads=[ko, 'mt'], writes=[f'ps{p1}'])
                tk.op('act', lambda e: e.copy(out=mo[:, oc, :], in_=PS[p1][:, 0:TL]), reads=[f'ps{p1}'], writes=['mo'] + [f'mo{k_}' for k_ in range(8)])
                tk.op('act', lambda e: e.activation(out=sqs[:, oc, :], in_=PS[p1][:, 0:TL], func=AF.Square), reads=[f'ps{p1}'], writes=['sqs'])
            rms_rstd(sqs, rstd, 'sqs')

            def residual(cG):
                for k in range(8):
                    tk.op('dve', lambda e: e.scalar_tensor_tensor(out=mo[:, k, :], in0=mo[:, k, :], scalar=cj(cG, k, j), in1=rstd, op0=ALU.mult, op1=ALU.mult),
                          reads=['mo', 'rstd'], writes=[f'mo{k}'])
                tk.op('dve', lambda e: e.tensor_tensor(out=xt[:, 0:5, :], in0=xt[:, 0:5, :], in1=mo[:, 0:5, :], op=ALU.add),
                      reads=[f'mo{k_}' for k_ in range(5)] + ['xt', 'xtA'], writes=['xtA'])
                tk.op('pool', lambda e: e.tensor_tensor(out=xt[:, 5:8, :], in0=xt[:, 5:8, :], in1=mo[:, 5:8, :], op=ALU.add),
                      reads=[f'mo{k_}' for k_ in range(5, 8)] + ['xt', 'xtB'], writes=['xtB'])
            residual(cG1)
            tk.op('act', lambda e: e.activation(out=sqs, in_=xt, func=AF.Square), reads=['xtA', 'xtB'], writes=['sqs'])
            rms_rstd(sqs, rstd, 'sqs')
            for k in range(8):
                tk.op('dve', lambda e: e.scalar_tensor_tensor(out=mo[:, k, :], in0=xt[:, k, :], scalar=cj(cA2, k, j), in1=rstd, op0=ALU.mult, op1=ALU.mult),
                      reads=['xtA', 'xtB', 'rstd'] + [f'mo{k_}' for k_ in range(8)], writes=[f'mo{k}'])
                tk.op('act', lambda e: e.activation(out=mt[:, k, :], in_=mo[:, k, :], func=AF.Identity, bias=cj(cB2, k, j), scale=1.0),
                      reads=[f'mo{k}'], writes=['mt', f'mt{k}'])
            for blk in range(11):
                w1, k1 = wload(kp(WFI, blk * 256, blk * 256 + 256), 8, 256)
                w3, k3 = wload(kp(WFI, 2816 + blk * 256, 2816 + blk * 256 + 256), 8, 256)
                for sub in range(2):
                    cs = slice(sub * 128, (sub + 1) * 128)
                    p1, p3 = nps(), nps()
                    tk.mm([(PS[p1][:, 0:TL], w1[:, k, cs], mt[:, k, :], dict(start=(k == 0), stop=(k == 7))) for k in range(8)],
                          reads=[k1, 'mt'] + [f'mt{k_}' for k_ in range(8)], writes=[f'ps{p1}'])
                    tk.mm([(PS[p3][:, 0:TL], w3[:, k, cs], mt[:, k, :], dict(start=(k == 0), stop=(k == 7))) for k in range(8)],
                          reads=[k3, 'mt'] + [f'mt{k_}' for k_ in range(8)], writes=[f'ps{p3}'])
                    tm = tm4[2 + sub]
                    tk.op('act', lambda e: e.activation(out=tm, in_=PS[p1][:, 0:TL], func=AF.Silu), reads=[f'ps{p1}'], writes=[f'tm{2 + sub}'])
                    tk.op('dve', lambda e: e.tensor_tensor(out=hmid[:, blk * 2 + sub, :], in0=tm, in1=PS[p3][:, 0:TL], op=ALU.mult),
                          reads=[f'tm{2 + sub}', f'ps{p3}'], writes=R1K)
            for oc in range(8):
                wf, kf = wload(WFO[:, oc * 128:(oc + 1) * 128].rearrange("(k p) n -> p k n", p=128), 22, 128)
                p1 = nps()
                tk.mm([(PS[p1][:, 0:TL], wf[:, k, :], hmid[:, k, :], dict(start=(k == 0), stop=(k == 21))) for k in range(22)],
                      reads=[kf, 'hmid'], writes=[f'ps{p1}'])
                tk.op('act', lambda e: e.copy(out=mo[:, oc, :], in_=PS[p1][:, 0:TL]), reads=[f'ps{p1}'], writes=['mo'] + [f'mo{k_}' for k_ in range(8)])
                tk.op('act', lambda e: e.activation(out=sqs[:, oc, :], in_=PS[p1][:, 0:TL], func=AF.Square), reads=[f'ps{p1}'], writes=['sqs'])
            rms_rstd(sqs, rstd, 'sqs')
            residual(cG2)
            tk.dma('pool', yk[:, :, t0:t0 + TL], xt, reads=['xt', 'xtA', 'xtB'], writes=['yT'], skey='yst_out')
        tk.barrier()

    tk.dma('pool', finl_d, finl, reads=['finl'], writes=['finl_d'], skey='fl')
    tk.dma('pool', fins_d, fins, reads=['fins'], writes=['fins_d'], skey='fs')
    tk.barrier()
    return nc


_NC = None


def _host_inputs(inp, c):
    f = np.float32
    g = lambda k: np.asarray(inp[k], dtype=f)

    def v8(vec):
        return np.ascontiguousarray(vec.reshape(-1, 128).T)
    m = {}
    m["xT_p"] = np.ascontiguousarray(g("x_prompt")[4 * c:4 * c + 4].reshape(1024, 1024).T)
    m["xT_s"] = np.ascontiguousarray(g("x_sample")[c].T)
    cv = np.stack([g("c_ctx"), g("c")[c]], axis=1)
    m["cv"] = np.ascontiguousarray(cv.reshape(8, 128, 2).transpose(1, 0, 2).reshape(128, 16))
    cols = [v8(g("g_pre_mix")[0]), v8(g("g_post_mix")[0]), v8(g("g_pre_ffn")[0]), v8(g("g_post_ffn")[0]),
            v8(g("b_mod")[0]), v8(g("conv_w")[0].reshape(-1)), v8(g("conv_b")[0]), v8(g("lru_b_r")[0].reshape(-1)),
            v8(g("lru_b_i")[0].reshape(-1)), v8(g("lru_lambda")[0].reshape(-1)), v8(g("b_gate")[0]),
            v8(g("s5_d")[0]), v8(g("s5_b_glu")[0])]
    m["vecs"] = np.ascontiguousarray(np.concatenate(cols, axis=1))
    assert m["vecs"].shape == (128, NV)
    lw = np.zeros((128, 2, 2, 8, 128), f)
    for gi, key in enumerate(("lru_w_r", "lru_w_i")):
        w = g(key)[0]
        for hh in range(2):
            lw[64 * hh:64 * hh + 64, :, gi, :, 64 * hh:64 * hh + 64] = w[:, hh::2].transpose(2, 0, 1, 3)
    m["lruw"] = lw.reshape(128, 32 * 128)
    m["h0l"] = np.ascontiguousarray(g("state_lru")[c, 0].reshape(2, 8, 128).transpose(2, 0, 1).reshape(128, 16))

    def unit(a):
        sh = a.shape[3:]
        a = a.reshape((2, 16, 2, 64) + sh)
        a = np.moveaxis(a, (2, 3), (0, 1))
        return np.ascontiguousarray(a.reshape((128, 32) + sh))
    ldt = np.broadcast_to(g("s5_log_dt")[0][:, :, None], (2, 32, 64))
    s5p = np.stack([unit(g("s5_a_re")[0]), unit(g("s5_a_im")[0]), unit(ldt),
                    unit(g("state_s5_re")[c, 0]), unit(g("state_s5_im")[c, 0])], axis=1)
    m["s5p"] = np.ascontiguousarray(s5p.reshape(128, 160))
    s5bc = np.stack([unit(g("s5_b_re")[0]), unit(g("s5_b_im")[0]),
                     unit(g("s5_c_re")[0].transpose(0, 1, 3, 2)), unit(g("s5_c_im")[0].transpose(0, 1, 3, 2))], axis=1)
    m["s5bc"] = np.ascontiguousarray(s5bc.reshape(128, 4 * 512))
    m["ident"] = np.eye(128, dtype=f)
    m["w_mod"] = g("w_mod")[0]; m["w_in"] = g("w_in")[0]; m["w_gate"] = g("w_gate")[0]; m["w_out"] = g("w_out")[0]
    m["w_pl"] = g("w_proj_lru")[0]; m["w_ps"] = g("w_proj_s5")[0]; m["w_fi"] = g("w_ff_in")[0]; m["w_fo"] = g("w_ff_out")[0]
    m["w_glu"] = g("s5_w_glu")[0]
    return m


def kernel(**inputs):
    global _NC
    if _NC is None:
        _NC = build_nc()
    nc = _NC
    in_maps = [_host_inputs(inputs, c) for c in range(8)]
    res = run_bass_kernel_spmd(nc, in_maps, core_ids=list(range(8)))
    y_p = np.zeros((32, 256, 1024), np.float32)
    y_s = np.zeros((8, 4096, 1024), np.float32)
    nl = np.zeros((32, 1, 2, 1024), np.float32)
    nre = np.zeros((32, 1, 2, 32, 64), np.float32)
    nim = np.zeros((32, 1, 2, 32, 64), np.float32)
    for c in range(8):
        r = res.results[c]
        y_p[4 * c:4 * c + 4] = r["yT_p"].T.reshape(4, 256, 1024)
        y_s[c] = r["yT_s"].T
        fl = r["fin_lru"].reshape(128, 4, 2, 8)
        nl[4 * c:4 * c + 4, 0] = fl.transpose(1, 2, 3, 0).reshape(4, 2, 1024)
        fs = r["fin_s5"].reshape(2, 64, 2, 4, 2, 16)
        fs = fs.transpose(2, 3, 4, 5, 0, 1).reshape(2, 4, 2, 32, 64)
        nre[4 * c:4 * c + 4, 0] = fs[0]
        nim[4 * c:4 * c + 4, 0] = fs[1]
    return (y_p, y_s, nl, nre, nim)
```
